# Optimizing a Trainium2 kernel written in Bass

```python
import jax, jax.numpy as jnp
from jax import lax
import numpy as np

D_MODEL = 1024
BATCH = 1
SEQ = 16384
DEPTH = 2

GRID_W = 64
HEAD_DIM = 64
A_Q_HEADS = 8
A_KV_HEADS = 2
A_GROUP = A_Q_HEADS // A_KV_HEADS
B_HEADS = 4
B_V_DIM = 2 * HEAD_DIM
A_WIDTH = A_Q_HEADS * HEAD_DIM
B_WIDTH = B_HEADS * B_V_DIM
A_Q_COLS = A_Q_HEADS * HEAD_DIM
A_KV_COLS = A_KV_HEADS * HEAD_DIM
B_QK_COLS = 2 * B_HEADS * HEAD_DIM
B_V_COLS = B_HEADS * B_V_DIM
IN_COLS = A_Q_COLS + 2 * A_KV_COLS + 2 * B_QK_COLS + B_V_COLS
N_BRANCH = 2
D_FF = 2816
Q_BLOCK = 128
ROPE_THETA = 10000.0
ROPE_AXIS_DIM = HEAD_DIM // 2
N_ADA = 9
EPS = 1e-6

kernel_name = 'hybrid_gqa_axialrope_diffattn_macaron_adaln'


def rms_norm(x, g):
    xf = x.astype(jnp.float32)
    y = xf * lax.rsqrt(jnp.mean(xf * xf, axis=-1, keepdims=True) + EPS)
    return (y * g.astype(jnp.float32)).astype(x.dtype)


def modulate(x, shift, scale):
    return x * (1 + scale[:, None, :]) + shift[:, None, :]


def swiglu(x, w_gate, w_up, w_down):
    return (jax.nn.silu(x @ w_gate) * (x @ w_up)) @ w_down


def axial_angles(seq):
    rows = seq // GRID_W
    row = jnp.broadcast_to(jnp.arange(rows)[:, None], (rows, GRID_W)).reshape(seq) - rows // 2
    col = jnp.broadcast_to(jnp.arange(GRID_W)[None, :], (rows, GRID_W)).reshape(seq) - GRID_W // 2
    inv = 1.0 / (ROPE_THETA ** (jnp.arange(0, ROPE_AXIS_DIM, 2, dtype=jnp.float32) / ROPE_AXIS_DIM))
    ang_r = row.astype(jnp.float32)[:, None] * inv
    ang_c = col.astype(jnp.float32)[:, None] * inv
    return ang_r, ang_c


def rotate_section(x, ang):
    cos = jnp.cos(ang)[:, None, :]
    sin = jnp.sin(ang)[:, None, :]
    x1, x2 = jnp.split(x, 2, axis=-1)
    return jnp.concatenate([x1 * cos - x2 * sin, x2 * cos + x1 * sin], axis=-1)


def axial_rope(x, ang_r, ang_c):
    xf = x.astype(jnp.float32)
    xr, xc = jnp.split(xf, 2, axis=-1)
    return jnp.concatenate([rotate_section(xr, ang_r), rotate_section(xc, ang_c)], axis=-1).astype(x.dtype)


def gqa_attention(q, k, v):
    b, s = q.shape[0], q.shape[1]
    nblk = s // Q_BLOCK
    qb = q.reshape(b, nblk, Q_BLOCK, A_KV_HEADS, A_GROUP, HEAD_DIM).transpose(1, 0, 2, 3, 4, 5)
    scale = HEAD_DIM ** -0.5

    def block(qi):
        sc = jnp.einsum('bqkgd,bskd->bkgqs', qi, k, preferred_element_type=jnp.float32) * scale
        p = jax.nn.softmax(sc, axis=-1).astype(v.dtype)
        return jnp.einsum('bkgqs,bskd->bqkgd', p, v)

    o = lax.map(block, qb)
    return o.transpose(1, 0, 2, 3, 4, 5).reshape(b, s, A_WIDTH)


def diff_attention(q, k, v, lam, slopes):
    b, s = q.shape[0], q.shape[1]
    nblk = s // Q_BLOCK
    qb = q.reshape(b, nblk, Q_BLOCK, 2, B_HEADS, HEAD_DIM).transpose(1, 0, 2, 3, 4, 5)
    scale = HEAD_DIM ** -0.5
    kpos = jnp.arange(s, dtype=jnp.float32)

    def block(args):
        qi, bi = args
        qpos = (bi * Q_BLOCK + jnp.arange(Q_BLOCK)).astype(jnp.float32)
        dist = jnp.abs(qpos[:, None] - kpos[None, :])
        bias = -slopes[:, None, None] * dist[None]
        sc = jnp.einsum('bqmhd,bsmhd->bmhqs', qi, k, preferred_element_type=jnp.float32) * scale
        p = jax.nn.softmax(sc + bias[None, None], axis=-1)
        w = (p[:, 0] - lam * p[:, 1]).astype(v.dtype)
        return jnp.einsum('bhqs,bshe->bqhe', w, v)

    o = lax.map(block, (qb, jnp.arange(nblk)))
    return o.transpose(1, 0, 2, 3, 4).reshape(b, s, B_HEADS, B_V_DIM)


def token_mixing(n, ang_r, ang_c, lam_init, w_in, qk_g, lam_p, subln_g, w_ba, w_bb, w_gate, b_gate, w_o):
    b, s, _ = n.shape
    proj = n @ w_in
    c1 = A_Q_COLS
    c2 = c1 + A_KV_COLS
    c3 = c2 + A_KV_COLS
    c4 = c3 + B_QK_COLS
    c5 = c4 + B_QK_COLS
    qa, ka, va, qd, kd, vd = jnp.split(proj, [c1, c2, c3, c4, c5], axis=-1)

    qa = axial_rope(rms_norm(qa.reshape(b, s, A_Q_HEADS, HEAD_DIM), qk_g[0]), ang_r, ang_c)
    ka = axial_rope(rms_norm(ka.reshape(b, s, A_KV_HEADS, HEAD_DIM), qk_g[1]), ang_r, ang_c)
    va = va.reshape(b, s, A_KV_HEADS, HEAD_DIM)
    qa = qa.reshape(b, s, A_KV_HEADS, A_GROUP, HEAD_DIM)
    ya = gqa_attention(qa, ka, va) @ w_ba

    lp = lam_p.astype(jnp.float32)
    lam = jnp.exp(jnp.sum(lp[0] * lp[1])) - jnp.exp(jnp.sum(lp[2] * lp[3])) + lam_init
    slopes = 2.0 ** (-8.0 * jnp.arange(1, B_HEADS + 1, dtype=jnp.float32) / B_HEADS)
    qd = qd.reshape(b, s, 2, B_HEADS, HEAD_DIM)
    kd = kd.reshape(b, s, 2, B_HEADS, HEAD_DIM)
    vd = vd.reshape(b, s, B_HEADS, B_V_DIM)
    od = diff_attention(qd, kd, vd, lam, slopes)
    od = rms_norm(od, subln_g) * (1.0 - lam_init)
    yb = od.reshape(b, s, B_WIDTH) @ w_bb

    g = jax.nn.sigmoid(n @ w_gate + b_gate)
    ga, gb = jnp.split(g, N_BRANCH, axis=-1)
    return (ga * ya + gb * yb) @ w_o


def setup_inputs(seed: int = 0) -> dict:
    key = jax.random.key(seed)
    ks = jax.random.split(key, 24)
    D, L, F = D_MODEL, DEPTH, D_FF

    def nrm(k, shape, scale):
        return jax.random.normal(k, shape, jnp.float32) * scale

    return {
        'x': nrm(ks[0], (BATCH, SEQ, D), 1.0),
        'c': nrm(ks[1], (BATCH, D), 1.0),
        'ada_w': nrm(ks[2], (L, D, N_ADA * D), 0.5 * D ** -0.5),
        'ada_b': nrm(ks[3], (L, N_ADA * D), 0.01),
        'norm_g': 1.0 + nrm(ks[4], (L, 3, D), 0.02),
        'ffn_wg': nrm(ks[5], (L, 2, D, F), D ** -0.5),
        'ffn_wu': nrm(ks[6], (L, 2, D, F), D ** -0.5),
        'ffn_wd': nrm(ks[7], (L, 2, F, D), F ** -0.5),
        'w_in': nrm(ks[8], (L, D, IN_COLS), D ** -0.5),
        'qk_g': 1.0 + nrm(ks[9], (L, 2, HEAD_DIM), 0.02),
        'lam_p': nrm(ks[10], (L, 4, HEAD_DIM), 0.1),
        'subln_g': 1.0 + nrm(ks[11], (L, B_V_DIM), 0.02),
        'w_ba': nrm(ks[12], (L, A_WIDTH, D), A_WIDTH ** -0.5),
        'w_bb': nrm(ks[13], (L, B_WIDTH, D), B_WIDTH ** -0.5),
        'w_gate': nrm(ks[14], (L, D, N_BRANCH * D), D ** -0.5),
        'b_gate': nrm(ks[15], (L, N_BRANCH * D), 0.01),
        'w_o': nrm(ks[16], (L, D, D), D ** -0.5),
        'final_g': 1.0 + nrm(ks[17], (D,), 0.02),
    }


def reference(x, c, ada_w, ada_b, norm_g, ffn_wg, ffn_wu, ffn_wd, w_in, qk_g, lam_p, subln_g, w_ba, w_bb, w_gate, b_gate, w_o, final_g):
    s = x.shape[1]
    ang_r, ang_c = axial_angles(s)
    c_act = jax.nn.silu(c)
    h = x
    for l in range(DEPTH):
        mod = c_act @ ada_w[l] + ada_b[l]
        sh1, sc1, g1, sh2, sc2, g2, sh3, sc3, g3 = jnp.split(mod, N_ADA, axis=-1)
        lam_init = 0.8 - 0.6 * float(np.exp(-0.3 * l))

        n1 = modulate(rms_norm(h, norm_g[l, 0]), sh1, sc1)
        h = h + 0.5 * g1[:, None, :] * swiglu(n1, ffn_wg[l, 0], ffn_wu[l, 0], ffn_wd[l, 0])

        n2 = modulate(rms_norm(h, norm_g[l, 1]), sh2, sc2)
        y = token_mixing(n2, ang_r, ang_c, lam_init, w_in[l], qk_g[l], lam_p[l], subln_g[l],
                         w_ba[l], w_bb[l], w_gate[l], b_gate[l], w_o[l])
        h = h + g2[:, None, :] * y

        n3 = modulate(rms_norm(h, norm_g[l, 2]), sh3, sc3)
        h = h + 0.5 * g3[:, None, :] * swiglu(n3, ffn_wg[l, 1], ffn_wu[l, 1], ffn_wd[l, 1])
    return rms_norm(h, final_g)
```

```python
import contextlib
import numpy as np
import ml_dtypes
import concourse.bass as bass
import concourse.mybir as mybir
from concourse.bass_utils import run_bass_kernel_spmd

F32 = mybir.dt.float32
BF16 = mybir.dt.bfloat16
AF = mybir.ActivationFunctionType
ALU = mybir.AluOpType

NCORES = 8
D = 1024
S = 16384
T = S // NCORES
L = 2
FF = 2816
NFC = FF // 128
HD = 64
EPS = 1e-6
KROWS = 640
VW = 640
BIGNEG = -16384.0

ENGS = ('pe', 'act', 'dve', 'pool', 'sp')
SEM_LIMIT = 20000


class DmaSem:
    def __init__(self, name):
        self.name = name
        self.count = 0
        self.handle = None


class Buf:
    __slots__ = ('name', 'last_w', 'rd_eng', 'rd_dma')

    def __init__(self, name):
        self.name = name
        self.last_w = None
        self.rd_eng = {}
        self.rd_dma = {}


class Op:
    __slots__ = ('eng', 'fn', 'deps', 'idx', 'needed', 'dsem', 'semval', 'semidx')

    def __init__(self, eng, fn, idx):
        self.eng = eng
        self.fn = fn
        self.idx = idx
        self.deps = []
        self.needed = False
        self.dsem = None
        self.semval = None
        self.semidx = None


class Prog:
    def __init__(self, nc):
        self.nc = nc
        self.ops = {e: [] for e in ENGS}
        self.seen = {e: {} for e in ENGS}
        self.dsems = []

    def dsem(self, name):
        s = DmaSem('%s_%d' % (name, len(self.dsems)))
        self.dsems.append(s)
        return s

    def buf(self, name):
        return Buf(name)

    def _add_dep(self, op, tok):
        if tok is None:
            return
        kind, key, val = tok
        if kind == 'eng' and key == op.eng and key == 'pe':
            return
        seen = self.seen[op.eng]
        if seen.get(key, -1) >= val:
            return
        seen[key] = val
        op.deps.append(tok)
        if kind == 'eng':
            self.ops[key][val].needed = True

    def op(self, eng, fn, reads=(), writes=(), dsem=None):
        lst = self.ops[eng]
        o = Op(eng, fn, len(lst))
        if dsem is not None:
            dsem.count += 16
            o.dsem = dsem
            tok = ('dma', dsem, dsem.count)
        else:
            tok = ('eng', eng, o.idx)
        lst.append(o)
        for r in reads:
            self._add_dep(o, r.last_w)
        for w in writes:
            self._add_dep(o, w.last_w)
            for e, i in w.rd_eng.items():
                self._add_dep(o, ('eng', e, i))
            for s, c in w.rd_dma.items():
                self._add_dep(o, ('dma', s, c))
        for r in reads:
            if dsem is not None:
                r.rd_dma[dsem] = dsem.count
            else:
                r.rd_eng[eng] = o.idx
        for w in writes:
            w.last_w = tok
            w.rd_eng = {}
            w.rd_dma = {}
        return o

    def barrier(self):
        toks = []
        for e in ENGS:
            if e == 'sp':
                continue
            if self.ops[e]:
                for o in reversed(self.ops[e]):
                    if o.fn is not None and o.dsem is None:
                        toks.append(('eng', e, o.idx))
                        break
        for s in self.dsems:
            if s.count > 0:
                toks.append(('dma', s, s.count))
        for e in ENGS:
            o = Op(e, None, len(self.ops[e]))
            self.ops[e].append(o)
            for t in toks:
                if t[0] == 'eng' and t[1] == e:
                    continue
                self._add_dep(o, t)

    def emit(self, stack):
        nc = self.nc
        for s in self.dsems:
            if s.count > 0:
                s.handle = stack.enter_context(nc.semaphore('d_' + s.name))
        esems = {}
        for e in ENGS:
            cnt = 0
            si = 0
            anyn = False
            for o in self.ops[e]:
                if o.needed and o.dsem is None:
                    anyn = True
                    if cnt >= SEM_LIMIT:
                        si += 1
                        cnt = 0
                    cnt += 1
                    o.semidx = si
                    o.semval = cnt
            esems[e] = [stack.enter_context(nc.semaphore('e_%s_%d' % (e, k)))
                        for k in range(si + 1 if anyn else 0)]
        block = stack.enter_context(nc.Block())
        starters = {'pe': block.tensor, 'act': block.scalar, 'dve': block.vector,
                    'pool': block.gpsimd, 'sp': block.sync}
        for e in ENGS:
            if not self.ops[e]:
                continue

            def body(h, e=e):
                for o in self.ops[e]:
                    for kind, key, val in o.deps:
                        if kind == 'eng':
                            po = self.ops[key][val]
                            h.wait_ge(esems[key][po.semidx], po.semval)
                        else:
                            h.wait_ge(key.handle, val)
                    if o.fn is None:
                        continue
                    ins = o.fn(h)
                    if o.dsem is not None:
                        ins.then_inc(o.dsem.handle, 16)
                    elif o.needed:
                        ins.then_inc(esems[e][o.semidx], 1)
            starters[e](body)


class Ring:
    def __init__(self, P, name, aps):
        self.aps = aps
        self.bufs = [P.buf('%s%d' % (name, i)) for i in range(len(aps))]
        self.sems = [P.dsem('%s%d' % (name, i)) for i in range(len(aps))]
        self.i = 0

    def next(self):
        k = self.i % len(self.aps)
        self.i += 1
        return self.aps[k], self.bufs[k], self.sems[k]


def _vec_offsets():
    off = {}
    n = 0

    def add(k, w):
        nonlocal n
        off[k] = n
        n += w
    add('c', 8)
    for l in range(L):
        add('adab%d' % l, 72)
        add('ng%d' % l, 24)
        add('bg%d' % l, 16)
        add('qkg%d' % l, 4)
        add('sub%d' % l, 1)
        add('lamp%d' % l, 4)
    add('fg', 8)
    return off, n


VOFF, NV = _vec_offsets()


def _rope_src():
    d = np.arange(64)
    dd = d % 32
    return np.where(dd < 16, d + 16, d - 16)


def _fm(v):
    return np.ascontiguousarray(v.reshape(-1, 128).T)


def _host_vecs(c, ada_b, norm_g, b_gate, qk_g, subln_g, lam_p, final_g):
    v = np.zeros((128, NV), np.float32)
    v[:, VOFF['c']:VOFF['c'] + 8] = _fm(c[0])
    src = _rope_src()
    for l in range(L):
        v[:, VOFF['adab%d' % l]:VOFF['adab%d' % l] + 72] = _fm(ada_b[l])
        for k in range(3):
            o = VOFF['ng%d' % l] + 8 * k
            v[:, o:o + 8] = _fm(norm_g[l, k])
        v[:, VOFF['bg%d' % l]:VOFF['bg%d' % l] + 16] = _fm(b_gate[l])
        o = VOFF['qkg%d' % l]
        p = np.arange(128) % 64
        v[:, o + 0] = qk_g[l, 0][p]
        v[:, o + 1] = qk_g[l, 0][src[p]]
        v[:, o + 2] = qk_g[l, 1][p]
        v[:, o + 3] = qk_g[l, 1][src[p]]
        v[:, VOFF['sub%d' % l]] = subln_g[l]
        v[0:64, VOFF['lamp%d' % l]:VOFF['lamp%d' % l] + 4] = lam_p[l].T
    v[:, VOFF['fg']:VOFF['fg'] + 8] = _fm(final_g)
    return v


def _host_rope(core):
    t = core * T + np.arange(T)
    row = (t // 64 - (S // 64) // 2).astype(np.float32)
    col = (t % 64 - 32).astype(np.float32)
    inv = (1.0 / (np.float32(10000.0) ** (np.arange(0, 32, 2, dtype=np.float32) / np.float32(32)))).astype(np.float32)
    out = np.zeros((128, 2, T), np.float32)
    for p in range(128):
        d = p % 64
        sec = d // 32
        dd = d % 32
        j = dd % 16
        pos = row if sec == 0 else col
        ang = (pos * inv[j]).astype(np.float32)
        out[p, 0] = np.cos(ang)
        out[p, 1] = -np.sin(ang) if dd < 16 else np.sin(ang)
    return out


def _host_alibi(core):
    bf = ml_dtypes.bfloat16
    cs = [8.0 * 2.0 ** (-2.0 * (h + 1)) for h in range(4)]
    q = core * T + np.arange(T)
    qaug = np.zeros((4, 4, T), np.float32)
    k = np.arange(S)
    kaug = np.zeros((4, 4, 4, S), np.float32)
    for h in range(4):
        c = cs[h]
        qaug[h, 0] = -c * 128.0 * (q // 128)
        qaug[h, 1] = -c * (q % 128)
        qaug[h, 2] = 1.0
        qaug[h, 3] = 1.0
        for g in range(4):
            q0 = core * T + g * 512
            sg = np.where(k < q0, 1.0, -1.0)
            diag = (k >= q0) & (k < q0 + 512)
            r0 = sg.copy()
            r1 = sg.copy()
            r2 = sg * c * 128.0 * (k // 128)
            r3 = sg * c * (k % 128)
            r0[diag] = 0.0
            r1[diag] = 0.0
            r2[diag] = BIGNEG
            r3[diag] = 0.0
            kaug[h, g, 0], kaug[h, g, 1], kaug[h, g, 2], kaug[h, g, 3] = r0, r1, r2, r3
    assert np.array_equal(qaug.astype(bf).astype(np.float32), qaug)
    assert np.array_equal(kaug.astype(bf).astype(np.float32), kaug)
    return qaug.astype(bf), kaug.astype(bf)


def _host_atab():
    kk = np.arange(128)[:, None]
    y = np.arange(896)[None, :]
    return np.abs(y - 384 - kk).astype(np.float32)


def _host_weights(ffn_wg, ffn_wu, ffn_wd, w_in, w_ba, w_bb, w_gate, w_o, ada_w):
    W = {}
    src = _rope_src()
    for l in range(L):
        for i in range(2):
            g = ffn_wg[l, i].reshape(8, 128, NFC, 128)
            u = ffn_wu[l, i].reshape(8, 128, NFC, 128)
            gu = np.stack([g, u], 0)
            W['wgu%d%d' % (l, i)] = np.ascontiguousarray(gu.transpose(3, 2, 0, 1, 4)).reshape(NFC, 128, 2048)
            d = ffn_wd[l, i].reshape(NFC, 128, 8, 128)
            W['wd%d%d' % (l, i)] = np.ascontiguousarray(d.transpose(2, 1, 0, 3)).reshape(8, 128, FF)
        wi = w_in[l]
        qa, ka, va, qd, kd, vd = np.split(wi, [512, 640, 768, 1280, 1792], axis=1)
        permq = (np.arange(512) // 64) * 64 + src[np.arange(512) % 64]
        permk = (np.arange(128) // 64) * 64 + src[np.arange(128) % 64]
        fmc = np.concatenate([qa, qa[:, permq], ka, ka[:, permk], qd, kd], axis=1)
        nch = fmc.shape[1] // 128
        x = fmc.reshape(8, 128, nch, 128)
        W['win%d' % l] = np.ascontiguousarray(x.transpose(2, 1, 0, 3)).reshape(nch, 128, 1024)
        wv = np.concatenate([va, vd], axis=1).reshape(8, 128, VW)
        W['wv%d' % l] = np.ascontiguousarray(wv.transpose(1, 0, 2)).reshape(128, 8 * VW)
        wg = w_gate[l].reshape(8, 128, 2, 8, 128)
        W['wgate%d' % l] = np.ascontiguousarray(wg.transpose(3, 1, 2, 0, 4)).reshape(8, 128, 2048)
        wa = w_ba[l].reshape(8, 64, 8, 128)
        W['wba%d' % l] = np.ascontiguousarray(wa.transpose(2, 1, 0, 3)).reshape(8, 64, 1024)
        wb = w_bb[l].reshape(4, 128, 8, 128)
        W['wbb%d' % l] = np.ascontiguousarray(wb.transpose(2, 1, 0, 3)).reshape(8, 128, 512)
        wo = w_o[l].reshape(8, 128, 8, 128)
        W['wo%d' % l] = np.ascontiguousarray(wo.transpose(2, 1, 0, 3)).reshape(8, 128, 1024)
        aw = ada_w[l].reshape(8, 128, 9 * D)
        W['ada%d' % l] = np.ascontiguousarray(aw.transpose(1, 0, 2))
    return W


WSHAPES = {}
for _l in range(L):
    for _i in range(2):
        WSHAPES['wgu%d%d' % (_l, _i)] = [NFC, 128, 2048]
        WSHAPES['wd%d%d' % (_l, _i)] = [8, 128, FF]
    WSHAPES['win%d' % _l] = [18, 128, 1024]
    WSHAPES['wv%d' % _l] = [128, 8 * VW]
    WSHAPES['wgate%d' % _l] = [8, 128, 2048]
    WSHAPES['wba%d' % _l] = [8, 64, 1024]
    WSHAPES['wbb%d' % _l] = [8, 128, 512]
    WSHAPES['wo%d' % _l] = [8, 128, 1024]
    WSHAPES['ada%d' % _l] = [128, 8, 9 * D]

CH_QA, CH_QAP, CH_KA, CH_KAP, CH_QD, CH_KD = 0, 4, 8, 9, 10, 14


def build(segs, dbg=None):
    segs = set(segs)
    nc = bass.Bass("TRN2", target_bir_lowering=False)
    fused = {l: ('p1_%d' % l in segs and 'p23_%d' % l in segs) for l in range(L)}
    layers_used = [l for l in range(L) if ('p1_%d' % l in segs or 'p23_%d' % l in segs)]

    def dram(name, shape, dt, kind):
        return nc.dram_tensor(name, shape, dt, kind=kind).ap()

    first = 'p1_0' in segs
    last = 'fin' in segs
    h_in = dram('xT' if first else 'h_in', [D, T], F32, 'ExternalInput')
    h_out = dram('outT' if last else 'h_out', [D, T], F32, 'ExternalOutput')
    vecs_d = dram('vecs', [128, NV], F32, 'ExternalInput')
    rope_d = dram('rope', [128, 2, T], F32, 'ExternalInput')
    qaug_d = dram('qaug', [4, 4, T], BF16, 'ExternalInput')
    kaug_d = dram('kaug', [4, 4, 4, S], BF16, 'ExternalInput')
    atab_d = dram('atab', [128, 896], F32, 'ExternalInput')
    Wd = {}
    for k, shp in WSHAPES.items():
        l = int(k[-2]) if k.startswith('wgu') or (k.startswith('wd') and len(k) == 4) else int(k[-1])
        if l in layers_used:
            Wd[k] = dram(k, shp, F32, 'ExternalInput')
    SC = {}
    for l in layers_used:
        p1 = 'p1_%d' % l in segs
        p23 = 'p23_%d' % l in segs
        if fused[l]:
            kl = kg = 'Internal'
        elif p1:
            kl, kg = 'ExternalOutput', None
        else:
            kl, kg = 'ExternalInput', 'ExternalInput'
        SC['qa%d' % l] = dram('qa%d' % l, [2, 64, 16, 4, 128], BF16, kl)
        SC['qb%d' % l] = dram('qb%d' % l, [512, T], BF16, kl)
        SC['ktl%d' % l] = dram('ktl%d' % l, [KROWS, T], BF16, kl)
        SC['vl%d' % l] = dram('vl%d' % l, [T, VW], BF16, kl)
        if kg is not None:
            SC['kta%d' % l] = dram('kta%d' % l, [NCORES * KROWS, T], BF16, kg)
            SC['va%d' % l] = dram('va%d' % l, [S, VW], BF16, kg)

    st = contextlib.ExitStack()
    with st:
        def sb(name, shape, dt):
            return st.enter_context(nc.sbuf_tensor(name, shape, dt))
        hT = sb('hT', [128, 8, T], F32)
        NBF = 48 * 1024
        NFP = 9 * 1024 + 512
        BFA = sb('bfa', [128, NBF], BF16)
        FPA = sb('fpa', [128, NFP], F32)
        vecs = sb('vecs_sb', [128, NV], F32)
        modT = sb('modT', [128, L * 72], F32)
        dsc = sb('dsc', [128, L * 64], F32)
        misc = sb('misc', [128, 32], F32)
        cact = sb('cact', [128, 8], BF16)
        ones_bf = sb('ones_bf', [128, 128], BF16)
        bd_bf = sb('bd_bf', [128, 128], BF16)
        ones_f = sb('ones_f', [64, 128], F32)
        psum = [st.enter_context(nc.psum_tensor('ps%d' % i, [128, 512], F32)) for i in range(8)]

        P = Prog(nc)
        Bps = [P.buf('ps%d' % i) for i in range(8)]
        B_h = [[P.buf('h%d_%d' % (c, g)) for g in range(4)] for c in range(8)]
        B_vecs, B_mod, B_dsc, B_misc, B_cact = P.buf('vecs'), P.buf('mod'), P.buf('dsc'), P.buf('misc'), P.buf('cact')
        B_const = P.buf('const')
        d_misc = P.dsem('misc')
        d_h = P.dsem('hload')
        d_out = P.dsem('out')
        d_sc = P.dsem('scratch')
        B_scr = {k: P.buf(k) for k in SC}

        def bf_view(off, n, **kw):
            return BFA[:, off:off + n]

        def tokslice(g):
            return slice(g * 512, (g + 1) * 512)

        P.op('sp', lambda e: e.dma_start(out=vecs[:], in_=vecs_d), writes=[B_vecs], dsem=d_misc)
        for c in range(8):
            for g in range(4):
                P.op('sp', lambda e, c=c, g=g: e.dma_start(out=hT[:, c, tokslice(g)], in_=h_in[c * 128:(c + 1) * 128, tokslice(g)]),
                     writes=[B_h[c][g]], dsem=d_h)
        P.op('dve', lambda e: e.memset(ones_bf[:], 1.0), writes=[B_const])
        P.op('dve', lambda e: e.memset(bd_bf[:], 0.0), writes=[B_const])
        P.op('dve', lambda e: e.memset(bd_bf[0:64, 0:64], 1.0), writes=[B_const])
        P.op('dve', lambda e: e.memset(bd_bf[64:128, 64:128], 1.0), writes=[B_const])
        P.op('dve', lambda e: e.memset(ones_f[:], 1.0), writes=[B_const])

        WR_OFF = NBF - 3 * 2048
        wring = Ring(P, 'wr', [BFA[:, WR_OFF + i * 2048: WR_OFF + (i + 1) * 2048] for i in range(3)])

        def vcol(key, j=0, n=1):
            o = VOFF[key] + j
            return vecs[:, o:o + n]

        P.op('act', lambda e: e.activation(out=cact[:], in_=vcol('c', 0, 8), func=AF.Silu), reads=[B_vecs], writes=[B_cact])
        MOD = psum[7]
        for l in range(L):
            if l not in layers_used:
                continue
            for sp_i in range(36):
                slot, sbuf_, ssem = wring.next()
                sv = slot.rearrange("p (k f) -> p k f", k=8)
                P.op('pool', lambda e, sv=sv, l=l, sp_i=sp_i: e.dma_start(out=sv, in_=Wd['ada%d' % l][:, :, sp_i * 256:(sp_i + 1) * 256]),
                     writes=[sbuf_], dsem=ssem)
                for jj in range(2):
                    j = sp_i * 2 + jj
                    for kc in range(8):
                        P.op('pe', lambda e, sv=sv, jj=jj, kc=kc, col=l * 72 + j: e.matmul(
                            MOD[:, col:col + 1], lhsT=sv[:, kc, jj * 128:(jj + 1) * 128], rhs=cact[:, kc:kc + 1],
                            start=(kc == 0), stop=(kc == 7)), reads=[sbuf_, B_cact], writes=[Bps[7]])
            P.op('dve', lambda e, l=l: e.tensor_tensor(out=modT[:, l * 72:(l + 1) * 72], in0=MOD[:, l * 72:(l + 1) * 72],
                                                       in1=vcol('adab%d' % l, 0, 72), op=ALU.add),
                 reads=[Bps[7], B_vecs], writes=[B_mod])
            base = l * 64
            for k in range(3):
                P.op('dve', lambda e, l=l, k=k, base=base: e.tensor_scalar(
                    out=dsc[:, base + 8 * k: base + 8 * k + 8], in0=modT[:, l * 72 + 24 * k + 8: l * 72 + 24 * k + 16],
                    scalar1=1.0, scalar2=None, op0=ALU.add), reads=[B_mod], writes=[B_dsc])
                P.op('dve', lambda e, l=l, k=k, base=base: e.tensor_tensor(
                    out=dsc[:, base + 8 * k: base + 8 * k + 8], in0=dsc[:, base + 8 * k: base + 8 * k + 8],
                    in1=vcol('ng%d' % l, 8 * k, 8), op=ALU.mult), reads=[B_dsc, B_vecs], writes=[B_dsc])
            for k, mo in ((0, 16), (1, 64)):
                P.op('dve', lambda e, l=l, k=k, mo=mo, base=base: e.tensor_scalar(
                    out=dsc[:, base + 24 + 8 * k: base + 32 + 8 * k], in0=modT[:, l * 72 + mo: l * 72 + mo + 8],
                    scalar1=0.5, scalar2=None, op0=ALU.mult), reads=[B_mod], writes=[B_dsc])
            mb = l * 8
            lam_init = 0.8 - 0.6 * float(np.exp(-0.3 * l))
            lo = VOFF['lamp%d' % l]
            P.op('dve', lambda e, mb=mb, lo=lo: e.tensor_tensor(out=misc[0:64, mb + 3:mb + 4], in0=vecs[0:64, lo:lo + 1],
                                                                 in1=vecs[0:64, lo + 1:lo + 2], op=ALU.mult),
                 reads=[B_vecs], writes=[B_misc])
            P.op('dve', lambda e, mb=mb, lo=lo: e.tensor_tensor(out=misc[0:64, mb + 4:mb + 5], in0=vecs[0:64, lo + 2:lo + 3],
                                                                 in1=vecs[0:64, lo + 3:lo + 4], op=ALU.mult),
                 reads=[B_vecs], writes=[B_misc])
            LP = psum[6]
            P.op('pe', lambda e, mb=mb: e.matmul(LP[:, 0:2], lhsT=ones_f[0:64, :], rhs=misc[0:64, mb + 3:mb + 5],
                                                 start=True, stop=True), reads=[B_misc, B_const], writes=[Bps[6]])
            P.op('act', lambda e, mb=mb: e.activation(out=misc[:, mb + 5:mb + 7], in_=LP[:, 0:2], func=AF.Exp),
                 reads=[Bps[6]], writes=[B_misc])
            P.op('dve', lambda e, mb=mb: e.tensor_tensor(out=misc[:, mb:mb + 1], in0=misc[:, mb + 5:mb + 6],
                                                         in1=misc[:, mb + 6:mb + 7], op=ALU.subtract),
                 reads=[B_misc], writes=[B_misc])
            P.op('dve', lambda e, mb=mb, li=lam_init: e.tensor_scalar(out=misc[:, mb:mb + 1], in0=misc[:, mb:mb + 1],
                                                                      scalar1=li, scalar2=None, op0=ALU.add),
                 reads=[B_misc], writes=[B_misc])
            P.op('dve', lambda e, mb=mb: e.tensor_scalar(out=misc[:, mb + 1:mb + 2], in0=misc[:, mb:mb + 1],
                                                         scalar1=-1.0, scalar2=None, op0=ALU.mult),
                 reads=[B_misc], writes=[B_misc])
            P.op('dve', lambda e, mb=mb, l=l, li=lam_init: e.tensor_scalar(
                out=misc[:, mb + 2:mb + 3], in0=vcol('sub%d' % l), scalar1=1.0 - li, scalar2=None, op0=ALU.mult),
                reads=[B_vecs], writes=[B_misc])

        def dcol(l, which, j):
            o = l * 64 + 8 * which + j
            return dsc[:, o:o + 1]

        def mcol(l, idx, j):
            o = l * 72 + idx * 8 + j
            return modT[:, o:o + 1]

        ftmp = [FPA[:, i * 512:(i + 1) * 512] for i in range(10)]
        B_ft = [P.buf('ft%d' % i) for i in range(10)]
        TAB_OFF = 10 * 512

        def norm_to_nT(l, which, half, nT, B_nT, gsc_fn, sh_fn, sq_aps, B_sq):
            ST = psum[6]
            for tg in range(2):
                g = half * 2 + tg
                for kc in range(8):
                    sq, bsq = sq_aps[kc % 2], B_sq[kc % 2]
                    P.op('dve', lambda e, sq=sq, kc=kc, g=g: e.tensor_tensor(out=sq, in0=hT[:, kc, tokslice(g)],
                                                                             in1=hT[:, kc, tokslice(g)], op=ALU.mult),
                         reads=[B_h[kc][g]], writes=[bsq])
                    P.op('pe', lambda e, sq=sq, kc=kc: e.matmul(ST[:, :], lhsT=ones_bf[:, :], rhs=sq, start=(kc == 0), stop=(kc == 7)),
                         reads=[bsq, B_const], writes=[Bps[6]])
                rs, brs = ftmp[8], B_ft[8]
                P.op('act', lambda e, rs=rs: e.activation(out=rs, in_=ST[:, :], func=AF.Sqrt, scale=1.0 / D, bias=EPS),
                     reads=[Bps[6]], writes=[brs])
                rstd, brstd = ftmp[9], B_ft[9]
                P.op('dve', lambda e, rs=rs, rstd=rstd: e.reciprocal(out=rstd, in_=rs), reads=[brs], writes=[brstd])
                for kc in range(8):
                    tmp, btmp = ftmp[kc % 2], B_ft[kc % 2]
                    P.op('dve', lambda e, tmp=tmp, kc=kc, g=g, rstd=rstd: e.scalar_tensor_tensor(
                        out=tmp, in0=hT[:, kc, tokslice(g)], scalar=gsc_fn(kc), in1=rstd, op0=ALU.mult, op1=ALU.mult),
                        reads=[B_h[kc][g], brstd, B_dsc, B_vecs], writes=[btmp])
                    if sh_fn is not None:
                        P.op('act', lambda e, tmp=tmp, kc=kc, tg=tg: e.activation(
                            out=nT[:, kc, tokslice(tg)], in_=tmp, func=AF.Identity, bias=sh_fn(kc), scale=1.0),
                            reads=[btmp, B_mod], writes=[B_nT[kc][tg]])

        def ffn(l, i):
            P.barrier()
            nT = BFA[:, 0:8192].rearrange("p (k t) -> p k t", k=8)
            hid = BFA[:, 8192:8192 + NFC * 1024].rearrange("p (f t) -> p f t", f=NFC)
            o = 8192 + NFC * 1024
            sq_aps = [BFA[:, o:o + 512], BFA[:, o + 512:o + 1024]]
            o += 1024
            wdr = Ring(P, 'wd', [BFA[:, o + k * FF:o + (k + 1) * FF] for k in range(2)])
            assert o + 2 * FF <= WR_OFF
            B_nT = [[P.buf('nT') for _ in range(2)] for _ in range(8)]
            B_hid = [[P.buf('hid') for _ in range(2)] for _ in range(NFC)]
            B_sq = [P.buf('sq0'), P.buf('sq1')]
            kn = 0 if i == 0 else 2
            hgw = 3 if i == 0 else 4
            wgu = Wd['wgu%d%d' % (l, i)]
            wdd = Wd['wd%d%d' % (l, i)]
            cnt = 0
            for half in range(2):
                norm_to_nT(l, kn, half, nT, B_nT, lambda kc: dcol(l, kn, kc), lambda kc: mcol(l, 3 * kn, kc), sq_aps, B_sq)
                for fc in range(NFC):
                    slot, sbuf_, ssem = wring.next()
                    sv = slot.rearrange("p (a k f) -> p a k f", a=2, k=8)
                    P.op('pool', lambda e, slot=slot, fc=fc: e.dma_start(
                        out=slot.rearrange("p (a x) -> p a x", a=2), in_=wgu[fc].rearrange("p (a x) -> p a x", a=2)),
                        writes=[sbuf_], dsem=ssem)
                    for tg in range(2):
                        gi, ui = cnt % 2, 2 + cnt % 2
                        cnt += 1
                        G, U = psum[gi], psum[ui]
                        for kc in range(8):
                            P.op('pe', lambda e, G=G, sv=sv, kc=kc, tg=tg: e.matmul(
                                G[:, :], lhsT=sv[:, 0, kc, :], rhs=nT[:, kc, tokslice(tg)], start=(kc == 0), stop=(kc == 7)),
                                reads=[sbuf_, B_nT[kc][tg]], writes=[Bps[gi]])
                        for kc in range(8):
                            P.op('pe', lambda e, U=U, sv=sv, kc=kc, tg=tg: e.matmul(
                                U[:, :], lhsT=sv[:, 1, kc, :], rhs=nT[:, kc, tokslice(tg)], start=(kc == 0), stop=(kc == 7)),
                                reads=[sbuf_, B_nT[kc][tg]], writes=[Bps[ui]])
                        sg, bsg = ftmp[2 + gi], B_ft[2 + gi]
                        P.op('act', lambda e, sg=sg, G=G: e.activation(out=sg, in_=G[:, :], func=AF.Silu),
                             reads=[Bps[gi]], writes=[bsg])
                        P.op('dve', lambda e, sg=sg, U=U, fc=fc, tg=tg: e.tensor_tensor(
                            out=hid[:, fc, tokslice(tg)], in0=sg, in1=U[:, :], op=ALU.mult),
                            reads=[bsg, Bps[ui]], writes=[B_hid[fc][tg]])
                for dc in range(8):
                    slot, sbuf_, ssem = wdr.next()
                    sv = slot.rearrange("p (f d) -> p f d", f=NFC)
                    P.op('pool', lambda e, sv=sv, dc=dc: e.dma_start(out=sv, in_=wdd[dc].rearrange("p (f d) -> p f d", f=NFC)),
                         writes=[sbuf_], dsem=ssem)
                    for tg in range(2):
                        g = half * 2 + tg
                        yi = 4 + cnt % 2
                        cnt += 1
                        Y = psum[yi]
                        for fc in range(NFC):
                            P.op('pe', lambda e, Y=Y, sv=sv, fc=fc, tg=tg: e.matmul(
                                Y[:, :], lhsT=sv[:, fc, :], rhs=hid[:, fc, tokslice(tg)], start=(fc == 0), stop=(fc == NFC - 1)),
                                reads=[sbuf_, B_hid[fc][tg]], writes=[Bps[yi]])
                        P.op('dve', lambda e, Y=Y, dc=dc, g=g: e.scalar_tensor_tensor(
                            out=hT[:, dc, tokslice(g)], in0=Y[:, :], scalar=dcol(l, hgw, dc), in1=hT[:, dc, tokslice(g)],
                            op0=ALU.mult, op1=ALU.add), reads=[Bps[yi], B_h[dc][g], B_dsc], writes=[B_h[dc][g]])

        def phase1b(l):
            P.barrier()
            nT = BFA[:, 0:8192].rearrange("p (k t) -> p k t", k=8)
            o = 8192
            sq_aps = [BFA[:, o:o + 512], BFA[:, o + 512:o + 1024]]
            o += 1024
            stg = [BFA[:, o + k * 512:o + (k + 1) * 512] for k in range(4)]
            o += 2048
            vst = [BFA[:, o + k * VW:o + (k + 1) * VW] for k in range(2)]
            o += 2 * VW
            wv = BFA[:, o:o + 8 * VW].rearrange("p (k c) -> p k c", k=8)
            o += 8 * VW
            assert o <= WR_OFF
            B_nT = [[P.buf('nT') for _ in range(2)] for _ in range(8)]
            B_sq = [P.buf('sq0'), P.buf('sq1')]
            B_stg = [P.buf('stg%d' % k) for k in range(4)]
            B_vst = [P.buf('vst0'), P.buf('vst1')]
            B_wv = P.buf('wv')
            d_wv = P.dsem('wv')
            d_rope = P.dsem('rope')
            B_rope = P.buf('rope')
            ropeh = FPA[:, TAB_OFF:TAB_OFF + 2048].rearrange("p (a t) -> p a t", a=2)
            qa, qb, ktl, vl = SC['qa%d' % l], SC['qb%d' % l], SC['ktl%d' % l], SC['vl%d' % l]
            win = Wd['win%d' % l]
            qo = VOFF['qkg%d' % l]
            for kc in range(8):
                P.op('pool', lambda e, kc=kc: e.dma_start(out=wv[:, kc, :], in_=Wd['wv%d' % l][:, kc * VW:(kc + 1) * VW]),
                     writes=[B_wv], dsem=d_wv)
            cnt = 0
            scnt = 0
            for half in range(2):
                P.op('sp', lambda e, half=half: e.dma_start(out=ropeh, in_=rope_d[:, :, half * 1024:(half + 1) * 1024]),
                     writes=[B_rope], dsem=d_rope)
                norm_to_nT(l, 1, half, nT, B_nT, lambda kc: dcol(l, 1, kc), lambda kc: mcol(l, 3, kc), sq_aps, B_sq)
                for (ch, chp, isq, c) in [(CH_QA + c, CH_QAP + c, True, c) for c in range(4)] + [(CH_KA, CH_KAP, False, 0)]:
                    slot, sbuf_, ssem = wring.next()
                    sv = slot.rearrange("p (a k f) -> p a k f", a=2, k=8)
                    P.op('pool', lambda e, slot=slot, ch=ch: e.dma_start(out=slot[:, 0:1024], in_=win[ch]), writes=[sbuf_], dsem=ssem)
                    P.op('pool', lambda e, slot=slot, chp=chp: e.dma_start(out=slot[:, 1024:2048], in_=win[chp]), writes=[sbuf_], dsem=ssem)
                    for tg in range(2):
                        g = half * 2 + tg
                        qi, pi = cnt % 2, 2 + cnt % 2
                        cnt += 1
                        Q, QP = psum[qi], psum[pi]
                        for a, PS, bi in ((0, Q, qi), (1, QP, pi)):
                            for kc in range(8):
                                P.op('pe', lambda e, PS=PS, sv=sv, a=a, kc=kc, tg=tg: e.matmul(
                                    PS[:, :], lhsT=sv[:, a, kc, :], rhs=nT[:, kc, tokslice(tg)], start=(kc == 0), stop=(kc == 7)),
                                    reads=[sbuf_, B_nT[kc][tg]], writes=[Bps[bi]])
                        sq, bsq = sq_aps[0], B_sq[0]
                        P.op('act', lambda e, sq=sq, Q=Q: e.activation(out=sq, in_=Q[:, :], func=AF.Square), reads=[Bps[qi]], writes=[bsq])
                        SS = psum[4]
                        P.op('pe', lambda e, sq=sq, SS=SS: e.matmul(SS[:, :], lhsT=bd_bf[:, :], rhs=sq, start=True, stop=True),
                             reads=[bsq, B_const], writes=[Bps[4]])
                        P.op('act', lambda e, SS=SS: e.activation(out=ftmp[4], in_=SS[:, :], func=AF.Sqrt, scale=1.0 / HD, bias=EPS),
                             reads=[Bps[4]], writes=[B_ft[4]])
                        P.op('dve', lambda e: e.reciprocal(out=ftmp[5], in_=ftmp[4]), reads=[B_ft[4]], writes=[B_ft[5]])
                        gcol = qo + (0 if isq else 2)
                        P.op('dve', lambda e, Q=Q, gcol=gcol, tg=tg: e.scalar_tensor_tensor(
                            out=ftmp[6], in0=Q[:, :], scalar=vecs[:, gcol:gcol + 1], in1=ropeh[:, 0, tokslice(tg)],
                            op0=ALU.mult, op1=ALU.mult), reads=[Bps[qi], B_vecs, B_rope], writes=[B_ft[6]])
                        P.op('dve', lambda e, QP=QP, gcol=gcol, tg=tg: e.scalar_tensor_tensor(
                            out=ftmp[7], in0=QP[:, :], scalar=vecs[:, gcol + 1:gcol + 2], in1=ropeh[:, 1, tokslice(tg)],
                            op0=ALU.mult, op1=ALU.mult), reads=[Bps[pi], B_vecs, B_rope], writes=[B_ft[7]])
                        P.op('dve', lambda e: e.tensor_tensor(out=ftmp[6], in0=ftmp[6], in1=ftmp[7], op=ALU.add),
                             reads=[B_ft[6], B_ft[7]], writes=[B_ft[6]])
                        so, bso = stg[scnt % 4], B_stg[scnt % 4]
                        scnt += 1
                        P.op('dve', lambda e, so=so: e.tensor_tensor(out=so, in0=ftmp[6], in1=ftmp[5], op=ALU.mult),
                             reads=[B_ft[6], B_ft[5]], writes=[bso])
                        if isq:
                            for hh in range(2):
                                hd = 2 * c + hh
                                kv, gi = hd // 4, hd % 4
                                P.op('sp', lambda e, so=so, hh=hh, kv=kv, gi=gi, g=g: e.dma_start(
                                    out=qa[kv, :, g * 4:(g + 1) * 4, gi, :],
                                    in_=so[hh * 64:(hh + 1) * 64, :].rearrange("p (b q) -> p b q", b=4)),
                                    reads=[bso], writes=[B_scr['qa%d' % l]], dsem=d_sc)
                        else:
                            P.op('sp', lambda e, so=so, g=g: e.dma_start(out=ktl[0:128, tokslice(g)], in_=so),
                                 reads=[bso], writes=[B_scr['ktl%d' % l]], dsem=d_sc)
                for pair in range(4):
                    slot, sbuf_, ssem = wring.next()
                    sv = slot.rearrange("p (a k f) -> p a k f", a=2, k=8)
                    chs = [(CH_QD + 2 * pair, 'q', 2 * pair), (CH_QD + 2 * pair + 1, 'q', 2 * pair + 1)] if pair < 2 else \
                          [(CH_KD + 2 * (pair - 2), 'k', 2 * (pair - 2)), (CH_KD + 2 * (pair - 2) + 1, 'k', 2 * (pair - 2) + 1)]
                    for a, (ch, kind, c) in enumerate(chs):
                        P.op('pool', lambda e, slot=slot, ch=ch, a=a: e.dma_start(out=slot[:, a * 1024:(a + 1) * 1024], in_=win[ch]),
                             writes=[sbuf_], dsem=ssem)
                    for a, (ch, kind, c) in enumerate(chs):
                        for tg in range(2):
                            g = half * 2 + tg
                            qi = cnt % 4
                            cnt += 1
                            Q = psum[qi]
                            for kc in range(8):
                                P.op('pe', lambda e, Q=Q, sv=sv, a=a, kc=kc, tg=tg: e.matmul(
                                    Q[:, :], lhsT=sv[:, a, kc, :], rhs=nT[:, kc, tokslice(tg)], start=(kc == 0), stop=(kc == 7)),
                                    reads=[sbuf_, B_nT[kc][tg]], writes=[Bps[qi]])
                            so, bso = stg[scnt % 4], B_stg[scnt % 4]
                            scnt += 1
                            if scnt % 2:
                                P.op('act', lambda e, so=so, Q=Q: e.activation(out=so, in_=Q[:, :], func=AF.Copy), reads=[Bps[qi]], writes=[bso])
                            else:
                                P.op('dve', lambda e, so=so, Q=Q: e.tensor_copy(out=so, in_=Q[:, :]), reads=[Bps[qi]], writes=[bso])
                            if kind == 'q':
                                P.op('sp', lambda e, so=so, c=c, g=g: e.dma_start(out=qb[c * 128:(c + 1) * 128, tokslice(g)], in_=so),
                                     reads=[bso], writes=[B_scr['qb%d' % l]], dsem=d_sc)
                            else:
                                P.op('sp', lambda e, so=so, c=c, g=g: e.dma_start(out=ktl[128 + c * 128:128 + (c + 1) * 128, tokslice(g)], in_=so),
                                     reads=[bso], writes=[B_scr['ktl%d' % l]], dsem=d_sc)
                for tt in range(8):
                    tok0 = half * 1024 + tt * 128
                    VD, VA = psum[5], psum[7]
                    for kc in range(8):
                        P.op('pe', lambda e, kc=kc, tt=tt: e.matmul(VD[:, :], lhsT=nT[:, kc, tt * 128:(tt + 1) * 128], rhs=wv[:, kc, 128:640],
                                                                    start=(kc == 0), stop=(kc == 7)),
                             reads=[B_wv, B_nT[kc][tt // 4]], writes=[Bps[5]])
                    for kc in range(8):
                        P.op('pe', lambda e, kc=kc, tt=tt: e.matmul(VA[:, 0:128], lhsT=nT[:, kc, tt * 128:(tt + 1) * 128], rhs=wv[:, kc, 0:128],
                                                                    start=(kc == 0), stop=(kc == 7)),
                             reads=[B_wv, B_nT[kc][tt // 4]], writes=[Bps[7]])
                    vs, bvs = vst[tt % 2], B_vst[tt % 2]
                    P.op('act', lambda e, vs=vs: e.activation(out=vs[:, 128:640], in_=VD[:, :], func=AF.Copy), reads=[Bps[5]], writes=[bvs])
                    P.op('dve', lambda e, vs=vs: e.tensor_copy(out=vs[:, 0:128], in_=VA[:, 0:128]), reads=[Bps[7]], writes=[bvs])
                    P.op('sp', lambda e, vs=vs, tok0=tok0: e.dma_start(out=vl[tok0:tok0 + 128, :], in_=vs),
                         reads=[bvs], writes=[B_scr['vl%d' % l]], dsem=d_sc)

        def gather(l):
            kta, va = SC['kta%d' % l], SC['va%d' % l]
            d_cc = P.dsem('cc%d' % l)
            grp = [list(range(NCORES))]
            P.op('pool', lambda e: e.collective_compute("AllGather", ALU.bypass, replica_groups=grp,
                                                        ins=[SC['ktl%d' % l]], outs=[kta]),
                 reads=[B_scr['ktl%d' % l]], writes=[B_scr['kta%d' % l]], dsem=d_cc)
            P.op('pool', lambda e: e.collective_compute("AllGather", ALU.bypass, replica_groups=grp,
                                                        ins=[SC['vl%d' % l]], outs=[va]),
                 reads=[B_scr['vl%d' % l]], writes=[B_scr['va%d' % l]], dsem=d_cc)

        OA_OFF, OB_OFF = 0, 16384
        OA = BFA[0:64, OA_OFF:OA_OFF + 16384].rearrange("p (h t) -> p h t", h=8)
        OB = BFA[:, OB_OFF:OB_OFF + 8192].rearrange("p (h t) -> p h t", h=4)
        B_OA = [[P.buf('OA') for _ in range(16)] for _ in range(2)]
        B_OB = [[P.buf('OB') for _ in range(4)] for _ in range(4)]

        def phase2(l):
            P.barrier()
            qa, qb, ktl, vl = SC['qa%d' % l], SC['qb%d' % l], SC['ktl%d' % l], SC['vl%d' % l]
            kta, va = SC['kta%d' % l], SC['va%d' % l]
            Bqa, Bqb, Bktl, Bvl = B_scr['qa%d' % l], B_scr['qb%d' % l], B_scr['ktl%d' % l], B_scr['vl%d' % l]
            Bkta, Bva = B_scr['kta%d' % l], B_scr['va%d' % l]
            o = 24576
            qring = Ring(P, 'q', [BFA[:, o + k * 512:o + (k + 1) * 512] for k in range(4)])
            o += 2048
            kring = Ring(P, 'k', [BFA[:, o + k * 512:o + (k + 1) * 512] for k in range(6)])
            o += 3072
            vringA = Ring(P, 'vA', [BFA[:, o + k * 260:o + (k + 1) * 260].rearrange("p (j c) -> p j c", j=4) for k in range(3)])
            o += 3 * 260
            vringB = Ring(P, 'vB', [BFA[:, o + k * 512:o + (k + 1) * 512].rearrange("p (j c) -> p j c", j=4) for k in range(3)])
            o += 1536
            pts = [BFA[:, o + k * 512:o + (k + 1) * 512] for k in range(4)]
            o += 2048
            sqb = BFA[:, o:o + 512]
            o += 512
            rhl = BFA[:, o:o + 1024]
            o += 1024
            assert o <= WR_OFF
            B_pt = [P.buf('pt%d' % k) for k in range(4)]
            B_sqb, B_rhl = P.buf('sqb'), P.buf('rhl')
            atab = FPA[:, TAB_OFF:TAB_OFF + 896]
            B_atab = P.buf('atab')
            P.op('sp', lambda e: e.dma_start(out=atab, in_=atab_d), writes=[B_atab], dsem=d_misc)
            for k in range(3):
                P.op('dve', lambda e, k=k: e.memset(vringA.aps[k][:, :, 64:65], 1.0), writes=[vringA.bufs[k]])
            pcnt = 0
            scnt = 0
            mb = l * 8
            for kv in range(2):
                for qblk in range(16):
                    if dbg is not None and (kv, qblk) not in dbg['A']:
                        continue
                    qs, bq, sq_ = qring.next()
                    P.op('sp', lambda e, qs=qs, kv=kv, qblk=qblk: e.dma_start(
                        out=qs[0:64, :].rearrange("p (g q) -> p g q", g=4), in_=qa[kv, :, qblk, :, :]),
                        reads=[Bqa], writes=[bq], dsem=sq_)
                    oi = 3 + (kv * 16 + qblk) % 2
                    O = psum[oi]
                    for ch in range(32):
                        r, cc = ch // 4, ch % 4
                        ks, bk, sk = kring.next()
                        P.op('sp', lambda e, ks=ks, r=r, cc=cc, kv=kv: e.dma_start(
                            out=ks[0:64, :], in_=kta[r * KROWS + kv * 64: r * KROWS + kv * 64 + 64, cc * 512:(cc + 1) * 512]),
                            reads=[Bkta], writes=[bk], dsem=sk)
                        vs, bv, sv_ = vringA.next()
                        P.op('sp', lambda e, vs=vs, ch=ch, kv=kv: e.dma_start(
                            out=vs[:, :, 0:64], in_=va[ch * 512:(ch + 1) * 512, kv * 64:(kv + 1) * 64].rearrange("(j p) c -> p j c", p=128)),
                            reads=[Bva], writes=[bv], dsem=sv_)
                        for j in range(4):
                            si = scnt % 3
                            scnt += 1
                            Sp = psum[si]
                            P.op('pe', lambda e, Sp=Sp, ks=ks, qs=qs, j=j: e.matmul(
                                Sp[:, :], lhsT=ks[0:64, j * 128:(j + 1) * 128], rhs=qs[0:64, :], start=True, stop=True),
                                reads=[bk, bq], writes=[Bps[si]])
                            pt, bpt = pts[pcnt % 4], B_pt[pcnt % 4]
                            pcnt += 1
                            P.op('act', lambda e, pt=pt, Sp=Sp: e.activation(out=pt, in_=Sp[:, :], func=AF.Exp, scale=0.125),
                                 reads=[Bps[si]], writes=[bpt])
                            P.op('pe', lambda e, O=O, vs=vs, pt=pt, j=j, ch=ch: e.matmul(
                                O[0:65, :], lhsT=vs[:, j, 0:65], rhs=pt, start=(ch == 0 and j == 0), stop=(ch == 31 and j == 3)),
                                reads=[bv, bpt], writes=[Bps[oi]])
                    rec = ftmp[0]
                    P.op('dve', lambda e, O=O, rec=rec: e.reciprocal(out=rec[64:65, :], in_=O[64:65, :]), reads=[Bps[oi]], writes=[B_ft[0]])
                    P.op('dve', lambda e, rec=rec: e.tensor_copy(out=rhl[64:65, 0:512], in_=rec[64:65, :]), reads=[B_ft[0]], writes=[B_rhl])
                    P.op('dve', lambda e, rec=rec: e.tensor_tensor(out=rhl[64:65, 512:1024], in0=rec[64:65, :], in1=rhl[64:65, 0:512],
                                                                   op=ALU.subtract), reads=[B_ft[0], B_rhl], writes=[B_rhl])
                    BC = psum[5]
                    P.op('pe', lambda e, BC=BC: e.matmul(BC[0:64, :], lhsT=ones_bf[64:65, 0:64], rhs=rhl[64:65, 0:512], start=True, stop=False),
                         reads=[B_rhl, B_const], writes=[Bps[5]])
                    P.op('pe', lambda e, BC=BC: e.matmul(BC[0:64, :], lhsT=ones_bf[64:65, 0:64], rhs=rhl[64:65, 512:1024], start=False, stop=True),
                         reads=[B_rhl, B_const], writes=[Bps[5]])
                    P.op('act', lambda e, O=O: e.activation(out=ftmp[1][0:64, :], in_=O[0:64, :], func=AF.Copy), reads=[Bps[oi]], writes=[B_ft[1]])
                    P.op('dve', lambda e, BC=BC, kv=kv, qblk=qblk: e.tensor_tensor(
                        out=OA[:, kv * 4:(kv + 1) * 4, qblk * 128:(qblk + 1) * 128],
                        in0=ftmp[1][0:64, :].rearrange("p (g q) -> p g q", g=4),
                        in1=BC[0:64, :].rearrange("p (g q) -> p g q", g=4), op=ALU.mult),
                        reads=[B_ft[1], Bps[5]], writes=[B_OA[kv][qblk]])
            cs = [8.0 * 2.0 ** (-2.0 * (h + 1)) for h in range(4)]
            for h in range(4):
                for qg in range(4):
                    if dbg is not None and (h, qg) not in dbg['B']:
                        continue
                    qsl = []
                    for m in range(2):
                        u = m * 4 + h
                        qs, bq, sq_ = qring.next()
                        P.op('sp', lambda e, qs=qs, u=u, qg=qg: e.dma_start(out=qs[0:64, :], in_=qb[u * 64:(u + 1) * 64, tokslice(qg)]),
                             reads=[Bqb], writes=[bq], dsem=sq_)
                        P.op('sp', lambda e, qs=qs, h=h, qg=qg: e.dma_start(out=qs[64:68, :], in_=qaug_d[h, :, tokslice(qg)]),
                             writes=[bq], dsem=sq_)
                        qsl.append((qs, bq))
                    OZ = [(psum[3], psum[4], 3, 4), (psum[5], psum[6], 5, 6)]
                    first_t = [True, True]
                    ntile = 32 * 4 + 4

                    def tile_ops(m, kslot, bk, nrows, j, vs, bv, diag_j, is_last):
                        nonlocal scnt, pcnt
                        qs, bq = qsl[m]
                        si = scnt % 3
                        scnt += 1
                        Sp = psum[si]
                        P.op('pe', lambda e, Sp=Sp, kslot=kslot, qs=qs, j=j, nrows=nrows: e.matmul(
                            Sp[:, :], lhsT=kslot[0:nrows, j * 128:(j + 1) * 128], rhs=qs[0:nrows, :], start=True, stop=True),
                            reads=[bk, bq], writes=[Bps[si]])
                        pt, bpt = pts[pcnt % 4], B_pt[pcnt % 4]
                        pcnt += 1
                        if diag_j is None:
                            P.op('act', lambda e, pt=pt, Sp=Sp: e.activation(out=pt, in_=Sp[:, :], func=AF.Exp, scale=0.125),
                                 reads=[Bps[si]], writes=[bpt])
                        else:
                            tb, btb = ftmp[2 + pcnt % 2], B_ft[2 + pcnt % 2]
                            a0 = 384 - 128 * diag_j
                            P.op('dve', lambda e, tb=tb, Sp=Sp, a0=a0, h=h: e.scalar_tensor_tensor(
                                out=tb, in0=atab[:, a0:a0 + 512], scalar=-cs[h], in1=Sp[:, :], op0=ALU.mult, op1=ALU.add),
                                reads=[B_atab, Bps[si]], writes=[btb])
                            P.op('act', lambda e, pt=pt, tb=tb: e.activation(out=pt, in_=tb, func=AF.Exp, scale=0.125),
                                 reads=[btb], writes=[bpt])
                        Oq, Zq, oi, zi = OZ[m]
                        st_ = first_t[m]
                        first_t[m] = False
                        P.op('pe', lambda e, Oq=Oq, vs=vs, pt=pt, j=j, st_=st_, is_last=is_last: e.matmul(
                            Oq[:, :], lhsT=vs[:, j, :], rhs=pt, start=st_, stop=is_last), reads=[bv, bpt], writes=[Bps[oi]])
                        P.op('pe', lambda e, Zq=Zq, pt=pt, st_=st_, is_last=is_last: e.matmul(
                            Zq[:, :], lhsT=ones_bf[:, :], rhs=pt, start=st_, stop=is_last), reads=[bpt, B_const], writes=[Bps[zi]])

                    for ch in range(32):
                        r, cc = ch // 4, ch % 4
                        kss = []
                        for m in range(2):
                            u = m * 4 + h
                            ks, bk, sk = kring.next()
                            r0 = r * KROWS + 128 + u * 64
                            P.op('sp', lambda e, ks=ks, r0=r0, cc=cc: e.dma_start(out=ks[0:64, :], in_=kta[r0:r0 + 64, cc * 512:(cc + 1) * 512]),
                                 reads=[Bkta], writes=[bk], dsem=sk)
                            P.op('sp', lambda e, ks=ks, h=h, qg=qg, ch=ch: e.dma_start(out=ks[64:68, :], in_=kaug_d[h, qg, :, ch * 512:(ch + 1) * 512]),
                                 writes=[bk], dsem=sk)
                            kss.append((ks, bk))
                        vs, bv, sv_ = vringB.next()
                        P.op('sp', lambda e, vs=vs, ch=ch, h=h: e.dma_start(
                            out=vs, in_=va[ch * 512:(ch + 1) * 512, 128 + h * 128:128 + (h + 1) * 128].rearrange("(j p) c -> p j c", p=128)),
                            reads=[Bva], writes=[bv], dsem=sv_)
                        for j in range(4):
                            for m in range(2):
                                tile_ops(m, kss[m][0], kss[m][1], 68, j, vs, bv, None, False)
                    kss = []
                    for m in range(2):
                        u = m * 4 + h
                        ks, bk, sk = kring.next()
                        P.op('sp', lambda e, ks=ks, u=u, qg=qg: e.dma_start(out=ks[0:64, :], in_=ktl[128 + u * 64:128 + (u + 1) * 64, tokslice(qg)]),
                             reads=[Bktl], writes=[bk], dsem=sk)
                        kss.append((ks, bk))
                    vs, bv, sv_ = vringB.next()
                    P.op('sp', lambda e, vs=vs, qg=qg, h=h: e.dma_start(
                        out=vs, in_=vl[qg * 512:(qg + 1) * 512, 128 + h * 128:128 + (h + 1) * 128].rearrange("(j p) c -> p j c", p=128)),
                        reads=[Bvl], writes=[bv], dsem=sv_)
                    for j in range(4):
                        for m in range(2):
                            tile_ops(m, kss[m][0], kss[m][1], 64, j, vs, bv, j, j == 3)
                    (O0, Z0, o0, z0), (O1, Z1, o1, z1) = OZ
                    P.op('dve', lambda e, Z0=Z0: e.reciprocal(out=ftmp[4], in_=Z0[:, :]), reads=[Bps[z0]], writes=[B_ft[4]])
                    P.op('dve', lambda e, O0=O0: e.tensor_tensor(out=ftmp[5], in0=O0[:, :], in1=ftmp[4], op=ALU.mult),
                         reads=[Bps[o0], B_ft[4]], writes=[B_ft[5]])
                    P.op('dve', lambda e, Z1=Z1: e.reciprocal(out=ftmp[6], in_=Z1[:, :]), reads=[Bps[z1]], writes=[B_ft[6]])
                    P.op('dve', lambda e, O1=O1: e.tensor_tensor(out=ftmp[7], in0=O1[:, :], in1=ftmp[6], op=ALU.mult),
                         reads=[Bps[o1], B_ft[6]], writes=[B_ft[7]])
                    P.op('dve', lambda e: e.scalar_tensor_tensor(out=ftmp[5], in0=ftmp[7], scalar=misc[:, mb + 1:mb + 2], in1=ftmp[5],
                                                                 op0=ALU.mult, op1=ALU.add),
                         reads=[B_ft[7], B_ft[5], B_misc], writes=[B_ft[5]])
                    P.op('act', lambda e: e.activation(out=sqb, in_=ftmp[5], func=AF.Square), reads=[B_ft[5]], writes=[B_sqb])
                    X = psum[7]
                    P.op('pe', lambda e, X=X: e.matmul(X[:, :], lhsT=ones_bf[:, :], rhs=sqb, start=True, stop=True),
                         reads=[B_sqb, B_const], writes=[Bps[7]])
                    P.op('act', lambda e, X=X: e.activation(out=ftmp[8], in_=X[:, :], func=AF.Sqrt, scale=1.0 / 128, bias=EPS),
                         reads=[Bps[7]], writes=[B_ft[8]])
                    P.op('dve', lambda e: e.reciprocal(out=ftmp[9], in_=ftmp[8]), reads=[B_ft[8]], writes=[B_ft[9]])
                    P.op('dve', lambda e, h=h, qg=qg: e.scalar_tensor_tensor(
                        out=OB[:, h, tokslice(qg)], in0=ftmp[5], scalar=misc[:, mb + 2:mb + 3], in1=ftmp[9], op0=ALU.mult, op1=ALU.mult),
                        reads=[B_ft[5], B_ft[9], B_misc], writes=[B_OB[h][qg]])

        def phase3(l):
            P.barrier()
            o = 24576
            nT = BFA[:, o:o + 8192].rearrange("p (k t) -> p k t", k=8)
            o += 8192
            mT = BFA[:, o:o + 8192].rearrange("p (k t) -> p k t", k=8)
            o += 8192
            sq_aps = [BFA[:, o:o + 512], BFA[:, o + 512:o + 1024]]
            o += 1024
            assert o <= WR_OFF
            B_nT = [[P.buf('nT') for _ in range(2)] for _ in range(8)]
            B_mT = [[P.buf('mT') for _ in range(2)] for _ in range(8)]
            B_sq = [P.buf('sq0'), P.buf('sq1')]
            bgo = VOFF['bg%d' % l]
            cnt = 0
            for half in range(2):
                norm_to_nT(l, 1, half, nT, B_nT, lambda kc: dcol(l, 1, kc), lambda kc: mcol(l, 3, kc), sq_aps, B_sq)
                for dc in range(8):
                    s1, b1, e1 = wring.next()
                    P.op('pool', lambda e, s1=s1, dc=dc: e.dma_start(out=s1.rearrange("p (a x) -> p a x", a=2),
                                                                     in_=Wd['wgate%d' % l][dc].rearrange("p (a x) -> p a x", a=2)),
                         writes=[b1], dsem=e1)
                    g1v = s1.rearrange("p (a k f) -> p a k f", a=2, k=8)
                    s2, b2, e2 = wring.next()
                    P.op('pool', lambda e, s2=s2, dc=dc: e.dma_start(out=s2[0:64, 0:1024], in_=Wd['wba%d' % l][dc]), writes=[b2], dsem=e2)
                    P.op('pool', lambda e, s2=s2, dc=dc: e.dma_start(out=s2[:, 1024:1536], in_=Wd['wbb%d' % l][dc]), writes=[b2], dsem=e2)
                    for tg in range(2):
                        g = half * 2 + tg
                        GA, GB, YA, YB = psum[0 + 4 * (cnt % 2)], psum[1 + 4 * (cnt % 2)], psum[2 + 4 * (cnt % 2)], psum[3 + 4 * (cnt % 2)]
                        ia = [0 + 4 * (cnt % 2), 1 + 4 * (cnt % 2), 2 + 4 * (cnt % 2), 3 + 4 * (cnt % 2)]
                        fo = 4 * (cnt % 2)
                        cnt += 1
                        for a, PS, bi in ((0, GA, ia[0]), (1, GB, ia[1])):
                            for kc in range(8):
                                P.op('pe', lambda e, PS=PS, g1v=g1v, a=a, kc=kc, tg=tg: e.matmul(
                                    PS[:, :], lhsT=g1v[:, a, kc, :], rhs=nT[:, kc, tokslice(tg)], start=(kc == 0), stop=(kc == 7)),
                                    reads=[b1, B_nT[kc][tg]], writes=[Bps[bi]])
                        for hd in range(8):
                            P.op('pe', lambda e, YA=YA, s2=s2, hd=hd, g=g: e.matmul(
                                YA[:, :], lhsT=s2[0:64, hd * 128:(hd + 1) * 128], rhs=OA[:, hd, tokslice(g)], start=(hd == 0), stop=(hd == 7)),
                                reads=[b2] + [B_OA[hd // 4][g * 4 + q] for q in range(4)], writes=[Bps[ia[2]]])
                        for hh in range(4):
                            P.op('pe', lambda e, YB=YB, s2=s2, hh=hh, g=g: e.matmul(
                                YB[:, :], lhsT=s2[:, 1024 + hh * 128:1024 + (hh + 1) * 128], rhs=OB[:, hh, tokslice(g)], start=(hh == 0), stop=(hh == 3)),
                                reads=[b2, B_OB[hh][g]], writes=[Bps[ia[3]]])
                        P.op('act', lambda e, GA=GA, dc=dc, fo=fo: e.activation(out=ftmp[fo], in_=GA[:, :], func=AF.Sigmoid,
                                                                               bias=vecs[:, bgo + dc:bgo + dc + 1], scale=1.0),
                             reads=[Bps[ia[0]], B_vecs], writes=[B_ft[fo]])
                        P.op('act', lambda e, GB=GB, dc=dc, fo=fo: e.activation(out=ftmp[fo + 1], in_=GB[:, :], func=AF.Sigmoid,
                                                                               bias=vecs[:, bgo + 8 + dc:bgo + 8 + dc + 1], scale=1.0),
                             reads=[Bps[ia[1]], B_vecs], writes=[B_ft[fo + 1]])
                        P.op('dve', lambda e, YA=YA, fo=fo: e.tensor_tensor(out=ftmp[fo + 2], in0=ftmp[fo], in1=YA[:, :], op=ALU.mult),
                             reads=[B_ft[fo], Bps[ia[2]]], writes=[B_ft[fo + 2]])
                        P.op('dve', lambda e, YB=YB, fo=fo: e.tensor_tensor(out=ftmp[fo + 3], in0=ftmp[fo + 1], in1=YB[:, :], op=ALU.mult),
                             reads=[B_ft[fo + 1], Bps[ia[3]]], writes=[B_ft[fo + 3]])
                        P.op('dve', lambda e, dc=dc, tg=tg, fo=fo: e.tensor_tensor(out=mT[:, dc, tokslice(tg)], in0=ftmp[fo + 2], in1=ftmp[fo + 3], op=ALU.add),
                             reads=[B_ft[fo + 2], B_ft[fo + 3]], writes=[B_mT[dc][tg]])
                for dco in range(8):
                    s1, b1, e1 = wring.next()
                    P.op('pool', lambda e, s1=s1, dco=dco: e.dma_start(out=s1[:, 0:1024], in_=Wd['wo%d' % l][dco]), writes=[b1], dsem=e1)
                    for tg in range(2):
                        g = half * 2 + tg
                        yi = 4 * (cnt % 2)
                        cnt += 1
                        Y = psum[yi]
                        for kc in range(8):
                            P.op('pe', lambda e, Y=Y, s1=s1, kc=kc, tg=tg: e.matmul(
                                Y[:, :], lhsT=s1[:, kc * 128:(kc + 1) * 128], rhs=mT[:, kc, tokslice(tg)], start=(kc == 0), stop=(kc == 7)),
                                reads=[b1, B_mT[kc][tg]], writes=[Bps[yi]])
                        P.op('dve', lambda e, Y=Y, dco=dco, g=g: e.scalar_tensor_tensor(
                            out=hT[:, dco, tokslice(g)], in0=Y[:, :], scalar=mcol(l, 5, dco), in1=hT[:, dco, tokslice(g)],
                            op0=ALU.mult, op1=ALU.add), reads=[Bps[yi], B_h[dco][g], B_mod], writes=[B_h[dco][g]])

        def final_out():
            P.barrier()
            B_o = P.buf('outd')
            if not last:
                for c in range(8):
                    for g in range(4):
                        P.op('sp', lambda e, c=c, g=g: e.dma_start(out=h_out[c * 128:(c + 1) * 128, tokslice(g)], in_=hT[:, c, tokslice(g)]),
                             reads=[B_h[c][g]], writes=[B_o], dsem=d_out)
            else:
                sq_aps = [BFA[:, 0:512], BFA[:, 512:1024]]
                B_sq = [P.buf('sq0'), P.buf('sq1')]
                ST = psum[6]
                fgo = VOFF['fg']
                for g in range(4):
                    for kc in range(8):
                        sq, bsq = sq_aps[kc % 2], B_sq[kc % 2]
                        P.op('dve', lambda e, sq=sq, kc=kc, g=g: e.tensor_tensor(out=sq, in0=hT[:, kc, tokslice(g)], in1=hT[:, kc, tokslice(g)], op=ALU.mult),
                             reads=[B_h[kc][g]], writes=[bsq])
                        P.op('pe', lambda e, sq=sq, kc=kc: e.matmul(ST[:, :], lhsT=ones_bf[:, :], rhs=sq, start=(kc == 0), stop=(kc == 7)),
                             reads=[bsq, B_const], writes=[Bps[6]])
                    P.op('act', lambda e: e.activation(out=ftmp[8], in_=ST[:, :], func=AF.Sqrt, scale=1.0 / D, bias=EPS), reads=[Bps[6]], writes=[B_ft[8]])
                    P.op('dve', lambda e: e.reciprocal(out=ftmp[9], in_=ftmp[8]), reads=[B_ft[8]], writes=[B_ft[9]])
                    for kc in range(8):
                        tmp, btmp = ftmp[kc % 4], B_ft[kc % 4]
                        P.op('dve', lambda e, tmp=tmp, kc=kc, g=g: e.scalar_tensor_tensor(
                            out=tmp, in0=hT[:, kc, tokslice(g)], scalar=vecs[:, fgo + kc:fgo + kc + 1], in1=ftmp[9], op0=ALU.mult, op1=ALU.mult),
                            reads=[B_h[kc][g], B_ft[9], B_vecs], writes=[btmp])
                        P.op('sp', lambda e, tmp=tmp, kc=kc, g=g: e.dma_start(out=h_out[kc * 128:(kc + 1) * 128, tokslice(g)], in_=tmp),
                             reads=[btmp], writes=[B_o], dsem=d_out)
            P.op('sp', None, reads=[B_o])

        for l in range(L):
            if 'p1_%d' % l in segs:
                ffn(l, 0)
                phase1b(l)
            if fused[l]:
                gather(l)
            if 'p23_%d' % l in segs:
                phase2(l)
                if dbg is not None:
                    P.barrier()
                    d_oa = dram('dbg_oa', [64, 16384], BF16, 'ExternalOutput')
                    d_ob = dram('dbg_ob', [128, 8192], BF16, 'ExternalOutput')
                    B_dbg = P.buf('dbgo')
                    P.op('sp', lambda e: e.dma_start(out=d_oa, in_=BFA[0:64, 0:16384]), writes=[B_dbg], dsem=d_out)
                    P.op('sp', lambda e: e.dma_start(out=d_ob, in_=BFA[:, 16384:24576]), writes=[B_dbg], dsem=d_out)
                    P.op('sp', None, reads=[B_dbg])
                    continue
                phase3(l)
                ffn(l, 1)
        final_out()
        P.op('sp', None, reads=[B_scr[k] for k in B_scr])
        P.emit(st)
    nc._w_names = list(Wd.keys())
    return nc


_CACHE = {}


def _get_nc(segs):
    key = tuple(segs)
    if key not in _CACHE:
        _CACHE[key] = build(segs)
    return _CACHE[key]


FUSED = False


def kernel(x, c, ada_w, ada_b, norm_g, ffn_wg, ffn_wu, ffn_wd, w_in, qk_g, lam_p, subln_g, w_ba, w_bb,
           w_gate, b_gate, w_o, final_g):
    f = lambda a: np.asarray(a, dtype=np.float32)
    x, c, ada_w, ada_b, norm_g = f(x), f(c), f(ada_w), f(ada_b), f(norm_g)
    ffn_wg, ffn_wu, ffn_wd, w_in = f(ffn_wg), f(ffn_wu), f(ffn_wd), f(w_in)
    qk_g, lam_p, subln_g, w_ba, w_bb = f(qk_g), f(lam_p), f(subln_g), f(w_ba), f(w_bb)
    w_gate, b_gate, w_o, final_g = f(w_gate), f(b_gate), f(w_o), f(final_g)
    vecs = _host_vecs(c, ada_b, norm_g, b_gate, qk_g, subln_g, lam_p, final_g)
    W = _host_weights(ffn_wg, ffn_wu, ffn_wd, w_in, w_ba, w_bb, w_gate, w_o, ada_w)
    atab = _host_atab()
    common = []
    for core in range(NCORES):
        qaug, kaug = _host_alibi(core)
        common.append({'vecs': vecs, 'rope': _host_rope(core), 'qaug': qaug, 'kaug': kaug, 'atab': atab})
    xT = [np.ascontiguousarray(x[0, core * T:(core + 1) * T, :].T) for core in range(NCORES)]
    cores = list(range(NCORES))

    def wsel(nc_keys):
        return {k: W[k] for k in nc_keys}

    def run(segs, extra):
        nc = _get_nc(segs)
        wk = nc._w_names
        in_maps = []
        for core in cores:
            m = dict(common[core])
            m.update(wsel(wk))
            m.update(extra[core])
            in_maps.append(m)
        return run_bass_kernel_spmd(nc, in_maps, core_ids=cores).results

    if FUSED:
        res = run(['p1_0', 'p23_0', 'p1_1', 'p23_1', 'fin'], [{'xT': xT[i]} for i in cores])
    else:
        r0 = run(['p1_0'], [{'xT': xT[i]} for i in cores])

        def gathered(r, l):
            kta = np.concatenate([r[i]['ktl%d' % l] for i in cores], axis=0)
            va = np.concatenate([r[i]['vl%d' % l] for i in cores], axis=0)
            return kta, va
        kta, va = gathered(r0, 0)
        r1 = run(['p23_0', 'p1_1'], [{'h_in': r0[i]['h_out'], 'qa0': r0[i]['qa0'], 'qb0': r0[i]['qb0'], 'ktl0': r0[i]['ktl0'],
                                      'vl0': r0[i]['vl0'], 'kta0': kta, 'va0': va} for i in cores])
        kta, va = gathered(r1, 1)
        res = run(['p23_1', 'fin'], [{'h_in': r1[i]['h_out'], 'qa1': r1[i]['qa1'], 'qb1': r1[i]['qb1'], 'ktl1': r1[i]['ktl1'],
                                      'vl1': r1[i]['vl1'], 'kta1': kta, 'va1': va} for i in cores])
    out = np.concatenate([res[i]['outT'].T for i in cores], axis=0)[None]
    return np.ascontiguousarray(out.astype(np.float32))
```

```python
import contextlib
import numpy as np
import ml_dtypes
import concourse.bass as bass
import concourse.mybir as mybir
from concourse.bass_utils import run_bass_kernel_spmd

F32 = mybir.dt.float32
BF16 = mybir.dt.bfloat16
AF = mybir.ActivationFunctionType
ALU = mybir.AluOpType

NCORES = 8
D = 1024
S = 16384
T = S // NCORES
L = 2
FF = 2816
NFC = FF // 128
HD = 64
EPS = 1e-6
KROWS = 640
VW = 640
BIGNEG = -16384.0

ENGS = ('pe', 'act', 'dve', 'pool', 'sp')
SEM_LIMIT = 20000


class DmaSem:
    def __init__(self, name):
        self.name = name
        self.count = 0
        self.handle = None


class Buf:
    __slots__ = ('name', 'last_w', 'rd_eng', 'rd_dma')

    def __init__(self, name):
        self.name = name
        self.last_w = None
        self.rd_eng = {}
        self.rd_dma = {}


class Op:
    __slots__ = ('eng', 'fn', 'deps', 'idx', 'needed', 'dsem', 'semval', 'semidx')

    def __init__(self, eng, fn, idx):
        self.eng = eng
        self.fn = fn
        self.idx = idx
        self.deps = []
        self.needed = False
        self.dsem = None
        self.semval = None
        self.semidx = None


class Prog:
    def __init__(self, nc):
        self.nc = nc
        self.ops = {e: [] for e in ENGS}
        self.seen = {e: {} for e in ENGS}
        self.dsems = []

    def dsem(self, name):
        s = DmaSem('%s_%d' % (name, len(self.dsems)))
        self.dsems.append(s)
        return s

    def buf(self, name):
        return Buf(name)

    def _add_dep(self, op, tok):
        if tok is None:
            return
        kind, key, val = tok
        if kind == 'eng' and key == op.eng and key == 'pe':
            return
        seen = self.seen[op.eng]
        if seen.get(key, -1) >= val:
            return
        seen[key] = val
        op.deps.append(tok)
        if kind == 'eng':
            self.ops[key][val].needed = True

    def op(self, eng, fn, reads=(), writes=(), dsem=None):
        lst = self.ops[eng]
        o = Op(eng, fn, len(lst))
        if dsem is not None:
            dsem.count += 16
            o.dsem = dsem
            tok = ('dma', dsem, dsem.count)
        else:
            tok = ('eng', eng, o.idx)
        lst.append(o)
        for r in reads:
            self._add_dep(o, r.last_w)
        for w in writes:
            self._add_dep(o, w.last_w)
            for e, i in w.rd_eng.items():
                self._add_dep(o, ('eng', e, i))
            for s, c in w.rd_dma.items():
                self._add_dep(o, ('dma', s, c))
        for r in reads:
            if dsem is not None:
                r.rd_dma[dsem] = dsem.count
            else:
                r.rd_eng[eng] = o.idx
        for w in writes:
            w.last_w = tok
            w.rd_eng = {}
            w.rd_dma = {}
        return o

    def barrier(self):
        toks = []
        for e in ENGS:
            if e == 'sp':
                continue
            if self.ops[e]:
                for o in reversed(self.ops[e]):
                    if o.fn is not None and o.dsem is None:
                        toks.append(('eng', e, o.idx))
                        break
        for s in self.dsems:
            if s.count > 0:
                toks.append(('dma', s, s.count))
        for e in ENGS:
            o = Op(e, None, len(self.ops[e]))
            self.ops[e].append(o)
            for t in toks:
                if t[0] == 'eng' and t[1] == e:
                    continue
                self._add_dep(o, t)

    def emit(self, stack):
        nc = self.nc
        for s in self.dsems:
            if s.count > 0:
                s.handle = stack.enter_context(nc.semaphore('d_' + s.name))
        esems = {}
        for e in ENGS:
            cnt = 0
            si = 0
            anyn = False
            for o in self.ops[e]:
                if o.needed and o.dsem is None:
                    anyn = True
                    if cnt >= SEM_LIMIT:
                        si += 1
                        cnt = 0
                    cnt += 1
                    o.semidx = si
                    o.semval = cnt
            esems[e] = [stack.enter_context(nc.semaphore('e_%s_%d' % (e, k)))
                        for k in range(si + 1 if anyn else 0)]
        block = stack.enter_context(nc.Block())
        starters = {'pe': block.tensor, 'act': block.scalar, 'dve': block.vector,
                    'pool': block.gpsimd, 'sp': block.sync}
        for e in ENGS:
            if not self.ops[e]:
                continue

            def body(h, e=e):
                for o in self.ops[e]:
                    for kind, key, val in o.deps:
                        if kind == 'eng':
                            po = self.ops[key][val]
                            h.wait_ge(esems[key][po.semidx], po.semval)
                        else:
                            h.wait_ge(key.handle, val)
                    if o.fn is None:
                        continue
                    ins = o.fn(h)
                    if o.dsem is not None:
                        ins.then_inc(o.dsem.handle, 16)
                    elif o.needed:
                        ins.then_inc(esems[e][o.semidx], 1)
            starters[e](body)


class Ring:
    def __init__(self, P, name, aps):
        self.aps = aps
        self.bufs = [P.buf('%s%d' % (name, i)) for i in range(len(aps))]
        self.sems = [P.dsem('%s%d' % (name, i)) for i in range(len(aps))]
        self.i = 0

    def next(self):
        k = self.i % len(self.aps)
        self.i += 1
        return self.aps[k], self.bufs[k], self.sems[k]


def _vec_offsets():
    off = {}
    n = 0

    def add(k, w):
        nonlocal n
        off[k] = n
        n += w
    add('c', 8)
    for l in range(L):
        add('adab%d' % l, 72)
        add('ng%d' % l, 24)
        add('bg%d' % l, 16)
        add('qkg%d' % l, 4)
        add('sub%d' % l, 1)
        add('lamp%d' % l, 4)
    add('fg', 8)
    return off, n


VOFF, NV = _vec_offsets()


def _rope_src():
    d = np.arange(64)
    dd = d % 32
    return np.where(dd < 16, d + 16, d - 16)


def _fm(v):
    return np.ascontiguousarray(v.reshape(-1, 128).T)


def _host_vecs(c, ada_b, norm_g, b_gate, qk_g, subln_g, lam_p, final_g):
    v = np.zeros((128, NV), np.float32)
    v[:, VOFF['c']:VOFF['c'] + 8] = _fm(c[0])
    src = _rope_src()
    for l in range(L):
        v[:, VOFF['adab%d' % l]:VOFF['adab%d' % l] + 72] = _fm(ada_b[l])
        for k in range(3):
            o = VOFF['ng%d' % l] + 8 * k
            v[:, o:o + 8] = _fm(norm_g[l, k])
        v[:, VOFF['bg%d' % l]:VOFF['bg%d' % l] + 16] = _fm(b_gate[l])
        o = VOFF['qkg%d' % l]
        p = np.arange(128) % 64
        v[:, o + 0] = qk_g[l, 0][p]
        v[:, o + 1] = qk_g[l, 0][src[p]]
        v[:, o + 2] = qk_g[l, 1][p]
        v[:, o + 3] = qk_g[l, 1][src[p]]
        v[:, VOFF['sub%d' % l]] = subln_g[l]
        v[0:64, VOFF['lamp%d' % l]:VOFF['lamp%d' % l] + 4] = lam_p[l].T
    v[:, VOFF['fg']:VOFF['fg'] + 8] = _fm(final_g)
    return v


def _host_rope(core):
    t = core * T + np.arange(T)
    row = (t // 64 - (S // 64) // 2).astype(np.float32)
    col = (t % 64 - 32).astype(np.float32)
    inv = (1.0 / (np.float32(10000.0) ** (np.arange(0, 32, 2, dtype=np.float32) / np.float32(32)))).astype(np.float32)
    out = np.zeros((128, 2, T), np.float32)
    for p in range(128):
        d = p % 64
        sec = d // 32
        dd = d % 32
        j = dd % 16
        pos = row if sec == 0 else col
        ang = (pos * inv[j]).astype(np.float32)
        out[p, 0] = np.cos(ang)
        out[p, 1] = -np.sin(ang) if dd < 16 else np.sin(ang)
    return out


def _host_alibi(core):
    bf = ml_dtypes.bfloat16
    cs = [8.0 * 2.0 ** (-2.0 * (h + 1)) for h in range(4)]
    q = core * T + np.arange(T)
    qaug = np.zeros((4, 4, T), np.float32)
    k = np.arange(S)
    kaug = np.zeros((4, 4, 4, S), np.float32)
    for h in range(4):
        c = cs[h]
        qaug[h, 0] = -c * 128.0 * (q // 128)
        qaug[h, 1] = -c * (q % 128)
        qaug[h, 2] = 1.0
        qaug[h, 3] = 1.0
        for g in range(4):
            q0 = core * T + g * 512
            sg = np.where(k < q0, 1.0, -1.0)
            diag = (k >= q0) & (k < q0 + 512)
            r0 = sg.copy()
            r1 = sg.copy()
            r2 = sg * c * 128.0 * (k // 128)
            r3 = sg * c * (k % 128)
            r0[diag] = 0.0
            r1[diag] = 0.0
            r2[diag] = BIGNEG
            r3[diag] = 0.0
            kaug[h, g, 0], kaug[h, g, 1], kaug[h, g, 2], kaug[h, g, 3] = r0, r1, r2, r3
    assert np.array_equal(qaug.astype(bf).astype(np.float32), qaug)
    assert np.array_equal(kaug.astype(bf).astype(np.float32), kaug)
    return qaug.astype(bf), kaug.astype(bf)


def _host_atab():
    kk = np.arange(128)[:, None]
    y = np.arange(896)[None, :]
    return np.abs(y - 384 - kk).astype(np.float32)


def _host_weights(ffn_wg, ffn_wu, ffn_wd, w_in, w_ba, w_bb, w_gate, w_o, ada_w):
    W = {}
    src = _rope_src()
    for l in range(L):
        for i in range(2):
            g = ffn_wg[l, i].reshape(8, 128, NFC, 128)
            u = ffn_wu[l, i].reshape(8, 128, NFC, 128)
            gu = np.stack([g, u], 0)
            W['wgu%d%d' % (l, i)] = np.ascontiguousarray(gu.transpose(3, 2, 0, 1, 4)).reshape(NFC, 128, 2048)
            d = ffn_wd[l, i].reshape(NFC, 128, 8, 128)
            W['wd%d%d' % (l, i)] = np.ascontiguousarray(d.transpose(2, 1, 0, 3)).reshape(8, 128, FF)
        wi = w_in[l]
        qa, ka, va, qd, kd, vd = np.split(wi, [512, 640, 768, 1280, 1792], axis=1)
        permq = (np.arange(512) // 64) * 64 + src[np.arange(512) % 64]
        permk = (np.arange(128) // 64) * 64 + src[np.arange(128) % 64]
        fmc = np.concatenate([qa, qa[:, permq], ka, ka[:, permk], qd, kd], axis=1)
        nch = fmc.shape[1] // 128
        x = fmc.reshape(8, 128, nch, 128)
        W['win%d' % l] = np.ascontiguousarray(x.transpose(2, 1, 0, 3)).reshape(nch, 128, 1024)
        wv = np.concatenate([va, vd], axis=1).reshape(8, 128, VW)
        W['wv%d' % l] = np.ascontiguousarray(wv.transpose(1, 0, 2)).reshape(128, 8 * VW)
        wg = w_gate[l].reshape(8, 128, 2, 8, 128)
        W['wgate%d' % l] = np.ascontiguousarray(wg.transpose(3, 1, 2, 0, 4)).reshape(8, 128, 2048)
        wa = w_ba[l].reshape(8, 64, 8, 128)
        W['wba%d' % l] = np.ascontiguousarray(wa.transpose(2, 1, 0, 3)).reshape(8, 64, 1024)
        wb = w_bb[l].reshape(4, 128, 8, 128)
        W['wbb%d' % l] = np.ascontiguousarray(wb.transpose(2, 1, 0, 3)).reshape(8, 128, 512)
        wo = w_o[l].reshape(8, 128, 8, 128)
        W['wo%d' % l] = np.ascontiguousarray(wo.transpose(2, 1, 0, 3)).reshape(8, 128, 1024)
        aw = ada_w[l].reshape(8, 128, 9 * D)
        W['ada%d' % l] = np.ascontiguousarray(aw.transpose(1, 0, 2))
    return W


WSHAPES = {}
for _l in range(L):
    for _i in range(2):
        WSHAPES['wgu%d%d' % (_l, _i)] = [NFC, 128, 2048]
        WSHAPES['wd%d%d' % (_l, _i)] = [8, 128, FF]
    WSHAPES['win%d' % _l] = [18, 128, 1024]
    WSHAPES['wv%d' % _l] = [128, 8 * VW]
    WSHAPES['wgate%d' % _l] = [8, 128, 2048]
    WSHAPES['wba%d' % _l] = [8, 64, 1024]
    WSHAPES['wbb%d' % _l] = [8, 128, 512]
    WSHAPES['wo%d' % _l] = [8, 128, 1024]
    WSHAPES['ada%d' % _l] = [128, 8, 9 * D]

CH_QA, CH_QAP, CH_KA, CH_KAP, CH_QD, CH_KD = 0, 4, 8, 9, 10, 14


def build(segs, dbg=None):
    segs = set(segs)
    nc = bass.Bass("TRN2", target_bir_lowering=False)
    fused = {l: ('p1_%d' % l in segs and 'p23_%d' % l in segs) for l in range(L)}
    layers_used = [l for l in range(L) if ('p1_%d' % l in segs or 'p23_%d' % l in segs)]

    def dram(name, shape, dt, kind):
        return nc.dram_tensor(name, shape, dt, kind=kind).ap()

    first = 'p1_0' in segs
    last = 'fin' in segs
    h_in = dram('xT' if first else 'h_in', [D, T], F32, 'ExternalInput')
    h_out = dram('outT' if last else 'h_out', [D, T], F32, 'ExternalOutput')
    vecs_d = dram('vecs', [128, NV], F32, 'ExternalInput')
    rope_d = dram('rope', [128, 2, T], F32, 'ExternalInput')
    qaug_d = dram('qaug', [4, 4, T], BF16, 'ExternalInput')
    kaug_d = dram('kaug', [4, 4, 4, S], BF16, 'ExternalInput')
    atab_d = dram('atab', [128, 896], F32, 'ExternalInput')
    Wd = {}
    for k, shp in WSHAPES.items():
        l = int(k[-2]) if k.startswith('wgu') or (k.startswith('wd') and len(k) == 4) else int(k[-1])
        if l in layers_used:
            Wd[k] = dram(k, shp, F32, 'ExternalInput')
    SC = {}
    for l in layers_used:
        p1 = 'p1_%d' % l in segs
        p23 = 'p23_%d' % l in segs
        if fused[l]:
            kl = kg = 'Internal'
        elif p1:
            kl, kg = 'ExternalOutput', None
        else:
            kl, kg = 'ExternalInput', 'ExternalInput'
        SC['qa%d' % l] = dram('qa%d' % l, [2, 64, 16, 4, 128], BF16, kl)
        SC['qb%d' % l] = dram('qb%d' % l, [512, T], BF16, kl)
        SC['ktl%d' % l] = dram('ktl%d' % l, [KROWS, T], BF16, kl)
        SC['vl%d' % l] = dram('vl%d' % l, [T, VW], BF16, kl)
        if kg is not None:
            SC['kta%d' % l] = dram('kta%d' % l, [NCORES * KROWS, T], BF16, kg)
            SC['va%d' % l] = dram('va%d' % l, [S, VW], BF16, kg)

    st = contextlib.ExitStack()
    with st:
        def sb(name, shape, dt):
            return st.enter_context(nc.sbuf_tensor(name, shape, dt))
        hT = sb('hT', [128, 8, T], F32)
        NBF = 48 * 1024
        NFP = 9 * 1024 + 512
        BFA = sb('bfa', [128, NBF], BF16)
        FPA = sb('fpa', [128, NFP], F32)
        vecs = sb('vecs_sb', [128, NV], F32)
        modT = sb('modT', [128, L * 72], F32)
        dsc = sb('dsc', [128, L * 64], F32)
        misc = sb('misc', [128, 32], F32)
        cact = sb('cact', [128, 8], BF16)
        ones_bf = sb('ones_bf', [128, 128], BF16)
        bd_bf = sb('bd_bf', [128, 128], BF16)
        ones_f = sb('ones_f', [64, 128], F32)
        psum = [st.enter_context(nc.psum_tensor('ps%d' % i, [128, 512], F32)) for i in range(8)]

        P = Prog(nc)
        Bps = [P.buf('ps%d' % i) for i in range(8)]
        B_h = [[P.buf('h%d_%d' % (c, g)) for g in range(4)] for c in range(8)]
        B_vecs, B_mod, B_dsc, B_misc, B_cact = P.buf('vecs'), P.buf('mod'), P.buf('dsc'), P.buf('misc'), P.buf('cact')
        B_const = P.buf('const')
        d_misc = P.dsem('misc')
        d_h = P.dsem('hload')
        d_out = P.dsem('out')
        d_sc = P.dsem('scratch')
        B_scr = {k: P.buf(k) for k in SC}

        def bf_view(off, n, **kw):
            return BFA[:, off:off + n]

        def tokslice(g):
            return slice(g * 512, (g + 1) * 512)

        P.op('sp', lambda e: e.dma_start(out=vecs[:], in_=vecs_d), writes=[B_vecs], dsem=d_misc)
        for c in range(8):
            for g in range(4):
                P.op('sp', lambda e, c=c, g=g: e.dma_start(out=hT[:, c, tokslice(g)], in_=h_in[c * 128:(c + 1) * 128, tokslice(g)]),
                     writes=[B_h[c][g]], dsem=d_h)
        P.op('dve', lambda e: e.memset(ones_bf[:], 1.0), writes=[B_const])
        P.op('dve', lambda e: e.memset(bd_bf[:], 0.0), writes=[B_const])
        P.op('dve', lambda e: e.memset(bd_bf[0:64, 0:64], 1.0), writes=[B_const])
        P.op('dve', lambda e: e.memset(bd_bf[64:128, 64:128], 1.0), writes=[B_const])
        P.op('dve', lambda e: e.memset(ones_f[:], 1.0), writes=[B_const])

        WR_OFF = NBF - 3 * 2048
        wring = Ring(P, 'wr', [BFA[:, WR_OFF + i * 2048: WR_OFF + (i + 1) * 2048] for i in range(3)])

        def vcol(key, j=0, n=1):
            o = VOFF[key] + j
            return vecs[:, o:o + n]

        P.op('act', lambda e: e.activation(out=cact[:], in_=vcol('c', 0, 8), func=AF.Silu), reads=[B_vecs], writes=[B_cact])
        MOD = psum[7]
        for l in range(L):
            if l not in layers_used:
                continue
            for sp_i in range(36):
                slot, sbuf_, ssem = wring.next()
                sv = slot.rearrange("p (k f) -> p k f", k=8)
                P.op('pool', lambda e, sv=sv, l=l, sp_i=sp_i: e.dma_start(out=sv, in_=Wd['ada%d' % l][:, :, sp_i * 256:(sp_i + 1) * 256]),
                     writes=[sbuf_], dsem=ssem)
                for jj in range(2):
                    j = sp_i * 2 + jj
                    for kc in range(8):
                        P.op('pe', lambda e, sv=sv, jj=jj, kc=kc, col=l * 72 + j: e.matmul(
                            MOD[:, col:col + 1], lhsT=sv[:, kc, jj * 128:(jj + 1) * 128], rhs=cact[:, kc:kc + 1],
                            start=(kc == 0), stop=(kc == 7)), reads=[sbuf_, B_cact], writes=[Bps[7]])
            P.op('dve', lambda e, l=l: e.tensor_tensor(out=modT[:, l * 72:(l + 1) * 72], in0=MOD[:, l * 72:(l + 1) * 72],
                                                       in1=vcol('adab%d' % l, 0, 72), op=ALU.add),
                 reads=[Bps[7], B_vecs], writes=[B_mod])
            base = l * 64
            for k in range(3):
                P.op('dve', lambda e, l=l, k=k, base=base: e.tensor_scalar(
                    out=dsc[:, base + 8 * k: base + 8 * k + 8], in0=modT[:, l * 72 + 24 * k + 8: l * 72 + 24 * k + 16],
                    scalar1=1.0, scalar2=None, op0=ALU.add), reads=[B_mod], writes=[B_dsc])
                P.op('dve', lambda e, l=l, k=k, base=base: e.tensor_tensor(
                    out=dsc[:, base + 8 * k: base + 8 * k + 8], in0=dsc[:, base + 8 * k: base + 8 * k + 8],
                    in1=vcol('ng%d' % l, 8 * k, 8), op=ALU.mult), reads=[B_dsc, B_vecs], writes=[B_dsc])
            for k, mo in ((0, 16), (1, 64)):
                P.op('dve', lambda e, l=l, k=k, mo=mo, base=base: e.tensor_scalar(
                    out=dsc[:, base + 24 + 8 * k: base + 32 + 8 * k], in0=modT[:, l * 72 + mo: l * 72 + mo + 8],
                    scalar1=0.5, scalar2=None, op0=ALU.mult), reads=[B_mod], writes=[B_dsc])
            mb = l * 8
            lam_init = 0.8 - 0.6 * float(np.exp(-0.3 * l))
            lo = VOFF['lamp%d' % l]
            P.op('dve', lambda e, mb=mb, lo=lo: e.tensor_tensor(out=misc[0:64, mb + 3:mb + 4], in0=vecs[0:64, lo:lo + 1],
                                                                 in1=vecs[0:64, lo + 1:lo + 2], op=ALU.mult),
                 reads=[B_vecs], writes=[B_misc])
            P.op('dve', lambda e, mb=mb, lo=lo: e.tensor_tensor(out=misc[0:64, mb + 4:mb + 5], in0=vecs[0:64, lo + 2:lo + 3],
                                                                 in1=vecs[0:64, lo + 3:lo + 4], op=ALU.mult),
                 reads=[B_vecs], writes=[B_misc])
            LP = psum[6]
            P.op('pe', lambda e, mb=mb: e.matmul(LP[:, 0:2], lhsT=ones_f[0:64, :], rhs=misc[0:64, mb + 3:mb + 5],
                                                 start=True, stop=True), reads=[B_misc, B_const], writes=[Bps[6]])
            P.op('act', lambda e, mb=mb: e.activation(out=misc[:, mb + 5:mb + 7], in_=LP[:, 0:2], func=AF.Exp),
                 reads=[Bps[6]], writes=[B_misc])
            P.op('dve', lambda e, mb=mb: e.tensor_tensor(out=misc[:, mb:mb + 1], in0=misc[:, mb + 5:mb + 6],
                                                         in1=misc[:, mb + 6:mb + 7], op=ALU.subtract),
                 reads=[B_misc], writes=[B_misc])
            P.op('dve', lambda e, mb=mb, li=lam_init: e.tensor_scalar(out=misc[:, mb:mb + 1], in0=misc[:, mb:mb + 1],
                                                                      scalar1=li, scalar2=None, op0=ALU.add),
                 reads=[B_misc], writes=[B_misc])
            P.op('dve', lambda e, mb=mb: e.tensor_scalar(out=misc[:, mb + 1:mb + 2], in0=misc[:, mb:mb + 1],
                                                         scalar1=-1.0, scalar2=None, op0=ALU.mult),
                 reads=[B_misc], writes=[B_misc])
            P.op('dve', lambda e, mb=mb, l=l, li=lam_init: e.tensor_scalar(
                out=misc[:, mb + 2:mb + 3], in0=vcol('sub%d' % l), scalar1=1.0 - li, scalar2=None, op0=ALU.mult),
                reads=[B_vecs], writes=[B_misc])

        def dcol(l, which, j):
            o = l * 64 + 8 * which + j
            return dsc[:, o:o + 1]

        def mcol(l, idx, j):
            o = l * 72 + idx * 8 + j
            return modT[:, o:o + 1]

        ftmp = [FPA[:, i * 512:(i + 1) * 512] for i in range(10)]
        B_ft = [P.buf('ft%d' % i) for i in range(10)]
        TAB_OFF = 10 * 512

        def norm_to_nT(l, which, half, nT, B_nT, gsc_fn, sh_fn, sq_aps, B_sq):
            ST = psum[6]
            for tg in range(2):
                g = half * 2 + tg
                for kc in range(8):
                    sq, bsq = sq_aps[kc % 2], B_sq[kc % 2]
                    P.op('dve', lambda e, sq=sq, kc=kc, g=g: e.tensor_tensor(out=sq, in0=hT[:, kc, tokslice(g)],
                                                                             in1=hT[:, kc, tokslice(g)], op=ALU.mult),
                         reads=[B_h[kc][g]], writes=[bsq])
                    P.op('pe', lambda e, sq=sq, kc=kc: e.matmul(ST[:, :], lhsT=ones_bf[:, :], rhs=sq, start=(kc == 0), stop=(kc == 7)),
                         reads=[bsq, B_const], writes=[Bps[6]])
                rs, brs = ftmp[8], B_ft[8]
                P.op('act', lambda e, rs=rs: e.activation(out=rs, in_=ST[:, :], func=AF.Sqrt, scale=1.0 / D, bias=EPS),
                     reads=[Bps[6]], writes=[brs])
                rstd, brstd = ftmp[9], B_ft[9]
                P.op('dve', lambda e, rs=rs, rstd=rstd: e.reciprocal(out=rstd, in_=rs), reads=[brs], writes=[brstd])
                for kc in range(8):
                    tmp, btmp = ftmp[kc % 2], B_ft[kc % 2]
                    P.op('dve', lambda e, tmp=tmp, kc=kc, g=g, rstd=rstd: e.scalar_tensor_tensor(
                        out=tmp, in0=hT[:, kc, tokslice(g)], scalar=gsc_fn(kc), in1=rstd, op0=ALU.mult, op1=ALU.mult),
                        reads=[B_h[kc][g], brstd, B_dsc, B_vecs], writes=[btmp])
                    if sh_fn is not None:
                        P.op('act', lambda e, tmp=tmp, kc=kc, tg=tg: e.activation(
                            out=nT[:, kc, tokslice(tg)], in_=tmp, func=AF.Identity, bias=sh_fn(kc), scale=1.0),
                            reads=[btmp, B_mod], writes=[B_nT[kc][tg]])

        def ffn(l, i):
            P.barrier()
            nT = BFA[:, 0:8192].rearrange("p (k t) -> p k t", k=8)
            hid = BFA[:, 8192:8192 + NFC * 1024].rearrange("p (f t) -> p f t", f=NFC)
            o = 8192 + NFC * 1024
            sq_aps = [BFA[:, o:o + 512], BFA[:, o + 512:o + 1024]]
            o += 1024
            wdr = Ring(P, 'wd', [BFA[:, o + k * FF:o + (k + 1) * FF] for k in range(2)])
            assert o + 2 * FF <= WR_OFF
            B_nT = [[P.buf('nT') for _ in range(2)] for _ in range(8)]
            B_hid = [[P.buf('hid') for _ in range(2)] for _ in range(NFC)]
            B_sq = [P.buf('sq0'), P.buf('sq1')]
            kn = 0 if i == 0 else 2
            hgw = 3 if i == 0 else 4
            wgu = Wd['wgu%d%d' % (l, i)]
            wdd = Wd['wd%d%d' % (l, i)]
            cnt = 0
            for half in range(2):
                norm_to_nT(l, kn, half, nT, B_nT, lambda kc: dcol(l, kn, kc), lambda kc: mcol(l, 3 * kn, kc), sq_aps, B_sq)
                for fc in range(NFC):
                    slot, sbuf_, ssem = wring.next()
                    sv = slot.rearrange("p (a k f) -> p a k f", a=2, k=8)
                    P.op('pool', lambda e, slot=slot, fc=fc: e.dma_start(
                        out=slot.rearrange("p (a x) -> p a x", a=2), in_=wgu[fc].rearrange("p (a x) -> p a x", a=2)),
                        writes=[sbuf_], dsem=ssem)
                    for tg in range(2):
                        gi, ui = cnt % 2, 2 + cnt % 2
                        cnt += 1
                        G, U = psum[gi], psum[ui]
                        for kc in range(8):
                            P.op('pe', lambda e, G=G, sv=sv, kc=kc, tg=tg: e.matmul(
                                G[:, :], lhsT=sv[:, 0, kc, :], rhs=nT[:, kc, tokslice(tg)], start=(kc == 0), stop=(kc == 7)),
                                reads=[sbuf_, B_nT[kc][tg]], writes=[Bps[gi]])
                        for kc in range(8):
                            P.op('pe', lambda e, U=U, sv=sv, kc=kc, tg=tg: e.matmul(
                                U[:, :], lhsT=sv[:, 1, kc, :], rhs=nT[:, kc, tokslice(tg)], start=(kc == 0), stop=(kc == 7)),
                                reads=[sbuf_, B_nT[kc][tg]], writes=[Bps[ui]])
                        sg, bsg = ftmp[2 + gi], B_ft[2 + gi]
                        P.op('act', lambda e, sg=sg, G=G: e.activation(out=sg, in_=G[:, :], func=AF.Silu),
                             reads=[Bps[gi]], writes=[bsg])
                        P.op('dve', lambda e, sg=sg, U=U, fc=fc, tg=tg: e.tensor_tensor(
                            out=hid[:, fc, tokslice(tg)], in0=sg, in1=U[:, :], op=ALU.mult),
                            reads=[bsg, Bps[ui]], writes=[B_hid[fc][tg]])
                for dc in range(8):
                    slot, sbuf_, ssem = wdr.next()
                    sv = slot.rearrange("p (f d) -> p f d", f=NFC)
                    P.op('pool', lambda e, sv=sv, dc=dc: e.dma_start(out=sv, in_=wdd[dc].rearrange("p (f d) -> p f d", f=NFC)),
                         writes=[sbuf_], dsem=ssem)
                    for tg in range(2):
                        g = half * 2 + tg
                        yi = 4 + cnt % 2
                        cnt += 1
                        Y = psum[yi]
                        for fc in range(NFC):
                            P.op('pe', lambda e, Y=Y, sv=sv, fc=fc, tg=tg: e.matmul(
                                Y[:, :], lhsT=sv[:, fc, :], rhs=hid[:, fc, tokslice(tg)], start=(fc == 0), stop=(fc == NFC - 1)),
                                reads=[sbuf_, B_hid[fc][tg]], writes=[Bps[yi]])
                        P.op('dve', lambda e, Y=Y, dc=dc, g=g: e.scalar_tensor_tensor(
                            out=hT[:, dc, tokslice(g)], in0=Y[:, :], scalar=dcol(l, hgw, dc), in1=hT[:, dc, tokslice(g)],
                            op0=ALU.mult, op1=ALU.add), reads=[Bps[yi], B_h[dc][g], B_dsc], writes=[B_h[dc][g]])

        def phase1b(l):
            P.barrier()
            nT = BFA[:, 0:8192].rearrange("p (k t) -> p k t", k=8)
            o = 8192
            sq_aps = [BFA[:, o:o + 512], BFA[:, o + 512:o + 1024]]
            o += 1024
            stg = [BFA[:, o + k * 512:o + (k + 1) * 512] for k in range(4)]
            o += 2048
            vst = [BFA[:, o + k * VW:o + (k + 1) * VW] for k in range(2)]
            o += 2 * VW
            wv = BFA[:, o:o + 8 * VW].rearrange("p (k c) -> p k c", k=8)
            o += 8 * VW
            assert o <= WR_OFF
            B_nT = [[P.buf('nT') for _ in range(2)] for _ in range(8)]
            B_sq = [P.buf('sq0'), P.buf('sq1')]
            B_stg = [P.buf('stg%d' % k) for k in range(4)]
            B_vst = [P.buf('vst0'), P.buf('vst1')]
            B_wv = P.buf('wv')
            d_wv = P.dsem('wv')
            d_rope = P.dsem('rope')
            B_rope = P.buf('rope')
            ropeh = FPA[:, TAB_OFF:TAB_OFF + 2048].rearrange("p (a t) -> p a t", a=2)
            qa, qb, ktl, vl = SC['qa%d' % l], SC['qb%d' % l], SC['ktl%d' % l], SC['vl%d' % l]
            win = Wd['win%d' % l]
            qo = VOFF['qkg%d' % l]
            for kc in range(8):
                P.op('pool', lambda e, kc=kc: e.dma_start(out=wv[:, kc, :], in_=Wd['wv%d' % l][:, kc * VW:(kc + 1) * VW]),
                     writes=[B_wv], dsem=d_wv)
            cnt = 0
            scnt = 0
            for half in range(2):
                P.op('sp', lambda e, half=half: e.dma_start(out=ropeh, in_=rope_d[:, :, half * 1024:(half + 1) * 1024]),
                     writes=[B_rope], dsem=d_rope)
                norm_to_nT(l, 1, half, nT, B_nT, lambda kc: dcol(l, 1, kc), lambda kc: mcol(l, 3, kc), sq_aps, B_sq)
                for (ch, chp, isq, c) in [(CH_QA + c, CH_QAP + c, True, c) for c in range(4)] + [(CH_KA, CH_KAP, False, 0)]:
                    slot, sbuf_, ssem = wring.next()
                    sv = slot.rearrange("p (a k f) -> p a k f", a=2, k=8)
                    P.op('pool', lambda e, slot=slot, ch=ch: e.dma_start(out=slot[:, 0:1024], in_=win[ch]), writes=[sbuf_], dsem=ssem)
                    P.op('pool', lambda e, slot=slot, chp=chp: e.dma_start(out=slot[:, 1024:2048], in_=win[chp]), writes=[sbuf_], dsem=ssem)
                    for tg in range(2):
                        g = half * 2 + tg
                        qi, pi = cnt % 2, 2 + cnt % 2
                        cnt += 1
                        Q, QP = psum[qi], psum[pi]
                        for a, PS, bi in ((0, Q, qi), (1, QP, pi)):
                            for kc in range(8):
                                P.op('pe', lambda e, PS=PS, sv=sv, a=a, kc=kc, tg=tg: e.matmul(
                                    PS[:, :], lhsT=sv[:, a, kc, :], rhs=nT[:, kc, tokslice(tg)], start=(kc == 0), stop=(kc == 7)),
                                    reads=[sbuf_, B_nT[kc][tg]], writes=[Bps[bi]])
                        sq, bsq = sq_aps[0], B_sq[0]
                        P.op('act', lambda e, sq=sq, Q=Q: e.activation(out=sq, in_=Q[:, :], func=AF.Square), reads=[Bps[qi]], writes=[bsq])
                        SS = psum[4]
                        P.op('pe', lambda e, sq=sq, SS=SS: e.matmul(SS[:, :], lhsT=bd_bf[:, :], rhs=sq, start=True, stop=True),
                             reads=[bsq, B_const], writes=[Bps[4]])
                        P.op('act', lambda e, SS=SS: e.activation(out=ftmp[4], in_=SS[:, :], func=AF.Sqrt, scale=1.0 / HD, bias=EPS),
                             reads=[Bps[4]], writes=[B_ft[4]])
                        P.op('dve', lambda e: e.reciprocal(out=ftmp[5], in_=ftmp[4]), reads=[B_ft[4]], writes=[B_ft[5]])
                        gcol = qo + (0 if isq else 2)
                        P.op('dve', lambda e, Q=Q, gcol=gcol, tg=tg: e.scalar_tensor_tensor(
                            out=ftmp[6], in0=Q[:, :], scalar=vecs[:, gcol:gcol + 1], in1=ropeh[:, 0, tokslice(tg)],
                            op0=ALU.mult, op1=ALU.mult), reads=[Bps[qi], B_vecs, B_rope], writes=[B_ft[6]])
                        P.op('dve', lambda e, QP=QP, gcol=gcol, tg=tg: e.scalar_tensor_tensor(
                            out=ftmp[7], in0=QP[:, :], scalar=vecs[:, gcol + 1:gcol + 2], in1=ropeh[:, 1, tokslice(tg)],
                            op0=ALU.mult, op1=ALU.mult), reads=[Bps[pi], B_vecs, B_rope], writes=[B_ft[7]])
                        P.op('dve', lambda e: e.tensor_tensor(out=ftmp[6], in0=ftmp[6], in1=ftmp[7], op=ALU.add),
                             reads=[B_ft[6], B_ft[7]], writes=[B_ft[6]])
                        so, bso = stg[scnt % 4], B_stg[scnt % 4]
                        scnt += 1
                        P.op('dve', lambda e, so=so: e.tensor_tensor(out=so, in0=ftmp[6], in1=ftmp[5], op=ALU.mult),
                             reads=[B_ft[6], B_ft[5]], writes=[bso])
                        if isq:
                            for hh in range(2):
                                hd = 2 * c + hh
                                kv, gi = hd // 4, hd % 4
                                P.op('sp', lambda e, so=so, hh=hh, kv=kv, gi=gi, g=g: e.dma_start(
                                    out=qa[kv, :, g * 4:(g + 1) * 4, gi, :],
                                    in_=so[hh * 64:(hh + 1) * 64, :].rearrange("p (b q) -> p b q", b=4)),
                                    reads=[bso], writes=[B_scr['qa%d' % l]], dsem=d_sc)
                        else:
                            P.op('sp', lambda e, so=so, g=g: e.dma_start(out=ktl[0:128, tokslice(g)], in_=so),
                                 reads=[bso], writes=[B_scr['ktl%d' % l]], dsem=d_sc)
                for pair in range(4):
                    slot, sbuf_, ssem = wring.next()
                    sv = slot.rearrange("p (a k f) -> p a k f", a=2, k=8)
                    chs = [(CH_QD + 2 * pair, 'q', 2 * pair), (CH_QD + 2 * pair + 1, 'q', 2 * pair + 1)] if pair < 2 else \
                          [(CH_KD + 2 * (pair - 2), 'k', 2 * (pair - 2)), (CH_KD + 2 * (pair - 2) + 1, 'k', 2 * (pair - 2) + 1)]
                    for a, (ch, kind, c) in enumerate(chs):
                        P.op('pool', lambda e, slot=slot, ch=ch, a=a: e.dma_start(out=slot[:, a * 1024:(a + 1) * 1024], in_=win[ch]),
                             writes=[sbuf_], dsem=ssem)
                    for a, (ch, kind, c) in enumerate(chs):
                        for tg in range(2):
                            g = half * 2 + tg
                            qi = cnt % 4
                            cnt += 1
                            Q = psum[qi]
                            for kc in range(8):
                                P.op('pe', lambda e, Q=Q, sv=sv, a=a, kc=kc, tg=tg: e.matmul(
                                    Q[:, :], lhsT=sv[:, a, kc, :], rhs=nT[:, kc, tokslice(tg)], start=(kc == 0), stop=(kc == 7)),
                                    reads=[sbuf_, B_nT[kc][tg]], writes=[Bps[qi]])
                            so, bso = stg[scnt % 4], B_stg[scnt % 4]
                            scnt += 1
                            if scnt % 2:
                                P.op('act', lambda e, so=so, Q=Q: e.activation(out=so, in_=Q[:, :], func=AF.Copy), reads=[Bps[qi]], writes=[bso])
                            else:
                                P.op('dve', lambda e, so=so, Q=Q: e.tensor_copy(out=so, in_=Q[:, :]), reads=[Bps[qi]], writes=[bso])
                            if kind == 'q':
                                P.op('sp', lambda e, so=so, c=c, g=g: e.dma_start(out=qb[c * 128:(c + 1) * 128, tokslice(g)], in_=so),
                                     reads=[bso], writes=[B_scr['qb%d' % l]], dsem=d_sc)
                            else:
                                P.op('sp', lambda e, so=so, c=c, g=g: e.dma_start(out=ktl[128 + c * 128:128 + (c + 1) * 128, tokslice(g)], in_=so),
                                     reads=[bso], writes=[B_scr['ktl%d' % l]], dsem=d_sc)
                for tt in range(8):
                    tok0 = half * 1024 + tt * 128
                    VD, VA = psum[5], psum[7]
                    for kc in range(8):
                        P.op('pe', lambda e, kc=kc, tt=tt: e.matmul(VD[:, :], lhsT=nT[:, kc, tt * 128:(tt + 1) * 128], rhs=wv[:, kc, 128:640],
                                                                    start=(kc == 0), stop=(kc == 7)),
                             reads=[B_wv, B_nT[kc][tt // 4]], writes=[Bps[5]])
                    for kc in range(8):
                        P.op('pe', lambda e, kc=kc, tt=tt: e.matmul(VA[:, 0:128], lhsT=nT[:, kc, tt * 128:(tt + 1) * 128], rhs=wv[:, kc, 0:128],
                                                                    start=(kc == 0), stop=(kc == 7)),
                             reads=[B_wv, B_nT[kc][tt // 4]], writes=[Bps[7]])
                    vs, bvs = vst[tt % 2], B_vst[tt % 2]
                    P.op('act', lambda e, vs=vs: e.activation(out=vs[:, 128:640], in_=VD[:, :], func=AF.Copy), reads=[Bps[5]], writes=[bvs])
                    P.op('dve', lambda e, vs=vs: e.tensor_copy(out=vs[:, 0:128], in_=VA[:, 0:128]), reads=[Bps[7]], writes=[bvs])
                    P.op('sp', lambda e, vs=vs, tok0=tok0: e.dma_start(out=vl[tok0:tok0 + 128, :], in_=vs),
                         reads=[bvs], writes=[B_scr['vl%d' % l]], dsem=d_sc)

        def gather(l):
            kta, va = SC['kta%d' % l], SC['va%d' % l]
            d_cc = P.dsem('cc%d' % l)
            grp = [list(range(NCORES))]
            P.op('pool', lambda e: e.collective_compute("AllGather", ALU.bypass, replica_groups=grp,
                                                        ins=[SC['ktl%d' % l]], outs=[kta]),
                 reads=[B_scr['ktl%d' % l]], writes=[B_scr['kta%d' % l]], dsem=d_cc)
            P.op('pool', lambda e: e.collective_compute("AllGather", ALU.bypass, replica_groups=grp,
                                                        ins=[SC['vl%d' % l]], outs=[va]),
                 reads=[B_scr['vl%d' % l]], writes=[B_scr['va%d' % l]], dsem=d_cc)

        OA_OFF, OB_OFF = 0, 16384
        OA = BFA[0:64, OA_OFF:OA_OFF + 16384].rearrange("p (h t) -> p h t", h=8)
        OB = BFA[:, OB_OFF:OB_OFF + 8192].rearrange("p (h t) -> p h t", h=4)
        B_OA = [[P.buf('OA') for _ in range(16)] for _ in range(2)]
        B_OB = [[P.buf('OB') for _ in range(4)] for _ in range(4)]

        def phase2(l):
            P.barrier()
            qa, qb, ktl, vl = SC['qa%d' % l], SC['qb%d' % l], SC['ktl%d' % l], SC['vl%d' % l]
            kta, va = SC['kta%d' % l], SC['va%d' % l]
            Bqa, Bqb, Bktl, Bvl = B_scr['qa%d' % l], B_scr['qb%d' % l], B_scr['ktl%d' % l], B_scr['vl%d' % l]
            Bkta, Bva = B_scr['kta%d' % l], B_scr['va%d' % l]
            o = 24576
            qring = Ring(P, 'q', [BFA[:, o + k * 512:o + (k + 1) * 512] for k in range(4)])
            o += 2048
            kring = Ring(P, 'k', [BFA[:, o + k * 512:o + (k + 1) * 512] for k in range(8)])
            o += 4096
            vring = Ring(P, 'v', [BFA[:, o + k * 512:o + (k + 1) * 512].rearrange("p (j c) -> p j c", j=4) for k in range(4)])
            o += 2048
            NPT = 6
            pts = [BFA[:, o + k * 512:o + (k + 1) * 512] for k in range(NPT)]
            o += NPT * 512
            sqb = BFA[:, o:o + 512]
            o += 512
            rhl = BFA[:, o:o + 1024]
            o += 1024
            assert o <= WR_OFF
            B_pt = [P.buf('pt%d' % k) for k in range(NPT)]
            B_sqb, B_rhl = P.buf('sqb'), P.buf('rhl')
            atab = FPA[:, TAB_OFF:TAB_OFF + 896]
            B_atab = P.buf('atab')
            P.op('sp', lambda e: e.dma_start(out=atab, in_=atab_d), writes=[B_atab], dsem=d_misc)
            mb = l * 8
            cs = [8.0 * 2.0 ** (-2.0 * (h + 1)) for h in range(4)]
            state = {'s': 0, 'p': 0}
            DSK = 2

            def emit_qk(t):
                si = state['s'] % 3
                state['s'] += 1
                Sp = psum[si]
                t['Sp'], t['si'] = Sp, si
                ks, qs, j, nr = t['ks'], t['qs'], t['j'], t['nrows']
                P.op('pe', lambda e: e.matmul(Sp[:, :], lhsT=ks[0:nr, j * 128:(j + 1) * 128], rhs=qs[0:nr, :], start=True, stop=True),
                     reads=[t['bk'], t['bq']], writes=[Bps[si]])

            def emit_rest(t):
                Sp, si = t['Sp'], t['si']
                pi = state['p'] % NPT
                state['p'] += 1
                pt, bpt = pts[pi], B_pt[pi]
                if t['diag'] is None:
                    P.op('act', lambda e: e.activation(out=pt, in_=Sp[:, :], func=AF.Exp, scale=0.125), reads=[Bps[si]], writes=[bpt])
                else:
                    tb, btb = ftmp[2 + pi % 2], B_ft[2 + pi % 2]
                    a0 = 384 - 128 * t['diag']
                    ch = cs[t['h']]
                    P.op('dve', lambda e: e.scalar_tensor_tensor(out=tb, in0=atab[:, a0:a0 + 512], scalar=-ch, in1=Sp[:, :],
                                                                 op0=ALU.mult, op1=ALU.add), reads=[B_atab, Bps[si]], writes=[btb])
                    P.op('act', lambda e: e.activation(out=pt, in_=tb, func=AF.Exp, scale=0.125), reads=[btb], writes=[bpt])
                O, oi, vs, j, st_, sp_ = t['O'], t['oi'], t['vs'], t['j'], t['start'], t['stop']
                P.op('pe', lambda e: e.matmul(O[:, :], lhsT=vs[:, j, :], rhs=pt, start=st_, stop=sp_), reads=[t['bv'], bpt], writes=[Bps[oi]])
                if t.get('Z') is not None:
                    Z, zi = t['Z'], t['zi']
                    P.op('pe', lambda e: e.matmul(Z[:, :], lhsT=ones_bf[:, :], rhs=pt, start=st_, stop=sp_), reads=[bpt, B_const], writes=[Bps[zi]])

            pending = []

            def push(t):
                emit_qk(t)
                pending.append(t)
                if len(pending) > DSK:
                    emit_rest(pending.pop(0))

            def flush():
                while pending:
                    emit_rest(pending.pop(0))

            for k in range(4):
                P.op('dve', lambda e, k=k: e.memset(vring.aps[k][:, :, 64:128], 1.0), writes=[vring.bufs[k]])
            apass = 0
            for kv in range(2):
                for qblk in range(16):
                    if dbg is not None and (kv, qblk) not in dbg['A']:
                        continue
                    qs, bq, sq_ = qring.next()
                    P.op('sp', lambda e, qs=qs, kv=kv, qblk=qblk: e.dma_start(
                        out=qs[0:64, :].rearrange("p (g q) -> p g q", g=4), in_=qa[kv, :, qblk, :, :]),
                        reads=[Bqa], writes=[bq], dsem=sq_)
                    oi = 3 + apass % 2
                    apass += 1
                    O = psum[oi]
                    for ch in range(32):
                        r, cc = ch // 4, ch % 4
                        ks, bk, sk = kring.next()
                        P.op('sp', lambda e, ks=ks, r=r, cc=cc, kv=kv: e.dma_start(
                            out=ks[0:64, :], in_=kta[r * KROWS + kv * 64: r * KROWS + kv * 64 + 64, cc * 512:(cc + 1) * 512]),
                            reads=[Bkta], writes=[bk], dsem=sk)
                        vs, bv, sv_ = vring.next()
                        P.op('sp', lambda e, vs=vs, ch=ch, kv=kv: e.dma_start(
                            out=vs[:, :, 0:64], in_=va[ch * 512:(ch + 1) * 512, kv * 64:(kv + 1) * 64].rearrange("(j p) c -> p j c", p=128)),
                            reads=[Bva], writes=[bv], dsem=sv_)
                        for j in range(4):
                            push(dict(ks=ks, bk=bk, qs=qs, bq=bq, j=j, nrows=64, diag=None, O=O, oi=oi, vs=vs, bv=bv,
                                      start=(ch == 0 and j == 0), stop=(ch == 31 and j == 3)))
                    flush()
                    rec = ftmp[0]
                    P.op('dve', lambda e, O=O, rec=rec: e.reciprocal(out=rec[64:65, :], in_=O[64:65, :]), reads=[Bps[oi]], writes=[B_ft[0]])
                    P.op('dve', lambda e, rec=rec: e.tensor_copy(out=rhl[64:65, 0:512], in_=rec[64:65, :]), reads=[B_ft[0]], writes=[B_rhl])
                    P.op('dve', lambda e, rec=rec: e.tensor_tensor(out=rhl[64:65, 512:1024], in0=rec[64:65, :], in1=rhl[64:65, 0:512],
                                                                   op=ALU.subtract), reads=[B_ft[0], B_rhl], writes=[B_rhl])
                    BC = psum[5]
                    P.op('pe', lambda e, BC=BC: e.matmul(BC[0:64, :], lhsT=ones_bf[64:65, 0:64], rhs=rhl[64:65, 0:512], start=True, stop=False),
                         reads=[B_rhl, B_const], writes=[Bps[5]])
                    P.op('pe', lambda e, BC=BC: e.matmul(BC[0:64, :], lhsT=ones_bf[64:65, 0:64], rhs=rhl[64:65, 512:1024], start=False, stop=True),
                         reads=[B_rhl, B_const], writes=[Bps[5]])
                    P.op('dve', lambda e, O=O: e.tensor_copy(out=ftmp[1][0:64, :], in_=O[0:64, :]), reads=[Bps[oi]], writes=[B_ft[1]])
                    P.op('dve', lambda e, BC=BC, kv=kv, qblk=qblk: e.tensor_tensor(
                        out=OA[:, kv * 4:(kv + 1) * 4, qblk * 128:(qblk + 1) * 128],
                        in0=ftmp[1][0:64, :].rearrange("p (g q) -> p g q", g=4),
                        in1=BC[0:64, :].rearrange("p (g q) -> p g q", g=4), op=ALU.mult),
                        reads=[B_ft[1], Bps[5]], writes=[B_OA[kv][qblk]])
            for h in range(4):
                for qg in range(4):
                    if dbg is not None and (h, qg) not in dbg['B']:
                        continue
                    qsl = []
                    for m in range(2):
                        u = m * 4 + h
                        qs, bq, sq_ = qring.next()
                        P.op('sp', lambda e, qs=qs, u=u, qg=qg: e.dma_start(out=qs[0:64, :], in_=qb[u * 64:(u + 1) * 64, tokslice(qg)]),
                             reads=[Bqb], writes=[bq], dsem=sq_)
                        P.op('sp', lambda e, qs=qs, h=h, qg=qg: e.dma_start(out=qs[64:68, :], in_=qaug_d[h, :, tokslice(qg)]),
                             writes=[bq], dsem=sq_)
                        qsl.append((qs, bq))
                    OZ = [(psum[3], psum[4], 3, 4), (psum[5], psum[6], 5, 6)]
                    first_t = [True, True]
                    for ch in range(33):
                        isd = (ch == 32)
                        r, cc = ch // 4, ch % 4
                        kss = []
                        for m in range(2):
                            u = m * 4 + h
                            ks, bk, sk = kring.next()
                            if not isd:
                                r0 = r * KROWS + 128 + u * 64
                                P.op('sp', lambda e, ks=ks, r0=r0, cc=cc: e.dma_start(out=ks[0:64, :], in_=kta[r0:r0 + 64, cc * 512:(cc + 1) * 512]),
                                     reads=[Bkta], writes=[bk], dsem=sk)
                                P.op('sp', lambda e, ks=ks, h=h, qg=qg, ch=ch: e.dma_start(out=ks[64:68, :], in_=kaug_d[h, qg, :, ch * 512:(ch + 1) * 512]),
                                     writes=[bk], dsem=sk)
                            else:
                                P.op('sp', lambda e, ks=ks, u=u, qg=qg: e.dma_start(out=ks[0:64, :], in_=ktl[128 + u * 64:128 + (u + 1) * 64, tokslice(qg)]),
                                     reads=[Bktl], writes=[bk], dsem=sk)
                            kss.append((ks, bk))
                        vs, bv, sv_ = vring.next()
                        if not isd:
                            P.op('sp', lambda e, vs=vs, ch=ch, h=h: e.dma_start(
                                out=vs, in_=va[ch * 512:(ch + 1) * 512, 128 + h * 128:128 + (h + 1) * 128].rearrange("(j p) c -> p j c", p=128)),
                                reads=[Bva], writes=[bv], dsem=sv_)
                        else:
                            P.op('sp', lambda e, vs=vs, qg=qg, h=h: e.dma_start(
                                out=vs, in_=vl[qg * 512:(qg + 1) * 512, 128 + h * 128:128 + (h + 1) * 128].rearrange("(j p) c -> p j c", p=128)),
                                reads=[Bvl], writes=[bv], dsem=sv_)
                        for j in range(4):
                            for m in range(2):
                                Oq, Zq, oi, zi = OZ[m]
                                st_ = first_t[m]
                                first_t[m] = False
                                push(dict(ks=kss[m][0], bk=kss[m][1], qs=qsl[m][0], bq=qsl[m][1], j=j, nrows=(64 if isd else 68),
                                          diag=(j if isd else None), h=h, O=Oq, oi=oi, Z=Zq, zi=zi, vs=vs, bv=bv,
                                          start=st_, stop=(isd and j == 3)))
                    flush()
                    (O0, Z0, o0, z0), (O1, Z1, o1, z1) = OZ
                    P.op('dve', lambda e, Z0=Z0: e.reciprocal(out=ftmp[4], in_=Z0[:, :]), reads=[Bps[z0]], writes=[B_ft[4]])
                    P.op('dve', lambda e, O0=O0: e.tensor_tensor(out=ftmp[5], in0=O0[:, :], in1=ftmp[4], op=ALU.mult),
                         reads=[Bps[o0], B_ft[4]], writes=[B_ft[5]])
                    P.op('dve', lambda e, Z1=Z1: e.reciprocal(out=ftmp[6], in_=Z1[:, :]), reads=[Bps[z1]], writes=[B_ft[6]])
                    P.op('dve', lambda e, O1=O1: e.tensor_tensor(out=ftmp[7], in0=O1[:, :], in1=ftmp[6], op=ALU.mult),
                         reads=[Bps[o1], B_ft[6]], writes=[B_ft[7]])
                    P.op('dve', lambda e: e.scalar_tensor_tensor(out=ftmp[5], in0=ftmp[7], scalar=misc[:, mb + 1:mb + 2], in1=ftmp[5],
                                                                 op0=ALU.mult, op1=ALU.add),
                         reads=[B_ft[7], B_ft[5], B_misc], writes=[B_ft[5]])
                    P.op('dve', lambda e: e.tensor_tensor(out=sqb, in0=ftmp[5], in1=ftmp[5], op=ALU.mult), reads=[B_ft[5]], writes=[B_sqb])
                    X = psum[7]
                    P.op('pe', lambda e, X=X: e.matmul(X[:, :], lhsT=ones_bf[:, :], rhs=sqb, start=True, stop=True),
                         reads=[B_sqb, B_const], writes=[Bps[7]])
                    P.op('act', lambda e, X=X: e.activation(out=ftmp[8], in_=X[:, :], func=AF.Sqrt, scale=1.0 / 128, bias=EPS),
                         reads=[Bps[7]], writes=[B_ft[8]])
                    P.op('dve', lambda e: e.reciprocal(out=ftmp[9], in_=ftmp[8]), reads=[B_ft[8]], writes=[B_ft[9]])
                    P.op('dve', lambda e, h=h, qg=qg: e.scalar_tensor_tensor(
                        out=OB[:, h, tokslice(qg)], in0=ftmp[5], scalar=misc[:, mb + 2:mb + 3], in1=ftmp[9], op0=ALU.mult, op1=ALU.mult),
                        reads=[B_ft[5], B_ft[9], B_misc], writes=[B_OB[h][qg]])

        def phase3(l):
            P.barrier()
            o = 24576
            nT = BFA[:, o:o + 8192].rearrange("p (k t) -> p k t", k=8)
            o += 8192
            mT = BFA[:, o:o + 8192].rearrange("p (k t) -> p k t", k=8)
            o += 8192
            sq_aps = [BFA[:, o:o + 512], BFA[:, o + 512:o + 1024]]
            o += 1024
            assert o <= WR_OFF
            B_nT = [[P.buf('nT') for _ in range(2)] for _ in range(8)]
            B_mT = [[P.buf('mT') for _ in range(2)] for _ in range(8)]
            B_sq = [P.buf('sq0'), P.buf('sq1')]
            bgo = VOFF['bg%d' % l]
            cnt = 0
            for half in range(2):
                norm_to_nT(l, 1, half, nT, B_nT, lambda kc: dcol(l, 1, kc), lambda kc: mcol(l, 3, kc), sq_aps, B_sq)
                for dc in range(8):
                    s1, b1, e1 = wring.next()
                    P.op('pool', lambda e, s1=s1, dc=dc: e.dma_start(out=s1.rearrange("p (a x) -> p a x", a=2),
                                                                     in_=Wd['wgate%d' % l][dc].rearrange("p (a x) -> p a x", a=2)),
                         writes=[b1], dsem=e1)
                    g1v = s1.rearrange("p (a k f) -> p a k f", a=2, k=8)
                    s2, b2, e2 = wring.next()
                    P.op('pool', lambda e, s2=s2, dc=dc: e.dma_start(out=s2[0:64, 0:1024], in_=Wd['wba%d' % l][dc]), writes=[b2], dsem=e2)
                    P.op('pool', lambda e, s2=s2, dc=dc: e.dma_start(out=s2[:, 1024:1536], in_=Wd['wbb%d' % l][dc]), writes=[b2], dsem=e2)
                    for tg in range(2):
                        g = half * 2 + tg
                        GA, GB, YA, YB = psum[0 + 4 * (cnt % 2)], psum[1 + 4 * (cnt % 2)], psum[2 + 4 * (cnt % 2)], psum[3 + 4 * (cnt % 2)]
                        ia = [0 + 4 * (cnt % 2), 1 + 4 * (cnt % 2), 2 + 4 * (cnt % 2), 3 + 4 * (cnt % 2)]
                        fo = 4 * (cnt % 2)
                        cnt += 1
                        for a, PS, bi in ((0, GA, ia[0]), (1, GB, ia[1])):
                            for kc in range(8):
                                P.op('pe', lambda e, PS=PS, g1v=g1v, a=a, kc=kc, tg=tg: e.matmul(
                                    PS[:, :], lhsT=g1v[:, a, kc, :], rhs=nT[:, kc, tokslice(tg)], start=(kc == 0), stop=(kc == 7)),
                                    reads=[b1, B_nT[kc][tg]], writes=[Bps[bi]])
                        for hd in range(8):
                            P.op('pe', lambda e, YA=YA, s2=s2, hd=hd, g=g: e.matmul(
                                YA[:, :], lhsT=s2[0:64, hd * 128:(hd + 1) * 128], rhs=OA[:, hd, tokslice(g)], start=(hd == 0), stop=(hd == 7)),
                                reads=[b2] + [B_OA[hd // 4][g * 4 + q] for q in range(4)], writes=[Bps[ia[2]]])
                        for hh in range(4):
                            P.op('pe', lambda e, YB=YB, s2=s2, hh=hh, g=g: e.matmul(
                                YB[:, :], lhsT=s2[:, 1024 + hh * 128:1024 + (hh + 1) * 128], rhs=OB[:, hh, tokslice(g)], start=(hh == 0), stop=(hh == 3)),
                                reads=[b2, B_OB[hh][g]], writes=[Bps[ia[3]]])
                        P.op('act', lambda e, GA=GA, dc=dc, fo=fo: e.activation(out=ftmp[fo], in_=GA[:, :], func=AF.Sigmoid,
                                                                               bias=vecs[:, bgo + dc:bgo + dc + 1], scale=1.0),
                             reads=[Bps[ia[0]], B_vecs], writes=[B_ft[fo]])
                        P.op('act', lambda e, GB=GB, dc=dc, fo=fo: e.activation(out=ftmp[fo + 1], in_=GB[:, :], func=AF.Sigmoid,
                                                                               bias=vecs[:, bgo + 8 + dc:bgo + 8 + dc + 1], scale=1.0),
                             reads=[Bps[ia[1]], B_vecs], writes=[B_ft[fo + 1]])
                        P.op('dve', lambda e, YA=YA, fo=fo: e.tensor_tensor(out=ftmp[fo + 2], in0=ftmp[fo], in1=YA[:, :], op=ALU.mult),
                             reads=[B_ft[fo], Bps[ia[2]]], writes=[B_ft[fo + 2]])
                        P.op('dve', lambda e, YB=YB, fo=fo: e.tensor_tensor(out=ftmp[fo + 3], in0=ftmp[fo + 1], in1=YB[:, :], op=ALU.mult),
                             reads=[B_ft[fo + 1], Bps[ia[3]]], writes=[B_ft[fo + 3]])
                        P.op('dve', lambda e, dc=dc, tg=tg, fo=fo: e.tensor_tensor(out=mT[:, dc, tokslice(tg)], in0=ftmp[fo + 2], in1=ftmp[fo + 3], op=ALU.add),
                             reads=[B_ft[fo + 2], B_ft[fo + 3]], writes=[B_mT[dc][tg]])
                for dco in range(8):
                    s1, b1, e1 = wring.next()
                    P.op('pool', lambda e, s1=s1, dco=dco: e.dma_start(out=s1[:, 0:1024], in_=Wd['wo%d' % l][dco]), writes=[b1], dsem=e1)
                    for tg in range(2):
                        g = half * 2 + tg
                        yi = 4 * (cnt % 2)
                        cnt += 1
                        Y = psum[yi]
                        for kc in range(8):
                            P.op('pe', lambda e, Y=Y, s1=s1, kc=kc, tg=tg: e.matmul(
                                Y[:, :], lhsT=s1[:, kc * 128:(kc + 1) * 128], rhs=mT[:, kc, tokslice(tg)], start=(kc == 0), stop=(kc == 7)),
                                reads=[b1, B_mT[kc][tg]], writes=[Bps[yi]])
                        P.op('dve', lambda e, Y=Y, dco=dco, g=g: e.scalar_tensor_tensor(
                            out=hT[:, dco, tokslice(g)], in0=Y[:, :], scalar=mcol(l, 5, dco), in1=hT[:, dco, tokslice(g)],
                            op0=ALU.mult, op1=ALU.add), reads=[Bps[yi], B_h[dco][g], B_mod], writes=[B_h[dco][g]])

        def final_out():
            P.barrier()
            B_o = P.buf('outd')
            if not last:
                for c in range(8):
                    for g in range(4):
                        P.op('sp', lambda e, c=c, g=g: e.dma_start(out=h_out[c * 128:(c + 1) * 128, tokslice(g)], in_=hT[:, c, tokslice(g)]),
                             reads=[B_h[c][g]], writes=[B_o], dsem=d_out)
            else:
                sq_aps = [BFA[:, 0:512], BFA[:, 512:1024]]
                B_sq = [P.buf('sq0'), P.buf('sq1')]
                ST = psum[6]
                fgo = VOFF['fg']
                for g in range(4):
                    for kc in range(8):
                        sq, bsq = sq_aps[kc % 2], B_sq[kc % 2]
                        P.op('dve', lambda e, sq=sq, kc=kc, g=g: e.tensor_tensor(out=sq, in0=hT[:, kc, tokslice(g)], in1=hT[:, kc, tokslice(g)], op=ALU.mult),
                             reads=[B_h[kc][g]], writes=[bsq])
                        P.op('pe', lambda e, sq=sq, kc=kc: e.matmul(ST[:, :], lhsT=ones_bf[:, :], rhs=sq, start=(kc == 0), stop=(kc == 7)),
                             reads=[bsq, B_const], writes=[Bps[6]])
                    P.op('act', lambda e: e.activation(out=ftmp[8], in_=ST[:, :], func=AF.Sqrt, scale=1.0 / D, bias=EPS), reads=[Bps[6]], writes=[B_ft[8]])
                    P.op('dve', lambda e: e.reciprocal(out=ftmp[9], in_=ftmp[8]), reads=[B_ft[8]], writes=[B_ft[9]])
                    for kc in range(8):
                        tmp, btmp = ftmp[kc % 4], B_ft[kc % 4]
                        P.op('dve', lambda e, tmp=tmp, kc=kc, g=g: e.scalar_tensor_tensor(
                            out=tmp, in0=hT[:, kc, tokslice(g)], scalar=vecs[:, fgo + kc:fgo + kc + 1], in1=ftmp[9], op0=ALU.mult, op1=ALU.mult),
                            reads=[B_h[kc][g], B_ft[9], B_vecs], writes=[btmp])
                        P.op('sp', lambda e, tmp=tmp, kc=kc, g=g: e.dma_start(out=h_out[kc * 128:(kc + 1) * 128, tokslice(g)], in_=tmp),
                             reads=[btmp], writes=[B_o], dsem=d_out)
            P.op('sp', None, reads=[B_o])

        for l in range(L):
            if 'p1_%d' % l in segs:
                ffn(l, 0)
                phase1b(l)
            if fused[l]:
                gather(l)
            if 'p23_%d' % l in segs:
                phase2(l)
                if dbg is not None:
                    P.barrier()
                    d_oa = dram('dbg_oa', [64, 16384], BF16, 'ExternalOutput')
                    d_ob = dram('dbg_ob', [128, 8192], BF16, 'ExternalOutput')
                    B_dbg = P.buf('dbgo')
                    P.op('sp', lambda e: e.dma_start(out=d_oa, in_=BFA[0:64, 0:16384]), writes=[B_dbg], dsem=d_out)
                    P.op('sp', lambda e: e.dma_start(out=d_ob, in_=BFA[:, 16384:24576]), writes=[B_dbg], dsem=d_out)
                    P.op('sp', None, reads=[B_dbg])
                    continue
                phase3(l)
                ffn(l, 1)
        final_out()
        P.op('sp', None, reads=[B_scr[k] for k in B_scr])
        P.emit(st)
    nc._w_names = list(Wd.keys())
    return nc


_CACHE = {}


def _get_nc(segs):
    key = tuple(segs)
    if key not in _CACHE:
        _CACHE[key] = build(segs)
    return _CACHE[key]


FUSED = False


def kernel(x, c, ada_w, ada_b, norm_g, ffn_wg, ffn_wu, ffn_wd, w_in, qk_g, lam_p, subln_g, w_ba, w_bb,
           w_gate, b_gate, w_o, final_g):
    f = lambda a: np.asarray(a, dtype=np.float32)
    x, c, ada_w, ada_b, norm_g = f(x), f(c), f(ada_w), f(ada_b), f(norm_g)
    ffn_wg, ffn_wu, ffn_wd, w_in = f(ffn_wg), f(ffn_wu), f(ffn_wd), f(w_in)
    qk_g, lam_p, subln_g, w_ba, w_bb = f(qk_g), f(lam_p), f(subln_g), f(w_ba), f(w_bb)
    w_gate, b_gate, w_o, final_g = f(w_gate), f(b_gate), f(w_o), f(final_g)
    vecs = _host_vecs(c, ada_b, norm_g, b_gate, qk_g, subln_g, lam_p, final_g)
    W = _host_weights(ffn_wg, ffn_wu, ffn_wd, w_in, w_ba, w_bb, w_gate, w_o, ada_w)
    atab = _host_atab()
    common = []
    for core in range(NCORES):
        qaug, kaug = _host_alibi(core)
        common.append({'vecs': vecs, 'rope': _host_rope(core), 'qaug': qaug, 'kaug': kaug, 'atab': atab})
    xT = [np.ascontiguousarray(x[0, core * T:(core + 1) * T, :].T) for core in range(NCORES)]
    cores = list(range(NCORES))

    def wsel(nc_keys):
        return {k: W[k] for k in nc_keys}

    def run(segs, extra):
        nc = _get_nc(segs)
        wk = nc._w_names
        in_maps = []
        for core in cores:
            m = dict(common[core])
            m.update(wsel(wk))
            m.update(extra[core])
            in_maps.append(m)
        return run_bass_kernel_spmd(nc, in_maps, core_ids=cores).results

    if FUSED:
        res = run(['p1_0', 'p23_0', 'p1_1', 'p23_1', 'fin'], [{'xT': xT[i]} for i in cores])
    else:
        r0 = run(['p1_0'], [{'xT': xT[i]} for i in cores])

        def gathered(r, l):
            kta = np.concatenate([r[i]['ktl%d' % l] for i in cores], axis=0)
            va = np.concatenate([r[i]['vl%d' % l] for i in cores], axis=0)
            return kta, va
        kta, va = gathered(r0, 0)
        r1 = run(['p23_0', 'p1_1'], [{'h_in': r0[i]['h_out'], 'qa0': r0[i]['qa0'], 'qb0': r0[i]['qb0'], 'ktl0': r0[i]['ktl0'],
                                      'vl0': r0[i]['vl0'], 'kta0': kta, 'va0': va} for i in cores])
        kta, va = gathered(r1, 1)
        res = run(['p23_1', 'fin'], [{'h_in': r1[i]['h_out'], 'qa1': r1[i]['qa1'], 'qb1': r1[i]['qb1'], 'ktl1': r1[i]['ktl1'],
                                      'vl1': r1[i]['vl1'], 'kta1': kta, 'va1': va} for i in cores])
    out = np.concatenate([res[i]['outT'].T for i in cores], axis=0)[None]
    return np.ascontiguousarray(out.astype(np.float32))
```

```python
import contextlib
import numpy as np
import ml_dtypes
import concourse.bass as bass
import concourse.mybir as mybir
from concourse.bass_utils import run_bass_kernel_spmd

F32 = mybir.dt.float32
BF16 = mybir.dt.bfloat16
AF = mybir.ActivationFunctionType
ALU = mybir.AluOpType

NCORES = 8
D = 1024
S = 16384
T = S // NCORES
L = 2
FF = 2816
NFC = FF // 128
HD = 64
EPS = 1e-6
KROWS = 640
VW = 640
BIGNEG = -16384.0

ENGS = ('pe', 'act', 'dve', 'pool', 'sp')
SEM_LIMIT = 20000


class DmaSem:
    def __init__(self, name):
        self.name = name
        self.count = 0
        self.handle = None


class Buf:
    __slots__ = ('name', 'last_w', 'rd_eng', 'rd_dma')

    def __init__(self, name):
        self.name = name
        self.last_w = None
        self.rd_eng = {}
        self.rd_dma = {}


class Op:
    __slots__ = ('eng', 'fn', 'deps', 'idx', 'needed', 'dsem', 'semval', 'semidx', 'inc')

    def __init__(self, eng, fn, idx):
        self.eng = eng
        self.fn = fn
        self.idx = idx
        self.deps = []
        self.needed = False
        self.dsem = None
        self.semval = None
        self.semidx = None
        self.inc = 16


class Prog:
    def __init__(self, nc):
        self.nc = nc
        self.ops = {e: [] for e in ENGS}
        self.seen = {e: {} for e in ENGS}
        self.dsems = []

    def dsem(self, name):
        s = DmaSem('%s_%d' % (name, len(self.dsems)))
        self.dsems.append(s)
        return s

    def buf(self, name):
        return Buf(name)

    def _add_dep(self, op, tok):
        if tok is None:
            return
        kind, key, val = tok
        if kind == 'eng' and key == op.eng and key == 'pe':
            return
        seen = self.seen[op.eng]
        if seen.get(key, -1) >= val:
            return
        seen[key] = val
        op.deps.append(tok)
        if kind == 'eng':
            self.ops[key][val].needed = True

    def op(self, eng, fn, reads=(), writes=(), dsem=None, inc=16):
        lst = self.ops[eng]
        o = Op(eng, fn, len(lst))
        o.inc = inc
        if dsem is not None:
            dsem.count += inc
            o.dsem = dsem
            tok = ('dma', dsem, dsem.count)
        else:
            tok = ('eng', eng, o.idx)
        lst.append(o)
        for r in reads:
            self._add_dep(o, r.last_w)
        for w in writes:
            self._add_dep(o, w.last_w)
            for e, i in w.rd_eng.items():
                self._add_dep(o, ('eng', e, i))
            for s, c in w.rd_dma.items():
                self._add_dep(o, ('dma', s, c))
        for r in reads:
            if dsem is not None:
                r.rd_dma[dsem] = dsem.count
            else:
                r.rd_eng[eng] = o.idx
        for w in writes:
            w.last_w = tok
            w.rd_eng = {}
            w.rd_dma = {}
        return o

    def barrier(self):
        toks = []
        for e in ENGS:
            if e == 'sp':
                continue
            if self.ops[e]:
                for o in reversed(self.ops[e]):
                    if o.fn is not None and o.dsem is None:
                        toks.append(('eng', e, o.idx))
                        break
        for s in self.dsems:
            if s.count > 0:
                toks.append(('dma', s, s.count))
        for e in ENGS:
            o = Op(e, None, len(self.ops[e]))
            self.ops[e].append(o)
            for t in toks:
                if t[0] == 'eng' and t[1] == e:
                    continue
                self._add_dep(o, t)

    def emit(self, stack):
        nc = self.nc
        for s in self.dsems:
            if s.count > 0:
                s.handle = stack.enter_context(nc.semaphore('d_' + s.name))
        esems = {}
        for e in ENGS:
            cnt = 0
            si = 0
            anyn = False
            for o in self.ops[e]:
                if o.needed and o.dsem is None:
                    anyn = True
                    if cnt >= SEM_LIMIT:
                        si += 1
                        cnt = 0
                    cnt += 1
                    o.semidx = si
                    o.semval = cnt
            esems[e] = [stack.enter_context(nc.semaphore('e_%s_%d' % (e, k)))
                        for k in range(si + 1 if anyn else 0)]
        block = stack.enter_context(nc.Block())
        starters = {'pe': block.tensor, 'act': block.scalar, 'dve': block.vector,
                    'pool': block.gpsimd, 'sp': block.sync}
        for e in ENGS:
            if not self.ops[e]:
                continue

            def body(h, e=e):
                for o in self.ops[e]:
                    for kind, key, val in o.deps:
                        if kind == 'eng':
                            po = self.ops[key][val]
                            h.wait_ge(esems[key][po.semidx], po.semval)
                        else:
                            h.wait_ge(key.handle, val)
                    if o.fn is None:
                        continue
                    ins = o.fn(h)
                    if o.dsem is not None:
                        if o.inc == 16:
                            ins.then_inc(o.dsem.handle, 16)
                        else:
                            ins.then_inc(o.dsem.handle)
                    elif o.needed:
                        ins.then_inc(esems[e][o.semidx], 1)
            starters[e](body)


class Ring:
    def __init__(self, P, name, aps):
        self.aps = aps
        self.bufs = [P.buf('%s%d' % (name, i)) for i in range(len(aps))]
        self.sems = [P.dsem('%s%d' % (name, i)) for i in range(len(aps))]
        self.i = 0

    def next(self):
        k = self.i % len(self.aps)
        self.i += 1
        return self.aps[k], self.bufs[k], self.sems[k]


def _vec_offsets():
    off = {}
    n = 0

    def add(k, w):
        nonlocal n
        off[k] = n
        n += w
    add('c', 8)
    for l in range(L):
        add('adab%d' % l, 72)
        add('ng%d' % l, 24)
        add('bg%d' % l, 16)
        add('qkg%d' % l, 4)
        add('sub%d' % l, 1)
        add('lamp%d' % l, 4)
    add('fg', 8)
    return off, n


VOFF, NV = _vec_offsets()


def _rope_src():
    d = np.arange(64)
    dd = d % 32
    return np.where(dd < 16, d + 16, d - 16)


def _fm(v):
    return np.ascontiguousarray(v.reshape(-1, 128).T)


def _host_vecs(c, ada_b, norm_g, b_gate, qk_g, subln_g, lam_p, final_g):
    v = np.zeros((128, NV), np.float32)
    v[:, VOFF['c']:VOFF['c'] + 8] = _fm(c[0])
    src = _rope_src()
    for l in range(L):
        v[:, VOFF['adab%d' % l]:VOFF['adab%d' % l] + 72] = _fm(ada_b[l])
        for k in range(3):
            o = VOFF['ng%d' % l] + 8 * k
            v[:, o:o + 8] = _fm(norm_g[l, k])
        v[:, VOFF['bg%d' % l]:VOFF['bg%d' % l] + 16] = _fm(b_gate[l])
        o = VOFF['qkg%d' % l]
        p = np.arange(128) % 64
        v[:, o + 0] = qk_g[l, 0][p]
        v[:, o + 1] = qk_g[l, 0][src[p]]
        v[:, o + 2] = qk_g[l, 1][p]
        v[:, o + 3] = qk_g[l, 1][src[p]]
        v[:, VOFF['sub%d' % l]] = subln_g[l]
        v[0:64, VOFF['lamp%d' % l]:VOFF['lamp%d' % l] + 4] = lam_p[l].T
    v[:, VOFF['fg']:VOFF['fg'] + 8] = _fm(final_g)
    return v


def _host_rope(core):
    t = core * T + np.arange(T)
    row = (t // 64 - (S // 64) // 2).astype(np.float32)
    col = (t % 64 - 32).astype(np.float32)
    inv = (1.0 / (np.float32(10000.0) ** (np.arange(0, 32, 2, dtype=np.float32) / np.float32(32)))).astype(np.float32)
    out = np.zeros((128, 2, T), np.float32)
    for p in range(128):
        d = p % 64
        sec = d // 32
        dd = d % 32
        j = dd % 16
        pos = row if sec == 0 else col
        ang = (pos * inv[j]).astype(np.float32)
        out[p, 0] = np.cos(ang)
        out[p, 1] = -np.sin(ang) if dd < 16 else np.sin(ang)
    return out


def _host_alibi(core):
    bf = ml_dtypes.bfloat16
    cs = [8.0 * 2.0 ** (-2.0 * (h + 1)) for h in range(4)]
    q = core * T + np.arange(T)
    qaug = np.zeros((4, 4, T), np.float32)
    k = np.arange(S)
    kaug = np.zeros((4, 4, 4, S), np.float32)
    for h in range(4):
        c = cs[h]
        qaug[h, 0] = -c * 128.0 * (q // 128)
        qaug[h, 1] = -c * (q % 128)
        qaug[h, 2] = 1.0
        qaug[h, 3] = 1.0
        for g in range(4):
            q0 = core * T + g * 512
            sg = np.where(k < q0, 1.0, -1.0)
            diag = (k >= q0) & (k < q0 + 512)
            r0 = sg.copy()
            r1 = sg.copy()
            r2 = sg * c * 128.0 * (k // 128)
            r3 = sg * c * (k % 128)
            r0[diag] = 0.0
            r1[diag] = 0.0
            r2[diag] = BIGNEG
            r3[diag] = 0.0
            kaug[h, g, 0], kaug[h, g, 1], kaug[h, g, 2], kaug[h, g, 3] = r0, r1, r2, r3
    assert np.array_equal(qaug.astype(bf).astype(np.float32), qaug)
    assert np.array_equal(kaug.astype(bf).astype(np.float32), kaug)
    return qaug.astype(bf), kaug.astype(bf)


def _host_atab():
    kk = np.arange(128)[:, None]
    y = np.arange(896)[None, :]
    return np.abs(y - 384 - kk).astype(np.float32)


def _host_weights(ffn_wg, ffn_wu, ffn_wd, w_in, w_ba, w_bb, w_gate, w_o, ada_w):
    W = {}
    src = _rope_src()
    for l in range(L):
        for i in range(2):
            g = ffn_wg[l, i].reshape(8, 128, NFC, 128)
            u = ffn_wu[l, i].reshape(8, 128, NFC, 128)
            gu = np.stack([g, u], 0)
            W['wgu%d%d' % (l, i)] = np.ascontiguousarray(gu.transpose(3, 2, 0, 1, 4)).reshape(NFC, 128, 2048)
            d = ffn_wd[l, i].reshape(NFC, 128, 8, 128)
            W['wd%d%d' % (l, i)] = np.ascontiguousarray(d.transpose(2, 1, 0, 3)).reshape(8, 128, FF)
        wi = w_in[l]
        qa, ka, va, qd, kd, vd = np.split(wi, [512, 640, 768, 1280, 1792], axis=1)
        permq = (np.arange(512) // 64) * 64 + src[np.arange(512) % 64]
        permk = (np.arange(128) // 64) * 64 + src[np.arange(128) % 64]
        fmc = np.concatenate([qa, qa[:, permq], ka, ka[:, permk], qd, kd], axis=1)
        nch = fmc.shape[1] // 128
        x = fmc.reshape(8, 128, nch, 128)
        W['win%d' % l] = np.ascontiguousarray(x.transpose(2, 1, 0, 3)).reshape(nch, 128, 1024)
        wv = np.concatenate([va, vd], axis=1).reshape(8, 128, VW)
        W['wv%d' % l] = np.ascontiguousarray(wv.transpose(1, 0, 2)).reshape(128, 8 * VW)
        wg = w_gate[l].reshape(8, 128, 2, 8, 128)
        W['wgate%d' % l] = np.ascontiguousarray(wg.transpose(3, 1, 2, 0, 4)).reshape(8, 128, 2048)
        wa = w_ba[l].reshape(8, 64, 8, 128)
        W['wba%d' % l] = np.ascontiguousarray(wa.transpose(2, 1, 0, 3)).reshape(8, 64, 1024)
        wb = w_bb[l].reshape(4, 128, 8, 128)
        W['wbb%d' % l] = np.ascontiguousarray(wb.transpose(2, 1, 0, 3)).reshape(8, 128, 512)
        wo = w_o[l].reshape(8, 128, 8, 128)
        W['wo%d' % l] = np.ascontiguousarray(wo.transpose(2, 1, 0, 3)).reshape(8, 128, 1024)
        aw = ada_w[l].reshape(8, 128, 9 * D)
        W['ada%d' % l] = np.ascontiguousarray(aw.transpose(1, 0, 2))
    return W


WSHAPES = {}
for _l in range(L):
    for _i in range(2):
        WSHAPES['wgu%d%d' % (_l, _i)] = [NFC, 128, 2048]
        WSHAPES['wd%d%d' % (_l, _i)] = [8, 128, FF]
    WSHAPES['win%d' % _l] = [18, 128, 1024]
    WSHAPES['wv%d' % _l] = [128, 8 * VW]
    WSHAPES['wgate%d' % _l] = [8, 128, 2048]
    WSHAPES['wba%d' % _l] = [8, 64, 1024]
    WSHAPES['wbb%d' % _l] = [8, 128, 512]
    WSHAPES['wo%d' % _l] = [8, 128, 1024]
    WSHAPES['ada%d' % _l] = [128, 8, 9 * D]

CH_QA, CH_QAP, CH_KA, CH_KAP, CH_QD, CH_KD = 0, 4, 8, 9, 10, 14


def build(segs, dbg=None):
    segs = set(segs)
    nc = bass.Bass("TRN2", target_bir_lowering=False)
    fused = {l: ('p1_%d' % l in segs and 'p23_%d' % l in segs) for l in range(L)}
    layers_used = [l for l in range(L) if ('p1_%d' % l in segs or 'p23_%d' % l in segs)]

    def dram(name, shape, dt, kind):
        return nc.dram_tensor(name, shape, dt, kind=kind).ap()

    first = 'p1_0' in segs
    last = 'fin' in segs
    h_in = dram('xT' if first else 'h_in', [D, T], F32, 'ExternalInput')
    h_out = dram('outT' if last else 'h_out', [D, T], F32, 'ExternalOutput')
    vecs_d = dram('vecs', [128, NV], F32, 'ExternalInput')
    rope_d = dram('rope', [128, 2, T], F32, 'ExternalInput')
    qaug_d = dram('qaug', [4, 4, T], BF16, 'ExternalInput')
    kaug_d = dram('kaug', [4, 4, 4, S], BF16, 'ExternalInput')
    atab_d = dram('atab', [128, 896], F32, 'ExternalInput')
    Wd = {}
    for k, shp in WSHAPES.items():
        l = int(k[-2]) if k.startswith('wgu') or (k.startswith('wd') and len(k) == 4) else int(k[-1])
        if l in layers_used:
            Wd[k] = dram(k, shp, F32, 'ExternalInput')
    SC = {}
    for l in layers_used:
        p1 = 'p1_%d' % l in segs
        p23 = 'p23_%d' % l in segs
        if fused[l]:
            kl = kg = 'Internal'
        elif p1:
            kl, kg = 'ExternalOutput', None
        else:
            kl, kg = 'ExternalInput', 'ExternalInput'
        SC['qa%d' % l] = dram('qa%d' % l, [2, 64, 16, 4, 128], BF16, kl)
        SC['qb%d' % l] = dram('qb%d' % l, [512, T], BF16, kl)
        SC['ktl%d' % l] = dram('ktl%d' % l, [KROWS, T], BF16, kl)
        SC['vl%d' % l] = dram('vl%d' % l, [T, VW], BF16, kl)
        if kg is not None:
            SC['kta%d' % l] = dram('kta%d' % l, [NCORES * KROWS, T], BF16, kg)
            SC['va%d' % l] = dram('va%d' % l, [S, VW], BF16, kg)

    st = contextlib.ExitStack()
    with st:
        def sb(name, shape, dt):
            return st.enter_context(nc.sbuf_tensor(name, shape, dt))
        hT = sb('hT', [128, 8, T], F32)
        NBF = 48 * 1024
        NFP = 9 * 1024 + 512
        BFA = sb('bfa', [128, NBF], BF16)
        FPA = sb('fpa', [128, NFP], F32)
        vecs = sb('vecs_sb', [128, NV], F32)
        modT = sb('modT', [128, L * 72], F32)
        dsc = sb('dsc', [128, L * 64], F32)
        misc = sb('misc', [128, 32], F32)
        cact = sb('cact', [128, 8], BF16)
        ones_bf = sb('ones_bf', [128, 128], BF16)
        bd_bf = sb('bd_bf', [128, 128], BF16)
        ones_f = sb('ones_f', [64, 128], F32)
        psum = [st.enter_context(nc.psum_tensor('ps%d' % i, [128, 512], F32)) for i in range(8)]

        P = Prog(nc)
        Bps = [P.buf('ps%d' % i) for i in range(8)]
        B_h = [[P.buf('h%d_%d' % (c, g)) for g in range(4)] for c in range(8)]
        B_vecs, B_mod, B_dsc, B_misc, B_cact = P.buf('vecs'), P.buf('mod'), P.buf('dsc'), P.buf('misc'), P.buf('cact')
        B_const = P.buf('const')
        d_misc = P.dsem('misc')
        d_h = P.dsem('hload')
        d_out = P.dsem('out')
        d_sc = P.dsem('scratch')
        B_scr = {k: P.buf(k) for k in SC}

        def bf_view(off, n, **kw):
            return BFA[:, off:off + n]

        def tokslice(g):
            return slice(g * 512, (g + 1) * 512)

        P.op('sp', lambda e: e.dma_start(out=vecs[:], in_=vecs_d), writes=[B_vecs], dsem=d_misc)
        for c in range(8):
            for g in range(4):
                P.op('sp', lambda e, c=c, g=g: e.dma_start(out=hT[:, c, tokslice(g)], in_=h_in[c * 128:(c + 1) * 128, tokslice(g)]),
                     writes=[B_h[c][g]], dsem=d_h)
        P.op('dve', lambda e: e.memset(ones_bf[:], 1.0), writes=[B_const])
        P.op('dve', lambda e: e.memset(bd_bf[:], 0.0), writes=[B_const])
        P.op('dve', lambda e: e.memset(bd_bf[0:64, 0:64], 1.0), writes=[B_const])
        P.op('dve', lambda e: e.memset(bd_bf[64:128, 64:128], 1.0), writes=[B_const])
        P.op('dve', lambda e: e.memset(ones_f[:], 1.0), writes=[B_const])

        WR_OFF = NBF - 3 * 2048
        wring = Ring(P, 'wr', [BFA[:, WR_OFF + i * 2048: WR_OFF + (i + 1) * 2048] for i in range(3)])

        def vcol(key, j=0, n=1):
            o = VOFF[key] + j
            return vecs[:, o:o + n]

        P.op('act', lambda e: e.activation(out=cact[:], in_=vcol('c', 0, 8), func=AF.Silu), reads=[B_vecs], writes=[B_cact])
        MOD = psum[7]
        for l in range(L):
            if l not in layers_used:
                continue
            for sp_i in range(36):
                slot, sbuf_, ssem = wring.next()
                sv = slot.rearrange("p (k f) -> p k f", k=8)
                P.op('pool', lambda e, sv=sv, l=l, sp_i=sp_i: e.dma_start(out=sv, in_=Wd['ada%d' % l][:, :, sp_i * 256:(sp_i + 1) * 256]),
                     writes=[sbuf_], dsem=ssem)
                for jj in range(2):
                    j = sp_i * 2 + jj
                    for kc in range(8):
                        P.op('pe', lambda e, sv=sv, jj=jj, kc=kc, col=l * 72 + j: e.matmul(
                            MOD[:, col:col + 1], lhsT=sv[:, kc, jj * 128:(jj + 1) * 128], rhs=cact[:, kc:kc + 1],
                            start=(kc == 0), stop=(kc == 7)), reads=[sbuf_, B_cact], writes=[Bps[7]])
            P.op('dve', lambda e, l=l: e.tensor_tensor(out=modT[:, l * 72:(l + 1) * 72], in0=MOD[:, l * 72:(l + 1) * 72],
                                                       in1=vcol('adab%d' % l, 0, 72), op=ALU.add),
                 reads=[Bps[7], B_vecs], writes=[B_mod])
            base = l * 64
            for k in range(3):
                P.op('dve', lambda e, l=l, k=k, base=base: e.tensor_scalar(
                    out=dsc[:, base + 8 * k: base + 8 * k + 8], in0=modT[:, l * 72 + 24 * k + 8: l * 72 + 24 * k + 16],
                    scalar1=1.0, scalar2=None, op0=ALU.add), reads=[B_mod], writes=[B_dsc])
                P.op('dve', lambda e, l=l, k=k, base=base: e.tensor_tensor(
                    out=dsc[:, base + 8 * k: base + 8 * k + 8], in0=dsc[:, base + 8 * k: base + 8 * k + 8],
                    in1=vcol('ng%d' % l, 8 * k, 8), op=ALU.mult), reads=[B_dsc, B_vecs], writes=[B_dsc])
            for k, mo in ((0, 16), (1, 64)):
                P.op('dve', lambda e, l=l, k=k, mo=mo, base=base: e.tensor_scalar(
                    out=dsc[:, base + 24 + 8 * k: base + 32 + 8 * k], in0=modT[:, l * 72 + mo: l * 72 + mo + 8],
                    scalar1=0.5, scalar2=None, op0=ALU.mult), reads=[B_mod], writes=[B_dsc])
            mb = l * 8
            lam_init = 0.8 - 0.6 * float(np.exp(-0.3 * l))
            lo = VOFF['lamp%d' % l]
            P.op('dve', lambda e, mb=mb, lo=lo: e.tensor_tensor(out=misc[0:64, mb + 3:mb + 4], in0=vecs[0:64, lo:lo + 1],
                                                                 in1=vecs[0:64, lo + 1:lo + 2], op=ALU.mult),
                 reads=[B_vecs], writes=[B_misc])
            P.op('dve', lambda e, mb=mb, lo=lo: e.tensor_tensor(out=misc[0:64, mb + 4:mb + 5], in0=vecs[0:64, lo + 2:lo + 3],
                                                                 in1=vecs[0:64, lo + 3:lo + 4], op=ALU.mult),
                 reads=[B_vecs], writes=[B_misc])
            LP = psum[6]
            P.op('pe', lambda e, mb=mb: e.matmul(LP[:, 0:2], lhsT=ones_f[0:64, :], rhs=misc[0:64, mb + 3:mb + 5],
                                                 start=True, stop=True), reads=[B_misc, B_const], writes=[Bps[6]])
            P.op('act', lambda e, mb=mb: e.activation(out=misc[:, mb + 5:mb + 7], in_=LP[:, 0:2], func=AF.Exp),
                 reads=[Bps[6]], writes=[B_misc])
            P.op('dve', lambda e, mb=mb: e.tensor_tensor(out=misc[:, mb:mb + 1], in0=misc[:, mb + 5:mb + 6],
                                                         in1=misc[:, mb + 6:mb + 7], op=ALU.subtract),
                 reads=[B_misc], writes=[B_misc])
            P.op('dve', lambda e, mb=mb, li=lam_init: e.tensor_scalar(out=misc[:, mb:mb + 1], in0=misc[:, mb:mb + 1],
                                                                      scalar1=li, scalar2=None, op0=ALU.add),
                 reads=[B_misc], writes=[B_misc])
            P.op('dve', lambda e, mb=mb: e.tensor_scalar(out=misc[:, mb + 1:mb + 2], in0=misc[:, mb:mb + 1],
                                                         scalar1=-1.0, scalar2=None, op0=ALU.mult),
                 reads=[B_misc], writes=[B_misc])
            P.op('dve', lambda e, mb=mb, l=l, li=lam_init: e.tensor_scalar(
                out=misc[:, mb + 2:mb + 3], in0=vcol('sub%d' % l), scalar1=1.0 - li, scalar2=None, op0=ALU.mult),
                reads=[B_vecs], writes=[B_misc])

        def dcol(l, which, j):
            o = l * 64 + 8 * which + j
            return dsc[:, o:o + 1]

        def mcol(l, idx, j):
            o = l * 72 + idx * 8 + j
            return modT[:, o:o + 1]

        ftmp = [FPA[:, i * 512:(i + 1) * 512] for i in range(10)]
        B_ft = [P.buf('ft%d' % i) for i in range(10)]
        TAB_OFF = 10 * 512

        def norm_to_nT(l, which, half, nT, B_nT, gsc_fn, sh_fn, sq_aps, B_sq):
            ST = psum[6]
            for tg in range(2):
                g = half * 2 + tg
                for kc in range(8):
                    sq, bsq = sq_aps[kc % 2], B_sq[kc % 2]
                    P.op('dve', lambda e, sq=sq, kc=kc, g=g: e.tensor_tensor(out=sq, in0=hT[:, kc, tokslice(g)],
                                                                             in1=hT[:, kc, tokslice(g)], op=ALU.mult),
                         reads=[B_h[kc][g]], writes=[bsq])
                    P.op('pe', lambda e, sq=sq, kc=kc: e.matmul(ST[:, :], lhsT=ones_bf[:, :], rhs=sq, start=(kc == 0), stop=(kc == 7)),
                         reads=[bsq, B_const], writes=[Bps[6]])
                rs, brs = ftmp[8], B_ft[8]
                P.op('act', lambda e, rs=rs: e.activation(out=rs, in_=ST[:, :], func=AF.Sqrt, scale=1.0 / D, bias=EPS),
                     reads=[Bps[6]], writes=[brs])
                rstd, brstd = ftmp[9], B_ft[9]
                P.op('dve', lambda e, rs=rs, rstd=rstd: e.reciprocal(out=rstd, in_=rs), reads=[brs], writes=[brstd])
                for kc in range(8):
                    tmp, btmp = ftmp[kc % 2], B_ft[kc % 2]
                    P.op('dve', lambda e, tmp=tmp, kc=kc, g=g, rstd=rstd: e.scalar_tensor_tensor(
                        out=tmp, in0=hT[:, kc, tokslice(g)], scalar=gsc_fn(kc), in1=rstd, op0=ALU.mult, op1=ALU.mult),
                        reads=[B_h[kc][g], brstd, B_dsc, B_vecs], writes=[btmp])
                    if sh_fn is not None:
                        P.op('act', lambda e, tmp=tmp, kc=kc, tg=tg: e.activation(
                            out=nT[:, kc, tokslice(tg)], in_=tmp, func=AF.Identity, bias=sh_fn(kc), scale=1.0),
                            reads=[btmp, B_mod], writes=[B_nT[kc][tg]])

        def ffn(l, i):
            P.barrier()
            nT = BFA[:, 0:8192].rearrange("p (k t) -> p k t", k=8)
            hid = BFA[:, 8192:8192 + NFC * 1024].rearrange("p (f t) -> p f t", f=NFC)
            o = 8192 + NFC * 1024
            sq_aps = [BFA[:, o:o + 512], BFA[:, o + 512:o + 1024]]
            o += 1024
            wdr = Ring(P, 'wd', [BFA[:, o + k * FF:o + (k + 1) * FF] for k in range(2)])
            assert o + 2 * FF <= WR_OFF
            B_nT = [[P.buf('nT') for _ in range(2)] for _ in range(8)]
            B_hid = [[P.buf('hid') for _ in range(2)] for _ in range(NFC)]
            B_sq = [P.buf('sq0'), P.buf('sq1')]
            kn = 0 if i == 0 else 2
            hgw = 3 if i == 0 else 4
            wgu = Wd['wgu%d%d' % (l, i)]
            wdd = Wd['wd%d%d' % (l, i)]
            cnt = 0
            for half in range(2):
                norm_to_nT(l, kn, half, nT, B_nT, lambda kc: dcol(l, kn, kc), lambda kc: mcol(l, 3 * kn, kc), sq_aps, B_sq)
                for fc in range(NFC):
                    slot, sbuf_, ssem = wring.next()
                    sv = slot.rearrange("p (a k f) -> p a k f", a=2, k=8)
                    P.op('pool', lambda e, slot=slot, fc=fc: e.dma_start(
                        out=slot.rearrange("p (a x) -> p a x", a=2), in_=wgu[fc].rearrange("p (a x) -> p a x", a=2)),
                        writes=[sbuf_], dsem=ssem)
                    for tg in range(2):
                        gi, ui = cnt % 2, 2 + cnt % 2
                        cnt += 1
                        G, U = psum[gi], psum[ui]
                        for kc in range(8):
                            P.op('pe', lambda e, G=G, sv=sv, kc=kc, tg=tg: e.matmul(
                                G[:, :], lhsT=sv[:, 0, kc, :], rhs=nT[:, kc, tokslice(tg)], start=(kc == 0), stop=(kc == 7)),
                                reads=[sbuf_, B_nT[kc][tg]], writes=[Bps[gi]])
                        for kc in range(8):
                            P.op('pe', lambda e, U=U, sv=sv, kc=kc, tg=tg: e.matmul(
                                U[:, :], lhsT=sv[:, 1, kc, :], rhs=nT[:, kc, tokslice(tg)], start=(kc == 0), stop=(kc == 7)),
                                reads=[sbuf_, B_nT[kc][tg]], writes=[Bps[ui]])
                        sg, bsg = ftmp[2 + gi], B_ft[2 + gi]
                        P.op('act', lambda e, sg=sg, G=G: e.activation(out=sg, in_=G[:, :], func=AF.Silu),
                             reads=[Bps[gi]], writes=[bsg])
                        P.op('dve', lambda e, sg=sg, U=U, fc=fc, tg=tg: e.tensor_tensor(
                            out=hid[:, fc, tokslice(tg)], in0=sg, in1=U[:, :], op=ALU.mult),
                            reads=[bsg, Bps[ui]], writes=[B_hid[fc][tg]])
                for dc in range(8):
                    slot, sbuf_, ssem = wdr.next()
                    sv = slot.rearrange("p (f d) -> p f d", f=NFC)
                    P.op('pool', lambda e, sv=sv, dc=dc: e.dma_start(out=sv, in_=wdd[dc].rearrange("p (f d) -> p f d", f=NFC)),
                         writes=[sbuf_], dsem=ssem)
                    for tg in range(2):
                        g = half * 2 + tg
                        yi = 4 + cnt % 2
                        cnt += 1
                        Y = psum[yi]
                        for fc in range(NFC):
                            P.op('pe', lambda e, Y=Y, sv=sv, fc=fc, tg=tg: e.matmul(
                                Y[:, :], lhsT=sv[:, fc, :], rhs=hid[:, fc, tokslice(tg)], start=(fc == 0), stop=(fc == NFC - 1)),
                                reads=[sbuf_, B_hid[fc][tg]], writes=[Bps[yi]])
                        P.op('dve', lambda e, Y=Y, dc=dc, g=g: e.scalar_tensor_tensor(
                            out=hT[:, dc, tokslice(g)], in0=Y[:, :], scalar=dcol(l, hgw, dc), in1=hT[:, dc, tokslice(g)],
                            op0=ALU.mult, op1=ALU.add), reads=[Bps[yi], B_h[dc][g], B_dsc], writes=[B_h[dc][g]])

        def phase1b(l):
            P.barrier()
            nT = BFA[:, 0:8192].rearrange("p (k t) -> p k t", k=8)
            o = 8192
            sq_aps = [BFA[:, o:o + 512], BFA[:, o + 512:o + 1024]]
            o += 1024
            stg = [BFA[:, o + k * 512:o + (k + 1) * 512] for k in range(4)]
            o += 2048
            vst = [BFA[:, o + k * VW:o + (k + 1) * VW] for k in range(2)]
            o += 2 * VW
            wv = BFA[:, o:o + 8 * VW].rearrange("p (k c) -> p k c", k=8)
            o += 8 * VW
            assert o <= WR_OFF
            B_nT = [[P.buf('nT') for _ in range(2)] for _ in range(8)]
            B_sq = [P.buf('sq0'), P.buf('sq1')]
            B_stg = [P.buf('stg%d' % k) for k in range(4)]
            B_vst = [P.buf('vst0'), P.buf('vst1')]
            B_wv = P.buf('wv')
            d_wv = P.dsem('wv')
            d_rope = P.dsem('rope')
            B_rope = P.buf('rope')
            ropeh = FPA[:, TAB_OFF:TAB_OFF + 2048].rearrange("p (a t) -> p a t", a=2)
            qa, qb, ktl, vl = SC['qa%d' % l], SC['qb%d' % l], SC['ktl%d' % l], SC['vl%d' % l]
            win = Wd['win%d' % l]
            qo = VOFF['qkg%d' % l]
            for kc in range(8):
                P.op('pool', lambda e, kc=kc: e.dma_start(out=wv[:, kc, :], in_=Wd['wv%d' % l][:, kc * VW:(kc + 1) * VW]),
                     writes=[B_wv], dsem=d_wv)
            cnt = 0
            scnt = 0
            for half in range(2):
                P.op('sp', lambda e, half=half: e.dma_start(out=ropeh, in_=rope_d[:, :, half * 1024:(half + 1) * 1024]),
                     writes=[B_rope], dsem=d_rope)
                norm_to_nT(l, 1, half, nT, B_nT, lambda kc: dcol(l, 1, kc), lambda kc: mcol(l, 3, kc), sq_aps, B_sq)
                for (ch, chp, isq, c) in [(CH_QA + c, CH_QAP + c, True, c) for c in range(4)] + [(CH_KA, CH_KAP, False, 0)]:
                    slot, sbuf_, ssem = wring.next()
                    sv = slot.rearrange("p (a k f) -> p a k f", a=2, k=8)
                    P.op('pool', lambda e, slot=slot, ch=ch: e.dma_start(out=slot[:, 0:1024], in_=win[ch]), writes=[sbuf_], dsem=ssem)
                    P.op('pool', lambda e, slot=slot, chp=chp: e.dma_start(out=slot[:, 1024:2048], in_=win[chp]), writes=[sbuf_], dsem=ssem)
                    for tg in range(2):
                        g = half * 2 + tg
                        qi, pi = cnt % 2, 2 + cnt % 2
                        cnt += 1
                        Q, QP = psum[qi], psum[pi]
                        for a, PS, bi in ((0, Q, qi), (1, QP, pi)):
                            for kc in range(8):
                                P.op('pe', lambda e, PS=PS, sv=sv, a=a, kc=kc, tg=tg: e.matmul(
                                    PS[:, :], lhsT=sv[:, a, kc, :], rhs=nT[:, kc, tokslice(tg)], start=(kc == 0), stop=(kc == 7)),
                                    reads=[sbuf_, B_nT[kc][tg]], writes=[Bps[bi]])
                        sq, bsq = sq_aps[0], B_sq[0]
                        P.op('act', lambda e, sq=sq, Q=Q: e.activation(out=sq, in_=Q[:, :], func=AF.Square), reads=[Bps[qi]], writes=[bsq])
                        SS = psum[4]
                        P.op('pe', lambda e, sq=sq, SS=SS: e.matmul(SS[:, :], lhsT=bd_bf[:, :], rhs=sq, start=True, stop=True),
                             reads=[bsq, B_const], writes=[Bps[4]])
                        P.op('act', lambda e, SS=SS: e.activation(out=ftmp[4], in_=SS[:, :], func=AF.Sqrt, scale=1.0 / HD, bias=EPS),
                             reads=[Bps[4]], writes=[B_ft[4]])
                        P.op('dve', lambda e: e.reciprocal(out=ftmp[5], in_=ftmp[4]), reads=[B_ft[4]], writes=[B_ft[5]])
                        gcol = qo + (0 if isq else 2)
                        P.op('dve', lambda e, Q=Q, gcol=gcol, tg=tg: e.scalar_tensor_tensor(
                            out=ftmp[6], in0=Q[:, :], scalar=vecs[:, gcol:gcol + 1], in1=ropeh[:, 0, tokslice(tg)],
                            op0=ALU.mult, op1=ALU.mult), reads=[Bps[qi], B_vecs, B_rope], writes=[B_ft[6]])
                        P.op('dve', lambda e, QP=QP, gcol=gcol, tg=tg: e.scalar_tensor_tensor(
                            out=ftmp[7], in0=QP[:, :], scalar=vecs[:, gcol + 1:gcol + 2], in1=ropeh[:, 1, tokslice(tg)],
                            op0=ALU.mult, op1=ALU.mult), reads=[Bps[pi], B_vecs, B_rope], writes=[B_ft[7]])
                        P.op('dve', lambda e: e.tensor_tensor(out=ftmp[6], in0=ftmp[6], in1=ftmp[7], op=ALU.add),
                             reads=[B_ft[6], B_ft[7]], writes=[B_ft[6]])
                        so, bso = stg[scnt % 4], B_stg[scnt % 4]
                        scnt += 1
                        P.op('dve', lambda e, so=so: e.tensor_tensor(out=so, in0=ftmp[6], in1=ftmp[5], op=ALU.mult),
                             reads=[B_ft[6], B_ft[5]], writes=[bso])
                        if isq:
                            for hh in range(2):
                                hd = 2 * c + hh
                                kv, gi = hd // 4, hd % 4
                                P.op('sp', lambda e, so=so, hh=hh, kv=kv, gi=gi, g=g: e.dma_start(
                                    out=qa[kv, :, g * 4:(g + 1) * 4, gi, :],
                                    in_=so[hh * 64:(hh + 1) * 64, :].rearrange("p (b q) -> p b q", b=4)),
                                    reads=[bso], writes=[B_scr['qa%d' % l]], dsem=d_sc)
                        else:
                            P.op('sp', lambda e, so=so, g=g: e.dma_start(out=ktl[0:128, tokslice(g)], in_=so),
                                 reads=[bso], writes=[B_scr['ktl%d' % l]], dsem=d_sc)
                for pair in range(4):
                    slot, sbuf_, ssem = wring.next()
                    sv = slot.rearrange("p (a k f) -> p a k f", a=2, k=8)
                    chs = [(CH_QD + 2 * pair, 'q', 2 * pair), (CH_QD + 2 * pair + 1, 'q', 2 * pair + 1)] if pair < 2 else \
                          [(CH_KD + 2 * (pair - 2), 'k', 2 * (pair - 2)), (CH_KD + 2 * (pair - 2) + 1, 'k', 2 * (pair - 2) + 1)]
                    for a, (ch, kind, c) in enumerate(chs):
                        P.op('pool', lambda e, slot=slot, ch=ch, a=a: e.dma_start(out=slot[:, a * 1024:(a + 1) * 1024], in_=win[ch]),
                             writes=[sbuf_], dsem=ssem)
                    for a, (ch, kind, c) in enumerate(chs):
                        for tg in range(2):
                            g = half * 2 + tg
                            qi = cnt % 4
                            cnt += 1
                            Q = psum[qi]
                            for kc in range(8):
                                P.op('pe', lambda e, Q=Q, sv=sv, a=a, kc=kc, tg=tg: e.matmul(
                                    Q[:, :], lhsT=sv[:, a, kc, :], rhs=nT[:, kc, tokslice(tg)], start=(kc == 0), stop=(kc == 7)),
                                    reads=[sbuf_, B_nT[kc][tg]], writes=[Bps[qi]])
                            so, bso = stg[scnt % 4], B_stg[scnt % 4]
                            scnt += 1
                            if scnt % 2:
                                P.op('act', lambda e, so=so, Q=Q: e.activation(out=so, in_=Q[:, :], func=AF.Copy), reads=[Bps[qi]], writes=[bso])
                            else:
                                P.op('dve', lambda e, so=so, Q=Q: e.tensor_copy(out=so, in_=Q[:, :]), reads=[Bps[qi]], writes=[bso])
                            if kind == 'q':
                                P.op('sp', lambda e, so=so, c=c, g=g: e.dma_start(out=qb[c * 128:(c + 1) * 128, tokslice(g)], in_=so),
                                     reads=[bso], writes=[B_scr['qb%d' % l]], dsem=d_sc)
                            else:
                                P.op('sp', lambda e, so=so, c=c, g=g: e.dma_start(out=ktl[128 + c * 128:128 + (c + 1) * 128, tokslice(g)], in_=so),
                                     reads=[bso], writes=[B_scr['ktl%d' % l]], dsem=d_sc)
                for tt in range(8):
                    tok0 = half * 1024 + tt * 128
                    VD, VA = psum[5], psum[7]
                    for kc in range(8):
                        P.op('pe', lambda e, kc=kc, tt=tt: e.matmul(VD[:, :], lhsT=nT[:, kc, tt * 128:(tt + 1) * 128], rhs=wv[:, kc, 128:640],
                                                                    start=(kc == 0), stop=(kc == 7)),
                             reads=[B_wv, B_nT[kc][tt // 4]], writes=[Bps[5]])
                    for kc in range(8):
                        P.op('pe', lambda e, kc=kc, tt=tt: e.matmul(VA[:, 0:128], lhsT=nT[:, kc, tt * 128:(tt + 1) * 128], rhs=wv[:, kc, 0:128],
                                                                    start=(kc == 0), stop=(kc == 7)),
                             reads=[B_wv, B_nT[kc][tt // 4]], writes=[Bps[7]])
                    vs, bvs = vst[tt % 2], B_vst[tt % 2]
                    P.op('act', lambda e, vs=vs: e.activation(out=vs[:, 128:640], in_=VD[:, :], func=AF.Copy), reads=[Bps[5]], writes=[bvs])
                    P.op('dve', lambda e, vs=vs: e.tensor_copy(out=vs[:, 0:128], in_=VA[:, 0:128]), reads=[Bps[7]], writes=[bvs])
                    P.op('sp', lambda e, vs=vs, tok0=tok0: e.dma_start(out=vl[tok0:tok0 + 128, :], in_=vs),
                         reads=[bvs], writes=[B_scr['vl%d' % l]], dsem=d_sc)

        def gather(l):
            kta, va = SC['kta%d' % l], SC['va%d' % l]
            d_cc = P.dsem('cc%d' % l)
            grp = [list(range(NCORES))]
            P.op('pool', lambda e: e.collective_compute("AllGather", ALU.bypass, replica_groups=grp,
                                                        ins=[SC['ktl%d' % l].bitcast(F32).opt()], outs=[kta.bitcast(F32).opt()]),
                 reads=[B_scr['ktl%d' % l]], writes=[B_scr['kta%d' % l]], dsem=d_cc, inc=1)
            d_cc2 = P.dsem('ccv%d' % l)
            P.op('pool', lambda e: e.collective_compute("AllGather", ALU.bypass, replica_groups=grp,
                                                        ins=[SC['vl%d' % l].bitcast(F32).opt()], outs=[va.bitcast(F32).opt()]),
                 reads=[B_scr['vl%d' % l]], writes=[B_scr['va%d' % l]], dsem=d_cc2, inc=1)

        OA_OFF, OB_OFF = 0, 16384
        OA = BFA[0:64, OA_OFF:OA_OFF + 16384].rearrange("p (h t) -> p h t", h=8)
        OB = BFA[:, OB_OFF:OB_OFF + 8192].rearrange("p (h t) -> p h t", h=4)
        B_OA = [[P.buf('OA') for _ in range(16)] for _ in range(2)]
        B_OB = [[P.buf('OB') for _ in range(4)] for _ in range(4)]

        def phase2(l):
            P.barrier()
            qa, qb, ktl, vl = SC['qa%d' % l], SC['qb%d' % l], SC['ktl%d' % l], SC['vl%d' % l]
            kta, va = SC['kta%d' % l], SC['va%d' % l]
            Bqa, Bqb, Bktl, Bvl = B_scr['qa%d' % l], B_scr['qb%d' % l], B_scr['ktl%d' % l], B_scr['vl%d' % l]
            Bkta, Bva = B_scr['kta%d' % l], B_scr['va%d' % l]
            o = 24576
            qring = Ring(P, 'q', [BFA[:, o + k * 512:o + (k + 1) * 512] for k in range(4)])
            o += 2048
            kring = Ring(P, 'k', [BFA[:, o + k * 512:o + (k + 1) * 512] for k in range(8)])
            o += 4096
            vring = Ring(P, 'v', [BFA[:, o + k * 512:o + (k + 1) * 512].rearrange("p (j c) -> p j c", j=4) for k in range(4)])
            o += 2048
            NPT = 6
            pts = [BFA[:, o + k * 512:o + (k + 1) * 512] for k in range(NPT)]
            o += NPT * 512
            sqb = BFA[:, o:o + 512]
            o += 512
            rhl = BFA[:, o:o + 1024]
            o += 1024
            assert o <= WR_OFF
            B_pt = [P.buf('pt%d' % k) for k in range(NPT)]
            B_sqb, B_rhl = P.buf('sqb'), P.buf('rhl')
            atab = FPA[:, TAB_OFF:TAB_OFF + 896]
            B_atab = P.buf('atab')
            P.op('sp', lambda e: e.dma_start(out=atab, in_=atab_d), writes=[B_atab], dsem=d_misc)
            mb = l * 8
            cs = [8.0 * 2.0 ** (-2.0 * (h + 1)) for h in range(4)]
            state = {'s': 0, 'p': 0}
            DSK = 2

            def emit_qk(t):
                si = state['s'] % 3
                state['s'] += 1
                Sp = psum[si]
                t['Sp'], t['si'] = Sp, si
                ks, qs, j, nr = t['ks'], t['qs'], t['j'], t['nrows']
                P.op('pe', lambda e: e.matmul(Sp[:, :], lhsT=ks[0:nr, j * 128:(j + 1) * 128], rhs=qs[0:nr, :], start=True, stop=True),
                     reads=[t['bk'], t['bq']], writes=[Bps[si]])

            def emit_rest(t):
                Sp, si = t['Sp'], t['si']
                pi = state['p'] % NPT
                state['p'] += 1
                pt, bpt = pts[pi], B_pt[pi]
                if t['diag'] is None:
                    P.op('act', lambda e: e.activation(out=pt, in_=Sp[:, :], func=AF.Exp, scale=0.125), reads=[Bps[si]], writes=[bpt])
                else:
                    tb, btb = ftmp[2 + pi % 2], B_ft[2 + pi % 2]
                    a0 = 384 - 128 * t['diag']
                    ch = cs[t['h']]
                    P.op('dve', lambda e: e.scalar_tensor_tensor(out=tb, in0=atab[:, a0:a0 + 512], scalar=-ch, in1=Sp[:, :],
                                                                 op0=ALU.mult, op1=ALU.add), reads=[B_atab, Bps[si]], writes=[btb])
                    P.op('act', lambda e: e.activation(out=pt, in_=tb, func=AF.Exp, scale=0.125), reads=[btb], writes=[bpt])
                O, oi, vs, j, st_, sp_ = t['O'], t['oi'], t['vs'], t['j'], t['start'], t['stop']
                P.op('pe', lambda e: e.matmul(O[:, :], lhsT=vs[:, j, :], rhs=pt, start=st_, stop=sp_), reads=[t['bv'], bpt], writes=[Bps[oi]])
                if t.get('Z') is not None:
                    Z, zi = t['Z'], t['zi']
                    P.op('pe', lambda e: e.matmul(Z[:, :], lhsT=ones_bf[:, :], rhs=pt, start=st_, stop=sp_), reads=[bpt, B_const], writes=[Bps[zi]])

            pending = []

            def push(t):
                emit_qk(t)
                pending.append(t)
                if len(pending) > DSK:
                    emit_rest(pending.pop(0))

            def flush():
                while pending:
                    emit_rest(pending.pop(0))

            for k in range(4):
                P.op('dve', lambda e, k=k: e.memset(vring.aps[k][:, :, 64:128], 1.0), writes=[vring.bufs[k]])
                P.op('dve', lambda e, k=k: e.memset(qring.aps[k][64:128, :], 0.0), writes=[qring.bufs[k]])
            for k in range(8):
                P.op('dve', lambda e, k=k: e.memset(kring.aps[k][64:128, :], 0.0), writes=[kring.bufs[k]])
            apass = 0
            for kv in range(2):
                for qblk in range(16):
                    if dbg is not None and (kv, qblk) not in dbg['A']:
                        continue
                    qs, bq, sq_ = qring.next()
                    P.op('sp', lambda e, qs=qs, kv=kv, qblk=qblk: e.dma_start(
                        out=qs[0:64, :].rearrange("p (g q) -> p g q", g=4), in_=qa[kv, :, qblk, :, :]),
                        reads=[Bqa], writes=[bq], dsem=sq_)
                    oi = 3 + apass % 2
                    apass += 1
                    O = psum[oi]
                    for ch in range(32):
                        r, cc = ch // 4, ch % 4
                        ks, bk, sk = kring.next()
                        P.op('sp', lambda e, ks=ks, r=r, cc=cc, kv=kv: e.dma_start(
                            out=ks[0:64, :], in_=kta[r * KROWS + kv * 64: r * KROWS + kv * 64 + 64, cc * 512:(cc + 1) * 512]),
                            reads=[Bkta], writes=[bk], dsem=sk)
                        vs, bv, sv_ = vring.next()
                        P.op('sp', lambda e, vs=vs, ch=ch, kv=kv: e.dma_start(
                            out=vs[:, :, 0:64], in_=va[ch * 512:(ch + 1) * 512, kv * 64:(kv + 1) * 64].rearrange("(j p) c -> p j c", p=128)),
                            reads=[Bva], writes=[bv], dsem=sv_)
                        for j in range(4):
                            push(dict(ks=ks, bk=bk, qs=qs, bq=bq, j=j, nrows=128, diag=None, O=O, oi=oi, vs=vs, bv=bv,
                                      start=(ch == 0 and j == 0), stop=(ch == 31 and j == 3)))
                    flush()
                    rec = ftmp[0]
                    P.op('dve', lambda e, O=O, rec=rec: e.reciprocal(out=rec[64:65, :], in_=O[64:65, :]), reads=[Bps[oi]], writes=[B_ft[0]])
                    P.op('dve', lambda e, rec=rec: e.tensor_copy(out=rhl[64:65, 0:512], in_=rec[64:65, :]), reads=[B_ft[0]], writes=[B_rhl])
                    P.op('dve', lambda e, rec=rec: e.tensor_tensor(out=rhl[64:65, 512:1024], in0=rec[64:65, :], in1=rhl[64:65, 0:512],
                                                                   op=ALU.subtract), reads=[B_ft[0], B_rhl], writes=[B_rhl])
                    BC = psum[5]
                    P.op('pe', lambda e, BC=BC: e.matmul(BC[0:64, :], lhsT=ones_bf[64:65, 0:64], rhs=rhl[64:65, 0:512], start=True, stop=False),
                         reads=[B_rhl, B_const], writes=[Bps[5]])
                    P.op('pe', lambda e, BC=BC: e.matmul(BC[0:64, :], lhsT=ones_bf[64:65, 0:64], rhs=rhl[64:65, 512:1024], start=False, stop=True),
                         reads=[B_rhl, B_const], writes=[Bps[5]])
                    P.op('dve', lambda e, O=O: e.tensor_copy(out=ftmp[1][0:64, :], in_=O[0:64, :]), reads=[Bps[oi]], writes=[B_ft[1]])
                    P.op('dve', lambda e, BC=BC, kv=kv, qblk=qblk: e.tensor_tensor(
                        out=OA[:, kv * 4:(kv + 1) * 4, qblk * 128:(qblk + 1) * 128],
                        in0=ftmp[1][0:64, :].rearrange("p (g q) -> p g q", g=4),
                        in1=BC[0:64, :].rearrange("p (g q) -> p g q", g=4), op=ALU.mult),
                        reads=[B_ft[1], Bps[5]], writes=[B_OA[kv][qblk]])
            for h in range(4):
                for qg in range(4):
                    if dbg is not None and (h, qg) not in dbg['B']:
                        continue
                    qsl = []
                    for m in range(2):
                        u = m * 4 + h
                        qs, bq, sq_ = qring.next()
                        P.op('sp', lambda e, qs=qs, u=u, qg=qg: e.dma_start(out=qs[0:64, :], in_=qb[u * 64:(u + 1) * 64, tokslice(qg)]),
                             reads=[Bqb], writes=[bq], dsem=sq_)
                        P.op('sp', lambda e, qs=qs, h=h, qg=qg: e.dma_start(out=qs[64:68, :], in_=qaug_d[h, :, tokslice(qg)]),
                             writes=[bq], dsem=sq_)
                        qsl.append((qs, bq))
                    OZ = [(psum[3], psum[4], 3, 4), (psum[5], psum[6], 5, 6)]
                    first_t = [True, True]
                    for ch in range(33):
                        isd = (ch == 32)
                        r, cc = ch // 4, ch % 4
                        kss = []
                        for m in range(2):
                            u = m * 4 + h
                            ks, bk, sk = kring.next()
                            if not isd:
                                r0 = r * KROWS + 128 + u * 64
                                P.op('sp', lambda e, ks=ks, r0=r0, cc=cc: e.dma_start(out=ks[0:64, :], in_=kta[r0:r0 + 64, cc * 512:(cc + 1) * 512]),
                                     reads=[Bkta], writes=[bk], dsem=sk)
                                P.op('sp', lambda e, ks=ks, h=h, qg=qg, ch=ch: e.dma_start(out=ks[64:68, :], in_=kaug_d[h, qg, :, ch * 512:(ch + 1) * 512]),
                                     writes=[bk], dsem=sk)
                            else:
                                P.op('sp', lambda e, ks=ks, u=u, qg=qg: e.dma_start(out=ks[0:64, :], in_=ktl[128 + u * 64:128 + (u + 1) * 64, tokslice(qg)]),
                                     reads=[Bktl], writes=[bk], dsem=sk)
                            kss.append((ks, bk))
                        vs, bv, sv_ = vring.next()
                        if not isd:
                            P.op('sp', lambda e, vs=vs, ch=ch, h=h: e.dma_start(
                                out=vs, in_=va[ch * 512:(ch + 1) * 512, 128 + h * 128:128 + (h + 1) * 128].rearrange("(j p) c -> p j c", p=128)),
                                reads=[Bva], writes=[bv], dsem=sv_)
                        else:
                            P.op('sp', lambda e, vs=vs, qg=qg, h=h: e.dma_start(
                                out=vs, in_=vl[qg * 512:(qg + 1) * 512, 128 + h * 128:128 + (h + 1) * 128].rearrange("(j p) c -> p j c", p=128)),
                                reads=[Bvl], writes=[bv], dsem=sv_)
                        for j in range(4):
                            for m in range(2):
                                Oq, Zq, oi, zi = OZ[m]
                                st_ = first_t[m]
                                first_t[m] = False
                                push(dict(ks=kss[m][0], bk=kss[m][1], qs=qsl[m][0], bq=qsl[m][1], j=j, nrows=(64 if isd else 68),
                                          diag=(j if isd else None), h=h, O=Oq, oi=oi, Z=Zq, zi=zi, vs=vs, bv=bv,
                                          start=st_, stop=(isd and j == 3)))
                    flush()
                    (O0, Z0, o0, z0), (O1, Z1, o1, z1) = OZ
                    P.op('dve', lambda e, Z0=Z0: e.reciprocal(out=ftmp[4], in_=Z0[:, :]), reads=[Bps[z0]], writes=[B_ft[4]])
                    P.op('dve', lambda e, O0=O0: e.tensor_tensor(out=ftmp[5], in0=O0[:, :], in1=ftmp[4], op=ALU.mult),
                         reads=[Bps[o0], B_ft[4]], writes=[B_ft[5]])
                    P.op('dve', lambda e, Z1=Z1: e.reciprocal(out=ftmp[6], in_=Z1[:, :]), reads=[Bps[z1]], writes=[B_ft[6]])
                    P.op('dve', lambda e, O1=O1: e.tensor_tensor(out=ftmp[7], in0=O1[:, :], in1=ftmp[6], op=ALU.mult),
                         reads=[Bps[o1], B_ft[6]], writes=[B_ft[7]])
                    P.op('dve', lambda e: e.scalar_tensor_tensor(out=ftmp[5], in0=ftmp[7], scalar=misc[:, mb + 1:mb + 2], in1=ftmp[5],
                                                                 op0=ALU.mult, op1=ALU.add),
                         reads=[B_ft[7], B_ft[5], B_misc], writes=[B_ft[5]])
                    P.op('dve', lambda e: e.tensor_tensor(out=sqb, in0=ftmp[5], in1=ftmp[5], op=ALU.mult), reads=[B_ft[5]], writes=[B_sqb])
                    X = psum[7]
                    P.op('pe', lambda e, X=X: e.matmul(X[:, :], lhsT=ones_bf[:, :], rhs=sqb, start=True, stop=True),
                         reads=[B_sqb, B_const], writes=[Bps[7]])
                    P.op('act', lambda e, X=X: e.activation(out=ftmp[8], in_=X[:, :], func=AF.Sqrt, scale=1.0 / 128, bias=EPS),
                         reads=[Bps[7]], writes=[B_ft[8]])
                    P.op('dve', lambda e: e.reciprocal(out=ftmp[9], in_=ftmp[8]), reads=[B_ft[8]], writes=[B_ft[9]])
                    P.op('dve', lambda e, h=h, qg=qg: e.scalar_tensor_tensor(
                        out=OB[:, h, tokslice(qg)], in0=ftmp[5], scalar=misc[:, mb + 2:mb + 3], in1=ftmp[9], op0=ALU.mult, op1=ALU.mult),
                        reads=[B_ft[5], B_ft[9], B_misc], writes=[B_OB[h][qg]])

        def phase3(l):
            P.barrier()
            o = 24576
            nT = BFA[:, o:o + 8192].rearrange("p (k t) -> p k t", k=8)
            o += 8192
            mT = BFA[:, o:o + 8192].rearrange("p (k t) -> p k t", k=8)
            o += 8192
            sq_aps = [BFA[:, o:o + 512], BFA[:, o + 512:o + 1024]]
            o += 1024
            assert o <= WR_OFF
            B_nT = [[P.buf('nT') for _ in range(2)] for _ in range(8)]
            B_mT = [[P.buf('mT') for _ in range(2)] for _ in range(8)]
            B_sq = [P.buf('sq0'), P.buf('sq1')]
            bgo = VOFF['bg%d' % l]
            cnt = 0
            for half in range(2):
                norm_to_nT(l, 1, half, nT, B_nT, lambda kc: dcol(l, 1, kc), lambda kc: mcol(l, 3, kc), sq_aps, B_sq)
                for dc in range(8):
                    s1, b1, e1 = wring.next()
                    P.op('pool', lambda e, s1=s1, dc=dc: e.dma_start(out=s1.rearrange("p (a x) -> p a x", a=2),
                                                                     in_=Wd['wgate%d' % l][dc].rearrange("p (a x) -> p a x", a=2)),
                         writes=[b1], dsem=e1)
                    g1v = s1.rearrange("p (a k f) -> p a k f", a=2, k=8)
                    s2, b2, e2 = wring.next()
                    P.op('pool', lambda e, s2=s2, dc=dc: e.dma_start(out=s2[0:64, 0:1024], in_=Wd['wba%d' % l][dc]), writes=[b2], dsem=e2)
                    P.op('pool', lambda e, s2=s2, dc=dc: e.dma_start(out=s2[:, 1024:1536], in_=Wd['wbb%d' % l][dc]), writes=[b2], dsem=e2)
                    for tg in range(2):
                        g = half * 2 + tg
                        GA, GB, YA, YB = psum[0 + 4 * (cnt % 2)], psum[1 + 4 * (cnt % 2)], psum[2 + 4 * (cnt % 2)], psum[3 + 4 * (cnt % 2)]
                        ia = [0 + 4 * (cnt % 2), 1 + 4 * (cnt % 2), 2 + 4 * (cnt % 2), 3 + 4 * (cnt % 2)]
                        fo = 4 * (cnt % 2)
                        cnt += 1
                        for a, PS, bi in ((0, GA, ia[0]), (1, GB, ia[1])):
                            for kc in range(8):
                                P.op('pe', lambda e, PS=PS, g1v=g1v, a=a, kc=kc, tg=tg: e.matmul(
                                    PS[:, :], lhsT=g1v[:, a, kc, :], rhs=nT[:, kc, tokslice(tg)], start=(kc == 0), stop=(kc == 7)),
                                    reads=[b1, B_nT[kc][tg]], writes=[Bps[bi]])
                        for hd in range(8):
                            P.op('pe', lambda e, YA=YA, s2=s2, hd=hd, g=g: e.matmul(
                                YA[:, :], lhsT=s2[0:64, hd * 128:(hd + 1) * 128], rhs=OA[:, hd, tokslice(g)], start=(hd == 0), stop=(hd == 7)),
                                reads=[b2] + [B_OA[hd // 4][g * 4 + q] for q in range(4)], writes=[Bps[ia[2]]])
                        for hh in range(4):
                            P.op('pe', lambda e, YB=YB, s2=s2, hh=hh, g=g: e.matmul(
                                YB[:, :], lhsT=s2[:, 1024 + hh * 128:1024 + (hh + 1) * 128], rhs=OB[:, hh, tokslice(g)], start=(hh == 0), stop=(hh == 3)),
                                reads=[b2, B_OB[hh][g]], writes=[Bps[ia[3]]])
                        P.op('act', lambda e, GA=GA, dc=dc, fo=fo: e.activation(out=ftmp[fo], in_=GA[:, :], func=AF.Sigmoid,
                                                                               bias=vecs[:, bgo + dc:bgo + dc + 1], scale=1.0),
                             reads=[Bps[ia[0]], B_vecs], writes=[B_ft[fo]])
                        P.op('act', lambda e, GB=GB, dc=dc, fo=fo: e.activation(out=ftmp[fo + 1], in_=GB[:, :], func=AF.Sigmoid,
                                                                               bias=vecs[:, bgo + 8 + dc:bgo + 8 + dc + 1], scale=1.0),
                             reads=[Bps[ia[1]], B_vecs], writes=[B_ft[fo + 1]])
                        P.op('dve', lambda e, YA=YA, fo=fo: e.tensor_tensor(out=ftmp[fo + 2], in0=ftmp[fo], in1=YA[:, :], op=ALU.mult),
                             reads=[B_ft[fo], Bps[ia[2]]], writes=[B_ft[fo + 2]])
                        P.op('dve', lambda e, YB=YB, fo=fo: e.tensor_tensor(out=ftmp[fo + 3], in0=ftmp[fo + 1], in1=YB[:, :], op=ALU.mult),
                             reads=[B_ft[fo + 1], Bps[ia[3]]], writes=[B_ft[fo + 3]])
                        P.op('dve', lambda e, dc=dc, tg=tg, fo=fo: e.tensor_tensor(out=mT[:, dc, tokslice(tg)], in0=ftmp[fo + 2], in1=ftmp[fo + 3], op=ALU.add),
                             reads=[B_ft[fo + 2], B_ft[fo + 3]], writes=[B_mT[dc][tg]])
                for dco in range(8):
                    s1, b1, e1 = wring.next()
                    P.op('pool', lambda e, s1=s1, dco=dco: e.dma_start(out=s1[:, 0:1024], in_=Wd['wo%d' % l][dco]), writes=[b1], dsem=e1)
                    for tg in range(2):
                        g = half * 2 + tg
                        yi = 4 * (cnt % 2)
                        cnt += 1
                        Y = psum[yi]
                        for kc in range(8):
                            P.op('pe', lambda e, Y=Y, s1=s1, kc=kc, tg=tg: e.matmul(
                                Y[:, :], lhsT=s1[:, kc * 128:(kc + 1) * 128], rhs=mT[:, kc, tokslice(tg)], start=(kc == 0), stop=(kc == 7)),
                                reads=[b1, B_mT[kc][tg]], writes=[Bps[yi]])
                        P.op('dve', lambda e, Y=Y, dco=dco, g=g: e.scalar_tensor_tensor(
                            out=hT[:, dco, tokslice(g)], in0=Y[:, :], scalar=mcol(l, 5, dco), in1=hT[:, dco, tokslice(g)],
                            op0=ALU.mult, op1=ALU.add), reads=[Bps[yi], B_h[dco][g], B_mod], writes=[B_h[dco][g]])

        def final_out():
            P.barrier()
            B_o = P.buf('outd')
            if not last:
                for c in range(8):
                    for g in range(4):
                        P.op('sp', lambda e, c=c, g=g: e.dma_start(out=h_out[c * 128:(c + 1) * 128, tokslice(g)], in_=hT[:, c, tokslice(g)]),
                             reads=[B_h[c][g]], writes=[B_o], dsem=d_out)
            else:
                sq_aps = [BFA[:, 0:512], BFA[:, 512:1024]]
                B_sq = [P.buf('sq0'), P.buf('sq1')]
                ST = psum[6]
                fgo = VOFF['fg']
                for g in range(4):
                    for kc in range(8):
                        sq, bsq = sq_aps[kc % 2], B_sq[kc % 2]
                        P.op('dve', lambda e, sq=sq, kc=kc, g=g: e.tensor_tensor(out=sq, in0=hT[:, kc, tokslice(g)], in1=hT[:, kc, tokslice(g)], op=ALU.mult),
                             reads=[B_h[kc][g]], writes=[bsq])
                        P.op('pe', lambda e, sq=sq, kc=kc: e.matmul(ST[:, :], lhsT=ones_bf[:, :], rhs=sq, start=(kc == 0), stop=(kc == 7)),
                             reads=[bsq, B_const], writes=[Bps[6]])
                    P.op('act', lambda e: e.activation(out=ftmp[8], in_=ST[:, :], func=AF.Sqrt, scale=1.0 / D, bias=EPS), reads=[Bps[6]], writes=[B_ft[8]])
                    P.op('dve', lambda e: e.reciprocal(out=ftmp[9], in_=ftmp[8]), reads=[B_ft[8]], writes=[B_ft[9]])
                    for kc in range(8):
                        tmp, btmp = ftmp[kc % 4], B_ft[kc % 4]
                        P.op('dve', lambda e, tmp=tmp, kc=kc, g=g: e.scalar_tensor_tensor(
                            out=tmp, in0=hT[:, kc, tokslice(g)], scalar=vecs[:, fgo + kc:fgo + kc + 1], in1=ftmp[9], op0=ALU.mult, op1=ALU.mult),
                            reads=[B_h[kc][g], B_ft[9], B_vecs], writes=[btmp])
                        P.op('sp', lambda e, tmp=tmp, kc=kc, g=g: e.dma_start(out=h_out[kc * 128:(kc + 1) * 128, tokslice(g)], in_=tmp),
                             reads=[btmp], writes=[B_o], dsem=d_out)
            P.op('sp', None, reads=[B_o])

        for l in range(L):
            if 'p1_%d' % l in segs:
                ffn(l, 0)
                phase1b(l)
            if fused[l]:
                gather(l)
            if 'p23_%d' % l in segs:
                phase2(l)
                if dbg is not None:
                    P.barrier()
                    d_oa = dram('dbg_oa', [64, 16384], BF16, 'ExternalOutput')
                    d_ob = dram('dbg_ob', [128, 8192], BF16, 'ExternalOutput')
                    B_dbg = P.buf('dbgo')
                    P.op('sp', lambda e: e.dma_start(out=d_oa, in_=BFA[0:64, 0:16384]), writes=[B_dbg], dsem=d_out)
                    P.op('sp', lambda e: e.dma_start(out=d_ob, in_=BFA[:, 16384:24576]), writes=[B_dbg], dsem=d_out)
                    P.op('sp', None, reads=[B_dbg])
                    continue
                phase3(l)
                ffn(l, 1)
        final_out()
        P.op('sp', None, reads=[B_scr[k] for k in B_scr])
        P.emit(st)
    nc._w_names = list(Wd.keys())
    return nc


_CACHE = {}


def _get_nc(segs):
    key = tuple(segs)
    if key not in _CACHE:
        _CACHE[key] = build(segs)
    return _CACHE[key]


FUSED = True


def kernel(x, c, ada_w, ada_b, norm_g, ffn_wg, ffn_wu, ffn_wd, w_in, qk_g, lam_p, subln_g, w_ba, w_bb,
           w_gate, b_gate, w_o, final_g):
    f = lambda a: np.asarray(a, dtype=np.float32)
    x, c, ada_w, ada_b, norm_g = f(x), f(c), f(ada_w), f(ada_b), f(norm_g)
    ffn_wg, ffn_wu, ffn_wd, w_in = f(ffn_wg), f(ffn_wu), f(ffn_wd), f(w_in)
    qk_g, lam_p, subln_g, w_ba, w_bb = f(qk_g), f(lam_p), f(subln_g), f(w_ba), f(w_bb)
    w_gate, b_gate, w_o, final_g = f(w_gate), f(b_gate), f(w_o), f(final_g)
    vecs = _host_vecs(c, ada_b, norm_g, b_gate, qk_g, subln_g, lam_p, final_g)
    W = _host_weights(ffn_wg, ffn_wu, ffn_wd, w_in, w_ba, w_bb, w_gate, w_o, ada_w)
    atab = _host_atab()
    common = []
    for core in range(NCORES):
        qaug, kaug = _host_alibi(core)
        common.append({'vecs': vecs, 'rope': _host_rope(core), 'qaug': qaug, 'kaug': kaug, 'atab': atab})
    xT = [np.ascontiguousarray(x[0, core * T:(core + 1) * T, :].T) for core in range(NCORES)]
    cores = list(range(NCORES))

    def wsel(nc_keys):
        return {k: W[k] for k in nc_keys}

    def run(segs, extra):
        nc = _get_nc(segs)
        wk = nc._w_names
        in_maps = []
        for core in cores:
            m = dict(common[core])
            m.update(wsel(wk))
            m.update(extra[core])
            in_maps.append(m)
        return run_bass_kernel_spmd(nc, in_maps, core_ids=cores).results

    if FUSED:
        res = run(['p1_0', 'p23_0', 'p1_1', 'p23_1', 'fin'], [{'xT': xT[i]} for i in cores])
    else:
        r0 = run(['p1_0'], [{'xT': xT[i]} for i in cores])

        def gathered(r, l):
            kta = np.concatenate([r[i]['ktl%d' % l] for i in cores], axis=0)
            va = np.concatenate([r[i]['vl%d' % l] for i in cores], axis=0)
            return kta, va
        kta, va = gathered(r0, 0)
        r1 = run(['p23_0', 'p1_1'], [{'h_in': r0[i]['h_out'], 'qa0': r0[i]['qa0'], 'qb0': r0[i]['qb0'], 'ktl0': r0[i]['ktl0'],
                                      'vl0': r0[i]['vl0'], 'kta0': kta, 'va0': va} for i in cores])
        kta, va = gathered(r1, 1)
        res = run(['p23_1', 'fin'], [{'h_in': r1[i]['h_out'], 'qa1': r1[i]['qa1'], 'qb1': r1[i]['qb1'], 'ktl1': r1[i]['ktl1'],
                                      'vl1': r1[i]['vl1'], 'kta1': kta, 'va1': va} for i in cores])
    out = np.concatenate([res[i]['outT'].T for i in cores], axis=0)[None]
    return np.ascontiguousarray(out.astype(np.float32))
```
